# Optimizing a Trainium2 kernel written in Bass

```python
import math
import jax, jax.numpy as jnp
from jax import lax
import numpy as np

D_MODEL = 2048
BATCH = 4
SEQ = 8192
DEPTH = 4

D_MIX = D_MODEL
CONV_K = 4

S5_WIDTH = D_MIX // 4
S5_GROUP = 16
S5_NGROUPS = S5_WIDTH // S5_GROUP
S5_STATE = 64
S5_DT_MIN = 1e-3
S5_DT_MAX = 1e-1

ML_HEADS = 4
ML_DV = 192
ML_DQK = ML_DV // 2
ML_WIDTH = ML_HEADS * ML_DV
ML_QK_DIM = ML_HEADS * ML_DQK
ML_CHUNK = 64

SSD_WIDTH = D_MIX - S5_WIDTH - ML_WIDTH
SSD_HEADDIM = 64
SSD_HEADS = SSD_WIDTH // SSD_HEADDIM
SSD_GROUPS = 4
SSD_RATIO = SSD_HEADS // SSD_GROUPS
SSD_STATE = 128
SSD_CHUNK = 128
SSD_CONV_DIM = SSD_WIDTH + 2 * SSD_GROUPS * SSD_STATE
SSD_DT_MIN = 1e-3
SSD_DT_MAX = 1e-1

IN_SIZES = (S5_WIDTH, S5_WIDTH,
            2 * ML_QK_DIM, ML_WIDTH,
            ML_HEADS, ML_HEADS, ML_WIDTH,
            ML_WIDTH,
            SSD_CONV_DIM, SSD_HEADS, SSD_WIDTH)
N_IN = sum(IN_SIZES)

DEEPNORM_ALPHA = (2.0 * DEPTH) ** 0.25
DEEPNORM_BETA = (8.0 * DEPTH) ** -0.25
EPS = 1e-5

kernel_name = 'hybrid_s5_mlstm_ssd_parallel_heads'


def split_in(h):
    idx = []
    acc = 0
    for s in IN_SIZES[:-1]:
        acc += s
        idx.append(acc)
    return jnp.split(h, idx, axis=-1)


def layer_norm(x, g, b):
    xf = x.astype(jnp.float32)
    mu = jnp.mean(xf, -1, keepdims=True)
    var = jnp.mean(jnp.square(xf - mu), -1, keepdims=True)
    y = (xf - mu) * lax.rsqrt(var + EPS) * g.astype(jnp.float32) + b.astype(jnp.float32)
    return y.astype(x.dtype)


def rms_norm(x, g):
    xf = x.astype(jnp.float32)
    return xf * lax.rsqrt(jnp.mean(jnp.square(xf), -1, keepdims=True) + EPS) * g.astype(jnp.float32)


def causal_dwconv(u, w, b):
    k = w.shape[0]
    out = lax.conv_general_dilated(u, w.astype(u.dtype)[:, None, :], window_strides=(1,),
                                   padding=[(k - 1, 0)], dimension_numbers=('NWC', 'WIO', 'NWC'),
                                   feature_group_count=u.shape[-1])
    return out + b.astype(u.dtype)


def s5_mixer(u, lam_re, lam_im, log_step, b_re, b_im, c_re, c_im, d, w_glu, b_glu):
    f32 = jnp.float32
    bsz, seq, _ = u.shape
    lam = lax.complex(lam_re.astype(f32), lam_im.astype(f32))
    step = jnp.exp(log_step.astype(f32))[:, None]
    lam_bar = jnp.exp(lam * step)
    b_mat = lax.complex(b_re.astype(f32), b_im.astype(f32))
    b_bar = ((lam_bar - 1.0) / lam)[..., None] * b_mat
    c_mat = lax.complex(c_re.astype(f32), c_im.astype(f32))
    uf = u.astype(f32)
    ug = uf.reshape(bsz, seq, S5_NGROUPS, S5_GROUP).astype(jnp.complex64)
    bu = jnp.einsum('blgc,gpc->blgp', ug, b_bar)
    a = jnp.broadcast_to(lam_bar, bu.shape)

    def combine(left, right):
        a_l, b_l = left
        a_r, b_r = right
        return a_r * a_l, a_r * b_l + b_r

    _, states = lax.associative_scan(combine, (a, bu), axis=1)
    y = jnp.einsum('blgp,gcp->blgc', states, c_mat).real.reshape(bsz, seq, S5_WIDTH)
    y = y + d.astype(f32) * uf
    g = jax.nn.gelu(y)
    return g * jax.nn.sigmoid(g @ w_glu.astype(f32) + b_glu.astype(f32))


def mlstm_mixer(q, k, v, i_pre, f_pre, o_pre, norm_g):
    f32 = jnp.float32
    bsz, seq, _ = q.shape
    nc = seq // ML_CHUNK

    def to_chunks(t, dim):
        return t.astype(f32).reshape(bsz, nc, ML_CHUNK, ML_HEADS, dim).transpose(1, 0, 3, 2, 4)

    def gate_chunks(t):
        return t.reshape(bsz, nc, ML_CHUNK, ML_HEADS).transpose(1, 0, 3, 2)

    qc = to_chunks(q, ML_DQK) * (ML_DQK ** -0.5)
    kc = to_chunks(k, ML_DQK)
    vc = to_chunks(v, ML_DV)
    ic = gate_chunks(i_pre.astype(f32))
    lfc = gate_chunks(jax.nn.log_sigmoid(f_pre.astype(f32)))
    causal = jnp.tril(jnp.ones((ML_CHUNK, ML_CHUNK), dtype=bool))

    def step(carry, inp):
        c_st, n_st, m_st = carry
        qj, kj, vj, ij, lf = inp
        bcum = jnp.cumsum(lf, axis=-1)
        log_d = jnp.where(causal, bcum[..., :, None] - bcum[..., None, :] + ij[..., None, :], -jnp.inf)
        inter = bcum + m_st[..., None]
        m_row = jnp.maximum(inter, jnp.max(log_d, -1))
        dmat = jnp.exp(log_d - m_row[..., None])
        inter_scale = jnp.exp(inter - m_row)
        s = jnp.einsum('bhid,bhjd->bhij', qj, kj) * dmat
        num = jnp.einsum('bhij,bhjv->bhiv', s, vj) + inter_scale[..., None] * jnp.einsum('bhid,bhdv->bhiv', qj, c_st)
        den = jnp.sum(s, -1) + inter_scale * jnp.einsum('bhid,bhd->bhi', qj, n_st)
        h = num / jnp.maximum(jnp.abs(den), jnp.exp(-m_row))[..., None]
        b_last = bcum[..., -1]
        log_w = b_last[..., None] - bcum + ij
        m_new = jnp.maximum(b_last + m_st, jnp.max(log_w, -1))
        w = jnp.exp(log_w - m_new[..., None])
        decay = jnp.exp(b_last + m_st - m_new)
        c_new = decay[..., None, None] * c_st + jnp.einsum('bhj,bhjd,bhjv->bhdv', w, kj, vj)
        n_new = decay[..., None] * n_st + jnp.einsum('bhj,bhjd->bhd', w, kj)
        return (c_new, n_new, m_new), h

    init = (jnp.zeros((bsz, ML_HEADS, ML_DQK, ML_DV), f32),
            jnp.zeros((bsz, ML_HEADS, ML_DQK), f32),
            jnp.zeros((bsz, ML_HEADS), f32))
    _, hs = lax.scan(step, init, (qc, kc, vc, ic, lfc))
    h = hs.transpose(1, 0, 3, 2, 4)
    h = h * lax.rsqrt(jnp.mean(jnp.square(h), -1, keepdims=True) + EPS)
    h = h.reshape(bsz, seq, ML_WIDTH) * norm_g.astype(f32)
    return jax.nn.sigmoid(o_pre.astype(f32)) * h


def ssd_mixer(xs, bs, cs, dt_raw, dt_bias, a_log, d_skip):
    f32 = jnp.float32
    bsz, seq, _ = xs.shape
    nc = seq // SSD_CHUNK
    shp = (bsz, nc, SSD_CHUNK, SSD_GROUPS, SSD_RATIO)
    x = xs.astype(f32).reshape(shp + (SSD_HEADDIM,))
    bm = bs.astype(f32).reshape(bsz, nc, SSD_CHUNK, SSD_GROUPS, SSD_STATE)
    cm = cs.astype(f32).reshape(bsz, nc, SSD_CHUNK, SSD_GROUPS, SSD_STATE)
    dt = jax.nn.softplus(dt_raw.astype(f32) + dt_bias.astype(f32)).reshape(shp)
    a = -jnp.exp(a_log.astype(f32)).reshape(SSD_GROUPS, SSD_RATIO)
    a_cum = jnp.cumsum(dt * a, axis=2)
    dtx = x * dt[..., None]
    causal = jnp.tril(jnp.ones((SSD_CHUNK, SSD_CHUNK), dtype=bool))
    seg = a_cum[:, :, :, None] - a_cum[:, :, None, :]
    lmat = jnp.exp(jnp.where(causal[:, :, None, None], seg, -jnp.inf))
    cb = jnp.einsum('bcign,bcjgn->bcijg', cm, bm)
    y_diag = jnp.einsum('bcijg,bcijgr,bcjgrp->bcigrp', cb, lmat, dtx)
    decay_s = jnp.exp(a_cum[:, :, -1:] - a_cum)
    states = jnp.einsum('bcjgn,bcjgr,bcjgrp->bcgrpn', bm, decay_s, dtx)
    chunk_decay = jnp.exp(a_cum[:, :, -1])

    def step(s_prev, inp):
        st, dec = inp
        return dec[..., None, None] * s_prev + st, s_prev

    init = jnp.zeros((bsz, SSD_GROUPS, SSD_RATIO, SSD_HEADDIM, SSD_STATE), f32)
    _, s_in = lax.scan(step, init, (states.transpose(1, 0, 2, 3, 4, 5), chunk_decay.transpose(1, 0, 2, 3)))
    s_in = s_in.transpose(1, 0, 2, 3, 4, 5)
    y_off = jnp.einsum('bcign,bcgrpn,bcigr->bcigrp', cm, s_in, jnp.exp(a_cum))
    y = y_diag + y_off + d_skip.astype(f32).reshape(SSD_GROUPS, SSD_RATIO)[..., None] * x
    return y.reshape(bsz, seq, SSD_WIDTH)


def setup_inputs(seed: int = 0) -> dict:
    key = jax.random.key(seed)
    ks = jax.random.split(key, 32)
    f32 = jnp.float32

    def nrm(k, shape, scale):
        return jax.random.normal(k, shape, f32) * scale

    x = jax.random.normal(ks[0], (BATCH, SEQ, D_MODEL), f32)
    w_in = nrm(ks[1], (DEPTH, D_MODEL, N_IN), D_MODEL ** -0.5)
    w_out = nrm(ks[2], (DEPTH, D_MIX, D_MODEL), DEEPNORM_BETA * D_MIX ** -0.5)
    ln_g = 1.0 + nrm(ks[3], (DEPTH, D_MODEL), 0.02)
    ln_b = nrm(ks[4], (DEPTH, D_MODEL), 0.02)
    n_idx = jnp.arange(S5_STATE, dtype=f32)
    s5_lambda_re = -0.5 + nrm(ks[5], (DEPTH, S5_NGROUPS, S5_STATE), 0.01)
    s5_lambda_im = math.pi * n_idx + nrm(ks[6], (DEPTH, S5_NGROUPS, S5_STATE), 0.01)
    s5_log_step = jax.random.uniform(ks[7], (DEPTH, S5_NGROUPS), f32, math.log(S5_DT_MIN), math.log(S5_DT_MAX))
    s5_b_re = nrm(ks[8], (DEPTH, S5_NGROUPS, S5_STATE, S5_GROUP), (2.0 * S5_GROUP) ** -0.5)
    s5_b_im = nrm(ks[9], (DEPTH, S5_NGROUPS, S5_STATE, S5_GROUP), (2.0 * S5_GROUP) ** -0.5)
    s5_c_re = nrm(ks[10], (DEPTH, S5_NGROUPS, S5_GROUP, S5_STATE), S5_STATE ** -0.5)
    s5_c_im = nrm(ks[11], (DEPTH, S5_NGROUPS, S5_GROUP, S5_STATE), S5_STATE ** -0.5)
    s5_d = nrm(ks[12], (DEPTH, S5_WIDTH), 1.0)
    s5_w_glu = nrm(ks[13], (DEPTH, S5_WIDTH, S5_WIDTH), S5_WIDTH ** -0.5)
    s5_b_glu = nrm(ks[14], (DEPTH, S5_WIDTH), 0.02)
    ml_conv_w = nrm(ks[15], (DEPTH, CONV_K, 2 * ML_QK_DIM), CONV_K ** -0.5)
    ml_conv_b = nrm(ks[16], (DEPTH, 2 * ML_QK_DIM), 0.02)
    ml_i_bias = nrm(ks[17], (DEPTH, ML_HEADS), 0.1)
    ml_f_bias = jnp.linspace(3.0, 6.0, ML_HEADS, dtype=f32) + nrm(ks[18], (DEPTH, ML_HEADS), 0.1)
    ml_norm_g = 1.0 + nrm(ks[19], (DEPTH, ML_WIDTH), 0.02)
    ssd_conv_w = nrm(ks[20], (DEPTH, CONV_K, SSD_CONV_DIM), CONV_K ** -0.5)
    ssd_conv_b = nrm(ks[21], (DEPTH, SSD_CONV_DIM), 0.02)
    dt0 = jnp.exp(jax.random.uniform(ks[22], (DEPTH, SSD_HEADS), f32, math.log(SSD_DT_MIN), math.log(SSD_DT_MAX)))
    ssd_dt_bias = dt0 + jnp.log(-jnp.expm1(-dt0))
    ssd_a_log = jnp.log(jax.random.uniform(ks[23], (DEPTH, SSD_HEADS), f32, 1.0, 16.0))
    ssd_d = 1.0 + nrm(ks[24], (DEPTH, SSD_HEADS), 0.1)
    ssd_norm_g = 1.0 + nrm(ks[25], (DEPTH, SSD_WIDTH), 0.02)
    return {'x': x, 'w_in': w_in, 'w_out': w_out, 'ln_g': ln_g, 'ln_b': ln_b,
            's5_lambda_re': s5_lambda_re, 's5_lambda_im': s5_lambda_im, 's5_log_step': s5_log_step,
            's5_b_re': s5_b_re, 's5_b_im': s5_b_im, 's5_c_re': s5_c_re, 's5_c_im': s5_c_im,
            's5_d': s5_d, 's5_w_glu': s5_w_glu, 's5_b_glu': s5_b_glu,
            'ml_conv_w': ml_conv_w, 'ml_conv_b': ml_conv_b, 'ml_i_bias': ml_i_bias,
            'ml_f_bias': ml_f_bias, 'ml_norm_g': ml_norm_g,
            'ssd_conv_w': ssd_conv_w, 'ssd_conv_b': ssd_conv_b, 'ssd_dt_bias': ssd_dt_bias,
            'ssd_a_log': ssd_a_log, 'ssd_d': ssd_d, 'ssd_norm_g': ssd_norm_g}


def reference(x, w_in, w_out, ln_g, ln_b,
              s5_lambda_re, s5_lambda_im, s5_log_step, s5_b_re, s5_b_im, s5_c_re, s5_c_im,
              s5_d, s5_w_glu, s5_b_glu,
              ml_conv_w, ml_conv_b, ml_i_bias, ml_f_bias, ml_norm_g,
              ssd_conv_w, ssd_conv_b, ssd_dt_bias, ssd_a_log, ssd_d, ssd_norm_g):
    f32 = jnp.float32
    for l in range(DEPTH):
        h = jnp.einsum('bld,dn->bln', x, w_in[l])
        (s5_u, s5_z, ml_qk, ml_v, ml_i, ml_f, ml_o, ml_z,
         ssd_xbc, ssd_dt, ssd_z) = split_in(h)
        y_s5 = s5_mixer(s5_u, s5_lambda_re[l], s5_lambda_im[l], s5_log_step[l],
                        s5_b_re[l], s5_b_im[l], s5_c_re[l], s5_c_im[l],
                        s5_d[l], s5_w_glu[l], s5_b_glu[l]) * jax.nn.silu(s5_z.astype(f32))
        qk = jax.nn.silu(causal_dwconv(ml_qk, ml_conv_w[l], ml_conv_b[l]))
        q, k = jnp.split(qk, 2, axis=-1)
        y_ml = mlstm_mixer(q, k, ml_v, ml_i.astype(f32) + ml_i_bias[l].astype(f32),
                           ml_f.astype(f32) + ml_f_bias[l].astype(f32), ml_o,
                           ml_norm_g[l]) * jax.nn.silu(ml_z.astype(f32))
        xbc = jax.nn.silu(causal_dwconv(ssd_xbc, ssd_conv_w[l], ssd_conv_b[l]))
        xs, bs, cs = jnp.split(xbc, [SSD_WIDTH, SSD_WIDTH + SSD_GROUPS * SSD_STATE], axis=-1)
        y = ssd_mixer(xs, bs, cs, ssd_dt, ssd_dt_bias[l], ssd_a_log[l], ssd_d[l])
        y_ssd = rms_norm(y * jax.nn.silu(ssd_z.astype(f32)), ssd_norm_g[l])
        mixed = jnp.concatenate([y_s5, y_ml, y_ssd], axis=-1).astype(x.dtype)
        out = jnp.einsum('bln,nd->bld', mixed, w_out[l])
        x = layer_norm(DEEPNORM_ALPHA * x + out, ln_g[l], ln_b[l])
    return x
```

```python
import math
from contextlib import ExitStack

import numpy as np
import concourse.bass as bass
import concourse.mybir as mybir
from concourse.bass_utils import run_bass_kernel_spmd

F32 = mybir.dt.float32
BF16 = mybir.dt.bfloat16
ALU = mybir.AluOpType
AF = mybir.ActivationFunctionType

ENGS = ("pe", "act", "dve", "pool", "sp")

P = 128
D = 2048
KT = 16
TT = 256
CH = 128
NCH = TT // CH
DEPTH = 4
N_IN = 6676
ALPHA = (2.0 * DEPTH) ** 0.25
EPS = 1e-5
DQK = 96
DV = 192
TWO_PI = 2.0 * math.pi
CW1 = 6.28125
CW2 = TWO_PI - 6.28125
MAGIC = 12582912.0

O_S5U, O_S5Z, O_MLQ, O_MLK, O_MLV = 0, 512, 1024, 1408, 1792
O_MLI, O_MLF, O_MLO, O_MLZ = 2560, 2564, 2568, 3336
O_SX, O_SB, O_SC, O_SDT, O_SZ = 4104, 4872, 5384, 5896, 5908


def _chunks():
    ch = []
    ar = np.arange
    ch.append(("gates", np.concatenate([ar(O_MLI, O_MLI + 4), ar(O_MLF, O_MLF + 4), ar(O_SDT, O_SDT + 12)])))
    for kt in range(4):
        ch.append((f"s5_{kt}", np.concatenate([ar(O_S5U + kt * 128, O_S5U + kt * 128 + 128), ar(O_S5Z + kt * 128, O_S5Z + kt * 128 + 128)])))
    for h in range(4):
        ch.append((f"mlA_{h}", np.concatenate([ar(O_MLQ + h * 96, O_MLQ + h * 96 + 96), ar(O_MLK + h * 96, O_MLK + h * 96 + 96)])))
        ch.append((f"mlB_{h}", ar(O_MLO + h * 192, O_MLO + h * 192 + 192)))
        ch.append((f"mlC_{h}", ar(O_MLZ + h * 192, O_MLZ + h * 192 + 192)))
        ch.append((f"mlV_{h}", ar(O_MLV + h * 192, O_MLV + h * 192 + 192)))
    for g in range(4):
        ch.append((f"sX_{g}", ar(O_SX + g * 192, O_SX + g * 192 + 192)))
        ch.append((f"sBC_{g}", np.concatenate([ar(O_SB + g * 128, O_SB + g * 128 + 128), ar(O_SC + g * 128, O_SC + g * 128 + 128)])))
        ch.append((f"sZ_{g}", ar(O_SZ + g * 192, O_SZ + g * 192 + 192)))
    return ch


CHUNKS = _chunks()
CH_OFF = {}
_o = 0
for _n, _c in CHUNKS:
    CH_OFF[_n] = (_o, len(_c))
    _o += len(_c)
assert _o == N_IN
PERM = np.concatenate([c for _, c in CHUNKS])

PK = {}
_o = 0
for _n, _w in [("lng", 16), ("lnb", 16), ("lre", 16), ("lim", 16), ("lst", 16), ("s5d", 4), ("bglu", 4),
               ("cw", 28 * 4), ("cb", 28), ("mlg", 8), ("sd", 12), ("sg", 12), ("gb", 20), ("al", 12)]:
    PK[_n] = _o
    _o += _w
NPK = _o


class Sched:
    SEM_EPOCH = 30000

    def __init__(self, nc, es, same_engine_sync=True):
        self.nc = nc
        self.es = es
        self.ops = {e: [] for e in ENGS}
        self.sems = {}
        self.esem = {}
        self.eseq = {}
        self.nsem = 0
        for e in ENGS:
            self._new_esem(e)
        self.dsem = {}
        self.last_write = {}
        self.readers = {}
        self.waited = {e: {} for e in ENGS}
        self.same_engine_sync = same_engine_sync
        self.n_ins = 0

    def _mksem(self, name):
        s = self.es.enter_context(self.nc.semaphore(name))
        self.nsem += 1
        self.sems[id(s)] = s
        return s

    def _new_esem(self, e):
        self.esem[e] = self._mksem(f"s_{e}_{self.nsem}")
        self.eseq[e] = 0

    ALIAS = {}

    def _deps(self, eng, reads, writes):
        need = {}
        reads = [self.ALIAS.get(k, k) for k in reads]
        writes = [self.ALIAS.get(k, k) for k in writes] + [k for k in reads if k.startswith("B_")]

        def add(tok):
            if tok is None:
                return
            sid, val, src = tok
            if src == eng and (eng == "pe" or not self.same_engine_sync):
                return
            if need.get(sid, 0) < val:
                need[sid] = val

        for k in reads:
            add(self.last_write.get(k))
        for k in writes:
            add(self.last_write.get(k))
            for t in self.readers.get(k, ()):
                add(t)
        w = self.waited[eng]
        for sid, val in need.items():
            if w.get(sid, 0) >= val:
                continue
            w[sid] = val
            self.ops[eng].append(("wait", self.sems[sid], val))

    def _commit(self, tok, reads, writes):
        reads = [self.ALIAS.get(k, k) for k in reads]
        writes = [self.ALIAS.get(k, k) for k in writes] + [k for k in reads if k.startswith("B_")]
        for k in writes:
            self.last_write[k] = tok
            self.readers[k] = []
        for k in reads:
            lst = self.readers.setdefault(k, [])
            lst.append(tok)
            if len(lst) > 8:
                best = {}
                for t in lst:
                    if best.get(t[0], (0, 0, 0))[1] < t[1]:
                        best[t[0]] = t
                self.readers[k] = list(best.values())

    def op(self, eng, fn, reads=(), writes=()):
        self._deps(eng, reads, writes)
        if self.eseq[eng] >= self.SEM_EPOCH:
            self._new_esem(eng)
        self.eseq[eng] += 1
        sem = self.esem[eng]
        tok = (id(sem), self.eseq[eng], eng)
        self.ops[eng].append(("ins", fn, sem, 1))
        self._commit(tok, reads, writes)
        self.n_ins += 1
        return tok

    def dma(self, q, fn, key, reads=(), writes=()):
        self._deps(q, reads, writes)
        if key not in self.dsem:
            self.dsem[key] = [self._mksem(f"d_{self.nsem}"), 0]
        ent = self.dsem[key]
        ent[1] += 16
        tok = (id(ent[0]), ent[1], "dma")
        self.ops[q].append(("ins", fn, ent[0], 16))
        self._commit(tok, reads, writes)
        self.n_ins += 1
        return tok

    def wait_tok(self, eng, tok):
        sid, val, _ = tok
        if self.waited[eng].get(sid, 0) >= val:
            return
        self.waited[eng][sid] = val
        self.ops[eng].append(("wait", self.sems[sid], val))

    def replay(self):
        nc = self.nc
        ops = self.ops

        def run(engobj, lst):
            for it in lst:
                if it[0] == "wait":
                    engobj.wait_ge(it[1], it[2])
                else:
                    it[1](engobj).then_inc(it[2], it[3])

        with nc.Block() as block:

            @block.tensor
            def _(e):
                run(e, ops["pe"])

            @block.scalar
            def _(e):
                run(e, ops["act"])

            @block.vector
            def _(e):
                run(e, ops["dve"])

            @block.gpsimd
            def _(e):
                run(e, ops["pool"])

            @block.sync
            def _(e):
                run(e, ops["sp"])


class _Stop(Exception):
    pass


def build(nc, es, L, T, layer0=0, debug=False, phase=99):
    NT = T // TT
    import os as _os
    S = Sched(nc, es, same_engine_sync=(_os.environ.get('SES', '1') == '1'))

    def dram(name, shape, kind, dt=F32):
        return nc.dram_tensor(name, shape, dt, kind=kind).ap()

    xT_d = dram("xT", [D, T], "ExternalInput")
    win_d = dram("win", [L, D, N_IN], "ExternalInput")
    wout_d = dram("wout", [L, D, D], "ExternalInput")
    pp_d = dram("pp", [L, P, NPK], "ExternalInput")
    bt_d = dram("bt", [L, P, 2 * 16 * 128], "ExternalInput")
    ct_d = dram("ct", [L, P, 2, 16, 128], "ExternalInput")
    wglu_d = dram("wglu", [L, 512, 512], "ExternalInput")
    jrow_d = dram("jrow", [P, 128], "ExternalInput")
    yT_d = dram("yT", [D, T], "ExternalOutput")
    xs_d = [dram(f"xs{i}", [D, T], "Internal") for i in range(2)] if L > 1 else []
    dbg_d = dram("dbg", [24, P, T], "ExternalOutput") if debug else None
    winb_d = [dram(f"winb{l}", [D, N_IN], "Internal", BF16) for l in range(L)]
    woutb_d = [dram(f"woutb{l}", [D, D], "Internal", BF16) for l in range(L)]

    def sb(name, shape, dt=F32):
        return es.enter_context(nc.sbuf_tensor(name, shape, dt))

    def psum(name, shape, dt=F32):
        return es.enter_context(nc.psum_tensor(name, shape, dt))

    ident_f = sb("ident_f", [P, 128]); ident_b = sb("ident_b", [P, 128], BF16)
    tri_f = sb("tri_f", [P, 128]); ones_f = sb("ones_f", [P, 128]); ones_b = sb("ones_b", [P, 128], BF16)
    jrow = sb("jrow_s", [P, 128])
    wfl = [sb(f"wfl{i}", [P, 16 * 256], BF16) for i in range(2)]
    xb = sb("xb", [P, KT, TT], BF16)
    xf = sb("xf", [P, KT, TT])
    mx_s5 = sb("mx_s5", [P, 4, TT], BF16); mx_ml = sb("mx_ml", [P, 8, TT], BF16); mx_ssd = sb("mx_ssd", [P, 12, TT], BF16)
    pp = sb("pp_s", [P, NPK])
    btb = sb("btb", [P, 2, 16, 128], BF16); ckb = sb("ckb", [P, 2, 16, 128], BF16)
    ctst = [sb(f"ctst{i}", [P, 2, 128]) for i in range(2)]
    wglub = sb("wglub", [P, 4, 512], BF16)
    E_re = sb("E_re", [P, 16, 128]); E_im = sb("E_im", [P, 16, 128])
    s5c = sb("s5c", [P, 24, 16])
    hs = sb("hs", [P, 2, 16])
    u_f = sb("u_f", [P, TT]); u_b = sb("u_b", [P, TT], BF16); zs5 = sb("zs5", [P, 4, TT], BF16)
    bp = sb("bp", [P, 2, TT]); kk = sb("kk", [P, 2, TT]); tq = [sb(f"tq{i}", [P, TT]) for i in range(3)]
    hb = [sb(f"hb{i}", [P, 2, TT], BF16) for i in range(2)]
    g_f = sb("g_f", [P, 4, TT]); g_b = sb("g_b", [P, 4, TT], BF16)
    tiny = sb("tiny", [P, 8])
    cbuf = [sb(f"cbuf{i}", [P, 3 + TT], BF16) for i in range(2)]
    halo = sb("halo", [P, 28, 3], BF16)
    dg = [sb(f"dg{i}", [P, 4, 128], BF16) for i in range(2)]
    q_b = sb("q_b", [P, 4, TT], BF16); k_b = sb("k_b", [P, 4, TT], BF16)
    oz = sb("oz", [P, 8, TT], BF16); osig = sb("osig", [P, TT], BF16)
    v_b = sb("v_b", [P, NCH, 4, 193], BF16)
    cst_f = sb("cst_f", [P, 4, 193]); cst_b = sb("cst_b", [P, 4, 192], BF16); nbc_b = sb("nbc_b", [P, 4, 96], BF16)
    x_f = sb("x_f", [P, 12, TT]); B_b = sb("B_b", [P, 4, TT], BF16); C_b = sb("C_b", [P, 4, TT], BF16)
    zssd = sb("zssd", [P, 12, TT], BF16)
    sst_f = sb("sst_f", [P, 12, 64]); sst_b = sb("sst_b", [P, 12, 64], BF16)
    yz = sb("yz", [P, 12, CH]); sqb = sb("sqb", [P, 12, CH], BF16)
    btok = sb("btok", [P, 128], BF16); cbm = sb("cbm", [P, 128])
    _xff = xf[:].rearrange("p k t -> p (k t)")
    _xsf = x_f[:].rearrange("p k t -> p (k t)")
    etmp = [_xff[:, 0:2048].rearrange("p (s j) -> p s j", s=16), _xff[:, 2048:4096].rearrange("p (s j) -> p s j", s=16),
            _xsf[:, 0:2048].rearrange("p (s j) -> p s j", s=16)]
    gsb = sb("gsb", [P, NCH, 20]); tmpg = sb("tmpg", [P, NCH, 16]); vals = sb("vals", [P, NCH, 16])
    dtv = sb("dtv", [P, NCH, 12]); cum = sb("cum", [P, NCH, 16]); bml = sb("bml", [P, NCH, 4]); ncum = sb("ncum", [P, NCH, 16])
    arow = sb("arow", [P, 12])
    lbh = [sb(f"lbh{i}", [P, 128], BF16) for i in range(2)]
    lbl = [sb(f"lbl{i}", [P, 128], BF16) for i in range(2)]
    vh = sb("vh", [P, NCH, 16], BF16); vl = sb("vl", [P, NCH, 16], BF16); vtmp = sb("vtmp", [P, 16])
    tri_b = sb("tri_b", [P, 128], BF16)
    xh = [sb(f"xh{i}", [P, CH], BF16) for i in range(2)]; xl = [sb(f"xl{i}", [P, CH], BF16) for i in range(2)]
    zb = [sb(f"zb{i}", [P, TT], BF16) for i in range(2)]
    ebc = [sb(f"ebc{i}", [P, 128]) for i in range(2)]
    tf = [sb(f"tf{i}", [P, 128]) for i in range(2)]
    dT = [sb(f"dT{i}", [P, 128]) for i in range(2)]
    A_b = [sb(f"A_b{i}", [P, 128], BF16) for i in range(2)]
    qs_b = [sb(f"qs_b{i}", [P, 128], BF16) for i in range(2)]
    Cs_b = [sb(f"Cs_b{i}", [P, 128], BF16) for i in range(2)]
    dtx_b = [sb(f"dtx_b{i}", [P, 64], BF16) for i in range(2)]
    dtxw_b = [sb(f"dtxw_b{i}", [P, 64], BF16) for i in range(2)]
    kw_b = [sb(f"kw_b{i}", [P, 96], BF16) for i in range(2)]
    wcol = [sb(f"wcol{i}", [P, 2]) for i in range(2)]
    hn = sb("hn", [P, 2, CH]); sqm = sb("sqm", [P, 2, CH], BF16); rd = sb("rd", [P, CH]); rs = sb("rs", [P, CH])
    yv = [sb(f"yv{i}", [P, CH]) for i in range(2)]
    mean = sb("mean", [P, TT]); m2 = sb("m2", [P, TT]); rstd = sb("rstd", [P, TT]); sqz = [None, None]; sqzb = [sb(f"sqzb{i}", [P, TT], BF16) for i in range(2)]
    ybuf = [sb(f"ybuf{i}", [P, TT]) for i in range(4)]
    lt = [sb(f"lt{i}", [P, TT]) for i in range(2)]

    pa = psum("pa", [P, 4, 256])
    ps_b = psum("ps_b", [P, 2, 256])
    ps_y = psum("ps_y", [P, 2, 256])
    ps_bc = psum("ps_bc", [P, 4, 128])
    ps_s = psum("ps_s", [P, 4, 128])
    ps_o = psum("ps_o", [P, 4, 128])
    ps_st = psum("ps_st", [P, 512])

    state = {"pa": 0, "wb": 0, "cb": 0, "yb": 0, "par": 0}
    al = {"pa0": "B_pa0", "pa1": "B_pa0", "pa2": "B_pa1", "pa3": "B_pa1", "ps_b0": "B_b", "ps_b1": "B_b",
          "ps_y0": "B_y", "ps_y1": "B_y", "ps_bc0": "B_bc", "ps_bc1": "B_bc",
          "ps_s0": "B_s", "ps_s1": "B_s", "ps_s1b": "B_s", "ps_s1c": "B_s", "ps_s2": "B_s", "ps_s3": "B_s",
          "ps_o0": "B_o", "ps_o1": "B_o", "ps_o2": "B_o", "ps_st": "B_st", "ps_stb0": "B_st", "ps_stb1": "B_st", "ps_stb2": "B_st"}
    S.ALIAS = al

    def next_pa():
        i = state["pa"]; state["pa"] = (i + 1) % 4
        return i

    def DVE(fn, r, w): return S.op("dve", fn, r, w)
    def ACT(fn, r, w): return S.op("act", fn, r, w)
    def PE(fn, r, w): return S.op("pe", fn, r, w)

    def mm(out, lhsT, rhs, start, stop, r, w):
        return PE(lambda e: e.matmul(out, lhsT=lhsT, rhs=rhs, start=start, stop=stop), r, w)

    def tt(out, a, b, op, r, w): return DVE(lambda e: e.tensor_tensor(out=out, in0=a, in1=b, op=op), r, w)
    def ts(out, a, s1, s2, op0, op1, r, w): return DVE(lambda e: e.tensor_scalar(out=out, in0=a, scalar1=s1, scalar2=s2, op0=op0, op1=op1), r, w)
    def ts1(out, a, s1, op0, r, w): return DVE(lambda e: e.tensor_single_scalar(out=out, in_=a, scalar=s1, op=op0), r, w)
    def stt(out, a, sc, b, op0, op1, r, w): return DVE(lambda e: e.scalar_tensor_tensor(out=out, in0=a, scalar=sc, in1=b, op0=op0, op1=op1), r, w)
    def cp(out, a, r, w): return DVE(lambda e: e.tensor_copy(out=out, in_=a), r, w)
    def act(out, a, func, r, w, bias=None, scale=None):
        kw = {}
        if bias is not None: kw["bias"] = bias
        if scale is not None: kw["scale"] = scale
        return ACT(lambda e: e.activation(out=out, in_=a, func=func, **kw), r, w)

    S.op("pool", lambda e: e.memset(ident_f[:], 1.0), [], ["ident_f"])
    S.op("pool", lambda e: e.affine_select(out=ident_f[:], in_=ident_f[:], pattern=[[-1, 128]], compare_op=ALU.is_equal, fill=0.0, base=0, channel_multiplier=1), ["ident_f"], ["ident_f"])
    S.op("pool", lambda e: e.memset(tri_f[:], 1.0), [], ["tri_f"])
    S.op("pool", lambda e: e.affine_select(out=tri_f[:], in_=tri_f[:], pattern=[[1, 128]], compare_op=ALU.is_ge, fill=0.0, base=0, channel_multiplier=-1), ["tri_f"], ["tri_f"])
    S.op("pool", lambda e: e.memset(ones_f[:], 1.0), [], ["ones_f"])
    S.op("pool", lambda e: e.memset(ones_b[:], 1.0), [], ["ones_b"])
    S.op("pool", lambda e: e.memset(v_b[:], 1.0), [], ["v_b"])
    cp(ident_b[:], ident_f[:], ["ident_f"], ["ident_b"])
    cp(tri_b[:], tri_f[:], ["tri_f"], ["tri_b"])
    S.dma("sp", lambda e: e.dma_start(out=jrow[:], in_=jrow_d), "jrow", [], ["jrow"])

    WKEYS = {}
    for l in range(L):
        keys = []
        for part in range(16):
            rs_ = slice(part * 128, (part + 1) * 128)
            k_ = f"winb{l}_{part}"
            S.dma("pool", lambda e, l=l, rs_=rs_: e.dma_start(out=winb_d[l][rs_, :], in_=win_d[l][rs_, :]), f"wcast{l}", [], [k_])
            keys.append(k_)
        for part in range(4):
            rs_ = slice(part * 512, (part + 1) * 512)
            k_ = f"woutb{l}_{part}"
            S.dma("pool", lambda e, l=l, rs_=rs_: e.dma_start(out=woutb_d[l][rs_, :], in_=wout_d[l][rs_, :]), f"wcast{l}", [], [k_])
            keys.append(k_)
        WKEYS[l] = keys

    def range_reduce_sin(out, ang, quarter, neg, keys_r, key_w, t0, t1, k0, k1):
        ts(t0, ang, 1.0 / TWO_PI, 0.25 * quarter, ALU.mult, ALU.add, keys_r, [k0])
        ts1(t0, t0, MAGIC, ALU.add, [k0], [k0])
        ts1(t0, t0, -MAGIC, ALU.add, [k0], [k0])
        stt(t1, t0, -CW1, ang, ALU.mult, ALU.add, keys_r + [k0], [k1])
        stt(t1, t0, -CW2, t1, ALU.mult, ALU.add, [k0, k1], [k1])
        lo = -math.pi - quarter * math.pi / 2
        ts(t1, t1, lo, lo + TWO_PI, ALU.max, ALU.min, [k1], [k1])
        if neg:
            act(out, t1, AF.Sin, [k1], [key_w], bias=-quarter * math.pi / 2, scale=-1.0)
        else:
            act(out, t1, AF.Sin, [k1], [key_w], bias=quarter * math.pi / 2, scale=1.0)

    def c(i):
        return s5c[:, i, :]

    def _phase(n):
        if phase <= n:
            raise _Stop()

    try:
        for li in range(L):
            xsrc = xT_d if li == 0 else xs_d[(li - 1) % 2]
            xdst = yT_d if li == L - 1 else xs_d[li % 2]
            xsrc_key = "xT" if li == 0 else f"xs{(li - 1) % 2}"
            xdst_key = "yT" if li == L - 1 else f"xs{li % 2}"

            if li > 0:
                for yb_ in range(4):
                    ent_ = S.dsem.get(f"out{yb_}")
                    if ent_:
                        S.wait_tok("pool", (id(ent_[0]), ent_[1], "dma"))
                        S.wait_tok("sp", (id(ent_[0]), ent_[1], "dma"))
            S.dma("sp", lambda e, li=li: e.dma_start(out=pp[:], in_=pp_d[li]), "pp", [], ["pp"])
            S.dma("pool", lambda e, li=li: e.dma_start(out=btb[:].rearrange("p a s c -> p (a s c)"), in_=bt_d[li]), "btb", [], ["btb"])
            S.dma("pool", lambda e, li=li: e.dma_start(out=wglub[:], in_=wglu_d[li].rearrange("(k p) c -> p k c", p=P)), "wglub", [], ["wglub"])

            def pk(name, i=0, n=1, rows=P):
                o = PK[name] + i
                return pp[0:rows, o:o + n]

            lre = pp[:, PK["lre"]:PK["lre"] + 16]; lim = pp[:, PK["lim"]:PK["lim"] + 16]; lst = pp[:, PK["lst"]:PK["lst"] + 16]
            K = "s5c"
            act(c(0), lst, AF.Exp, ["pp"], [K])
            tt(c(1), lre, c(0), ALU.mult, ["pp", K], [K])
            tt(c(2), lim, c(0), ALU.mult, ["pp", K], [K])
            act(c(3), c(1), AF.Exp, [K], [K])
            range_reduce_sin(c(4), c(2), 0, False, [K], K, c(20), c(21), K, K)
            range_reduce_sin(c(5), c(2), 1, False, [K], K, c(20), c(21), K, K)
            tt(c(6), c(3), c(5), ALU.mult, [K], [K])
            tt(c(7), c(3), c(4), ALU.mult, [K], [K])
            ts1(c(8), c(6), -1.0, ALU.add, [K], [K])
            tt(c(9), lre, lre, ALU.mult, ["pp"], [K])
            tt(c(10), lim, lim, ALU.mult, ["pp"], [K])
            tt(c(9), c(9), c(10), ALU.add, [K], [K])
            DVE(lambda e: e.reciprocal(out=c(9), in_=c(9)), [K], [K])
            tt(c(10), c(8), lre, ALU.mult, [K, "pp"], [K])
            tt(c(11), c(7), lim, ALU.mult, [K, "pp"], [K])
            tt(c(10), c(10), c(11), ALU.add, [K], [K])
            tt(c(12), c(10), c(9), ALU.mult, [K], [K])
            tt(c(10), c(7), lre, ALU.mult, [K, "pp"], [K])
            tt(c(11), c(8), lim, ALU.mult, [K, "pp"], [K])
            tt(c(10), c(10), c(11), ALU.subtract, [K], [K])
            tt(c(13), c(10), c(9), ALU.mult, [K], [K])
            ts1(c(14), c(12), -1.0, ALU.mult, [K], [K])
            ts1(c(15), c(13), -1.0, ALU.mult, [K], [K])
            for s in range(16):
                ts1(etmp[0][:, s, :], jrow[:], c(2)[:, s:s + 1], ALU.mult, ["jrow", K], ["xf"])
            E_re_fl = E_re[:].rearrange("p s j -> p (s j)"); E_im_fl = E_im[:].rearrange("p s j -> p (s j)")
            range_reduce_sin(E_re_fl, _xff[:, 0:2048], 1, False, ["xf"], "E_re", _xff[:, 2048:4096], _xsf[:, 0:2048], "xf", "x_f")
            range_reduce_sin(E_im_fl, _xff[:, 0:2048], 0, True, ["xf"], "E_im", _xff[:, 2048:4096], _xsf[:, 0:2048], "xf", "x_f")
            for s in range(16):
                b = s % 2
                S.dma("sp", lambda e, li=li, s=s, b=b: e.dma_start(out=ctst[b][:], in_=ct_d[li][:, :, s, :]), f"ctst{b}", [], [f"ctst{b}"])
                ts1(tq[0][:, 0:128], ctst[b][:, 0, :], c(12)[:, s:s + 1], ALU.mult, [f"ctst{b}", K], ["tq0"])
                stt(ckb[:, 0, s, :], ctst[b][:, 1, :], c(15)[:, s:s + 1], tq[0][:, 0:128], ALU.mult, ALU.add, [f"ctst{b}", K, "tq0"], ["ckb"])
                ts1(tq[1][:, 0:128], ctst[b][:, 0, :], c(15)[:, s:s + 1], ALU.mult, [f"ctst{b}", K], ["tq1"])
                stt(ckb[:, 1, s, :], ctst[b][:, 1, :], c(14)[:, s:s + 1], tq[1][:, 0:128], ALU.mult, ALU.add, [f"ctst{b}", K, "tq1"], ["ckb"])
            act(arow[:], pp[:, PK["al"]:PK["al"] + 12], AF.Exp, ["pp"], ["arow"])
            ts1(arow[:], arow[:], -1.0, ALU.mult, ["arow"], ["arow"])
            S.op("pool", lambda e: e.memset(hs[:], 0.0), [], ["hs"])
            S.op("pool", lambda e: e.memset(halo[:], 0.0), [], ["halo"])
            S.op("pool", lambda e: e.memset(cst_f[:], 0.0), [], ["cst_f"])
            S.op("pool", lambda e: e.memset(cst_b[:], 0.0), [], ["cst_b"])
            S.op("pool", lambda e: e.memset(nbc_b[:], 0.0), [], ["nbc_b"])
            S.op("pool", lambda e: e.memset(sst_f[:], 0.0), [], ["sst_f"])
            S.op("pool", lambda e: e.memset(sst_b[:], 0.0), [], ["sst_b"])

            _phase(0)
            win_v = winb_d[li].rearrange("(k p) c -> p k c", p=P)

            def load_chunk(name):
                c0, w = CH_OFF[name]
                b = state["wb"]; state["wb"] = 1 - b
                view = wfl[b][:, 0:16 * w].rearrange("p (k c) -> p k c", k=16)
                src = win_v[:, :, c0:c0 + w]
                S.dma("sp", lambda e, view=view, src=src: e.dma_start(out=view, in_=src), f"wfl{b}", WKEYS[li], [f"wfl{b}"])
                return view, f"wfl{b}"

            def proj_fm(wv, wkey, c0, M):
                i = next_pa()
                for k in range(KT):
                    mm(pa[0:M, i, :], wv[:, k, c0:c0 + M], xb[:, k, :], k == 0, k == KT - 1, [wkey, "xb"], [f"pa{i}"])
                return i

            def conv_tile(i, M, tile, out_ap, out_key):
                b = state["cb"]; state["cb"] = 1 - b
                ck = f"cbuf{b}"
                act(cbuf[b][0:M, 3:3 + TT], pa[0:M, i, :], AF.Copy, [f"pa{i}"], [ck])
                cp(cbuf[b][0:M, 0:3], halo[0:M, tile, :], ["halo"], [ck])
                cp(halo[0:M, tile, :], cbuf[b][0:M, TT:TT + 3], [ck], ["halo"])
                for k in range(4):
                    act(dg[b][0:M, k, 0:M], ident_f[0:M, 0:M], AF.Copy, ["ident_f", "pp"], [f"dg{b}"],
                        scale=pp[0:M, PK["cw"] + tile * 4 + k:PK["cw"] + tile * 4 + k + 1])
                j = next_pa()
                for k in range(4):
                    mm(pa[0:M, j, :], dg[b][0:M, k, 0:M], cbuf[b][0:M, k:k + TT], k == 0, k == 3, [f"dg{b}", ck], [f"pa{j}"])
                act(out_ap, pa[0:M, j, :], AF.Silu, [f"pa{j}", "pp"], [out_key], bias=pp[0:M, PK["cb"] + tile:PK["cb"] + tile + 1])

            for ti in range(NT):
                t0 = ti * TT
                S.dma("pool", lambda e, t0=t0, xsrc=xsrc: e.dma_start(out=xb[:], in_=xsrc.rearrange("(k p) t -> p k t", p=P)[:, :, t0:t0 + TT]), "xb", [xsrc_key], ["xb"])
                S.dma("sp", lambda e, t0=t0, xsrc=xsrc: e.dma_start(out=xf[:], in_=xsrc.rearrange("(k p) t -> p k t", p=P)[:, :, t0:t0 + TT]), "xf", [xsrc_key], ["xf"])

                _phase(0.3)
                wv, wkey = load_chunk("gates")
                _phase(0.4)
                for cc in range(NCH):
                    for k in range(KT):
                        mm(ps_st[:, 0:20], xb[:, k, cc * CH:(cc + 1) * CH], wv[:, k, 0:20], k == 0, k == KT - 1, ["xb", wkey], ["ps_st"])
                    _phase(0.5 if cc == 0 else 0.96 + (0.5 - 0.5) * 0.06)
                    tt(gsb[:, cc, :], ps_st[:, 0:20], pp[:, PK["gb"]:PK["gb"] + 20], ALU.add, ["ps_st", "pp"], ["gsb"])
                    _phase(0.6 if cc == 0 else 0.96 + (0.6 - 0.5) * 0.06)
                    act(tmpg[:, cc, 0:4], gsb[:, cc, 4:8], AF.Exp, ["gsb"], ["tmpg"], scale=-1.0)
                    act(tmpg[:, cc, 4:16], gsb[:, cc, 8:20], AF.Exp, ["gsb"], ["tmpg"])
                    act(vals[:, cc, 0:4], tmpg[:, cc, 0:4], AF.Ln, ["tmpg"], ["vals"], bias=1.0)
                    act(dtv[:, cc, :], tmpg[:, cc, 4:16], AF.Ln, ["tmpg"], ["dtv"], bias=1.0)
                    ts1(vals[:, cc, 0:4], vals[:, cc, 0:4], -1.0, ALU.mult, ["vals"], ["vals"])
                    tt(vals[:, cc, 4:16], dtv[:, cc, :], arow[:], ALU.mult, ["dtv", "arow"], ["vals"])
                    _phase(0.7 if cc == 0 else 0.96 + (0.7 - 0.5) * 0.06)
                    cp(vh[:, cc, :], vals[:, cc, :], ["vals"], ["vh"])
                    tt(vtmp[:], vals[:, cc, :], vh[:, cc, :], ALU.subtract, ["vals", "vh"], ["vtmp"])
                    cp(vl[:, cc, :], vtmp[:], ["vtmp"], ["vl"])
                    mm(ps_s[:, 1, 32:48], tri_b[:], vh[:, cc, :], True, False, ["tri_b", "vh"], ["ps_s1b"])
                    mm(ps_s[:, 1, 32:48], tri_b[:], vl[:, cc, :], False, True, ["tri_b", "vl"], ["ps_s1b"])
                    _phase(0.8 if cc == 0 else 0.96 + (0.8 - 0.5) * 0.06)
                    cp(cum[:, cc, :], ps_s[:, 1, 32:48], ["ps_s1b"], ["cum"])
                    _phase(0.85 if cc == 0 else 0.96 + (0.85 - 0.5) * 0.06)
                    tt(bml[:, cc, :], gsb[:, cc, 0:4], cum[:, cc, 0:4], ALU.subtract, ["gsb", "cum"], ["bml"])
                    _phase(0.9 if cc == 0 else 0.96 + (0.9 - 0.5) * 0.06)
                    ts1(ncum[:, cc, :], cum[:, cc, :], -1.0, ALU.mult, ["cum"], ["ncum"])
                    _phase(0.95 if cc == 0 else 0.96 + (0.95 - 0.5) * 0.06)

                _phase(1)
                for kt in range(4):
                    wv, wkey = load_chunk(f"s5_{kt}")
                    iu = proj_fm(wv, wkey, 0, 128)
                    iz = proj_fm(wv, wkey, 128, 128)
                    act(u_f[:], pa[:, iu, :], AF.Copy, [f"pa{iu}"], ["u_f"])
                    cp(u_b[:], pa[:, iu, :], [f"pa{iu}"], ["u_b"])
                    act(zs5[:, kt, :], pa[:, iz, :], AF.Silu, [f"pa{iz}"], ["zs5"])
                    for sl in range(4):
                        s = kt * 4 + sl
                        hbuf = hb[s % 2]; hk = f"hb{s % 2}"
                        mm(ps_b[:, 0, :], btb[:, 0, s, :], u_b[:], True, True, ["btb", "u_b"], ["ps_b0"])
                        mm(ps_b[:, 1, :], btb[:, 1, s, :], u_b[:], True, True, ["btb", "u_b"], ["ps_b1"])
                        for a in range(2):
                            sl_ = slice(a * 128, (a + 1) * 128)
                            er = E_re[:, s, :]; ei = E_im[:, s, :]
                            tt(tq[0][:, sl_], ps_b[:, 0, sl_], er, ALU.mult, ["ps_b0", "E_re"], ["tq0"])
                            tt(tq[1][:, sl_], ps_b[:, 1, sl_], ei, ALU.mult, ["ps_b1", "E_im"], ["tq1"])
                            tt(bp[:, 0, sl_], tq[0][:, sl_], tq[1][:, sl_], ALU.subtract, ["tq0", "tq1"], ["bp0"])
                            tt(tq[0][:, sl_], ps_b[:, 1, sl_], er, ALU.mult, ["ps_b1", "E_re"], ["tq0"])
                            tt(tq[1][:, sl_], ps_b[:, 0, sl_], ei, ALU.mult, ["ps_b0", "E_im"], ["tq1"])
                            tt(bp[:, 1, sl_], tq[0][:, sl_], tq[1][:, sl_], ALU.add, ["tq0", "tq1"], ["bp1"])
                        for a in range(2):
                            sl_ = slice(a * 128, (a + 1) * 128)
                            rbc = c(3)[:, s:s + 1].to_broadcast([P, 128])
                            for ri in range(2):
                                DVE(lambda e, ri=ri, sl_=sl_, rbc=rbc, s=s: e.tensor_tensor_scan(out=kk[:, ri, sl_], data0=rbc, data1=bp[:, ri, sl_], initial=hs[:, ri, s:s + 1], op0=ALU.mult, op1=ALU.add),
                                    [K, f"bp{ri}", "hs"], [f"kk{ri}"])
                            last = a * 128 + 127
                            erl = E_re[:, s, 127:128]; eil = E_im[:, s, 127:128]
                            tt(tiny[:, 0:1], kk[:, 1, last:last + 1], eil, ALU.mult, ["kk1", "E_im"], ["tiny0"])
                            tt(tiny[:, 1:2], kk[:, 0, last:last + 1], eil, ALU.mult, ["kk0", "E_im"], ["tiny1"])
                            stt(hs[:, 0, s:s + 1], kk[:, 0, last:last + 1], erl, tiny[:, 0:1], ALU.mult, ALU.add, ["kk0", "E_re", "tiny0"], ["hs"])
                            stt(hs[:, 1, s:s + 1], kk[:, 1, last:last + 1], erl, tiny[:, 1:2], ALU.mult, ALU.subtract, ["kk1", "E_re", "tiny1"], ["hs"])
                        for a in range(2):
                            sl_ = slice(a * 128, (a + 1) * 128)
                            er = E_re[:, s, :]; ei = E_im[:, s, :]
                            tt(tq[0][:, sl_], kk[:, 0, sl_], er, ALU.mult, ["kk0", "E_re"], ["tq0"])
                            tt(tq[1][:, sl_], kk[:, 1, sl_], ei, ALU.mult, ["kk1", "E_im"], ["tq1"])
                            tt(hbuf[:, 0, sl_], tq[0][:, sl_], tq[1][:, sl_], ALU.add, ["tq0", "tq1"], [hk])
                            tt(tq[0][:, sl_], kk[:, 1, sl_], er, ALU.mult, ["kk1", "E_re"], ["tq0"])
                            tt(tq[1][:, sl_], kk[:, 0, sl_], ei, ALU.mult, ["kk0", "E_im"], ["tq1"])
                            tt(hbuf[:, 1, sl_], tq[0][:, sl_], tq[1][:, sl_], ALU.subtract, ["tq0", "tq1"], [hk])
                        mm(ps_y[:, 0, :], ckb[:, 0, s, :], hbuf[:, 0, :], sl == 0, False, ["ckb", hk], ["ps_y0"])
                        mm(ps_y[:, 0, :], ckb[:, 1, s, :], hbuf[:, 1, :], False, sl == 3, ["ckb", hk], ["ps_y0"])
                    stt(tq[2][:], u_f[:], pk("s5d", kt), ps_y[:, 0, :], ALU.mult, ALU.add, ["u_f", "pp", "ps_y0"], ["tq2"])
                    tt(tq[0][:], tq[2][:], tq[2][:], ALU.mult, ["tq2"], ["tq0"])
                    ts(tq[0][:], tq[0][:], 0.044715, 1.0, ALU.mult, ALU.add, ["tq0"], ["tq0"])
                    tt(tq[0][:], tq[0][:], tq[2][:], ALU.mult, ["tq0", "tq2"], ["tq0"])
                    act(tq[1][:], tq[0][:], AF.Sigmoid, ["tq0"], ["tq1"], scale=2.0 * math.sqrt(2.0 / math.pi))
                    tt(g_f[:, kt, :], tq[2][:], tq[1][:], ALU.mult, ["tq2", "tq1"], ["g_f"])
                    cp(g_b[:, kt, :], g_f[:, kt, :], ["g_f"], ["g_b"])
                for mt in range(4):
                    for k in range(4):
                        mm(ps_y[:, 1, :], wglub[:, k, mt * 128:(mt + 1) * 128], g_b[:, k, :], k == 0, k == 3, ["wglub", "g_b"], ["ps_y1"])
                    act(tq[1][:], ps_y[:, 1, :], AF.Sigmoid, ["ps_y1", "pp"], ["tq1"], bias=pk("bglu", mt))
                    tt(tq[0][:], g_f[:, mt, :], tq[1][:], ALU.mult, ["g_f", "tq1"], ["tq0"])
                    tt(mx_s5[:, mt, :], tq[0][:], zs5[:, mt, :], ALU.mult, ["tq0", "zs5"], ["mx_s5"])

                _phase(2)
                for h in range(4):
                    wv, wkey = load_chunk(f"mlA_{h}")
                    i = proj_fm(wv, wkey, 0, 96)
                    conv_tile(i, 96, h, q_b[0:96, h, :], "q_b")
                    i = proj_fm(wv, wkey, 96, 96)
                    conv_tile(i, 96, 4 + h, k_b[0:96, h, :], "k_b")
                    wvB, wkB = load_chunk(f"mlB_{h}")
                    wvC, wkC = load_chunk(f"mlC_{h}")
                    for e2 in range(2):
                        i = proj_fm(wvB, wkB, e2 * 96, 96)
                        act(osig[0:96, :], pa[0:96, i, :], AF.Sigmoid, [f"pa{i}"], ["osig"])
                        i = proj_fm(wvC, wkC, e2 * 96, 96)
                        act(tq[0][0:96, :], pa[0:96, i, :], AF.Silu, [f"pa{i}"], ["tq0"])
                        tt(oz[0:96, 2 * h + e2, :], tq[0][0:96, :], osig[0:96, :], ALU.mult, ["tq0", "osig"], ["oz"])
                    wv, wkey = load_chunk(f"mlV_{h}")
                    for cc in range(NCH):
                        i = next_pa()
                        for k in range(KT):
                            mm(pa[:, i, 0:192], xb[:, k, cc * CH:(cc + 1) * CH], wv[:, k, 0:192], k == 0, k == KT - 1, ["xb", wkey], [f"pa{i}"])
                        act(v_b[:, cc, h, 0:192], pa[:, i, 0:192], AF.Copy, [f"pa{i}"], ["v_b"])
                _phase(3)
                for g in range(4):
                    wv, wkey = load_chunk(f"sX_{g}")
                    for r in range(3):
                        i = proj_fm(wv, wkey, r * 64, 64)
                        conv_tile(i, 64, 8 + g * 5 + r, x_f[0:64, 3 * g + r, :], "x_f")
                    wv, wkey = load_chunk(f"sBC_{g}")
                    i = proj_fm(wv, wkey, 0, 128)
                    conv_tile(i, 128, 8 + g * 5 + 3, B_b[:, g, :], "B_b")
                    i = proj_fm(wv, wkey, 128, 128)
                    conv_tile(i, 128, 8 + g * 5 + 4, C_b[:, g, :], "C_b")
                    wv, wkey = load_chunk(f"sZ_{g}")
                    for r in range(3):
                        i = proj_fm(wv, wkey, r * 64, 64)
                        act(zssd[0:64, 3 * g + r, :], pa[0:64, i, :], AF.Silu, [f"pa{i}"], ["zssd"])

                _phase(4)
                def get_bc(cc, col):
                    par = state["par"]; state["par"] = 1 - par
                    slot = par
                    S.op("pool", lambda e, par=par, cc=cc, col=col: e.tensor_copy(out=lbh[par][:], in_=vh[:, cc, col:col + 1].to_broadcast([P, 128])), ["vh"], [f"lbh{par}"])
                    S.op("pool", lambda e, par=par, cc=cc, col=col: e.tensor_copy(out=lbl[par][:], in_=vl[:, cc, col:col + 1].to_broadcast([P, 128])), ["vl"], [f"lbl{par}"])
                    mm(ps_bc[:, slot, :], lbh[par][:], tri_b[:], True, False, [f"lbh{par}", "tri_b"], [f"ps_bc{slot}"])
                    mm(ps_bc[:, slot, :], lbl[par][:], tri_b[:], False, True, [f"lbl{par}", "tri_b"], [f"ps_bc{slot}"])
                    return par, slot

                for cc in range(NCH):
                    cs = slice(cc * CH, (cc + 1) * CH)
                    for h in range(4):
                        par, slot = get_bc(cc, h)
                        bck = f"ps_bc{slot}"
                        act(ebc[par][:], ps_bc[:, slot, :], AF.Exp, [bck], [f"ebc{par}"])
                        ts(tf[par][:], ps_bc[:, slot, :], ncum[:, cc, h:h + 1], 0.0, ALU.add, ALU.min, [bck, "ncum"], [f"tf{par}"])
                        act(dT[par][:], tf[par][:], AF.Exp, [f"tf{par}", "gsb"], [f"dT{par}"], bias=gsb[:, cc, h:h + 1])
                        tt(dT[par][:], dT[par][:], tri_f[:], ALU.mult, [f"dT{par}", "tri_f"], [f"dT{par}"])
                        mm(ps_s[:, 0, :], k_b[0:96, h, cs], q_b[0:96, h, cs], True, True, ["k_b", "q_b"], ["ps_s0"])
                        stt(A_b[par][:], ps_s[:, 0, :], DQK ** -0.5, dT[par][:], ALU.mult, ALU.mult, ["ps_s0", f"dT{par}"], [f"A_b{par}"])
                        stt(qs_b[par][0:96, :], q_b[0:96, h, cs], DQK ** -0.5, ebc[par][0:96, :], ALU.mult, ALU.mult, ["q_b", f"ebc{par}"], [f"qs_b{par}"])
                        for e2 in range(2):
                            mm(ps_o[0:96, e2, :], v_b[:, cc, h, e2 * 96:(e2 + 1) * 96], A_b[par][:], True, False, ["v_b", f"A_b{par}"], [f"ps_o{e2}"])
                            mm(ps_o[0:96, e2, :], cst_b[0:96, h, e2 * 96:(e2 + 1) * 96], qs_b[par][0:96, :], False, True, ["cst_b", f"qs_b{par}"], [f"ps_o{e2}"])
                        mm(ps_o[0:96, 2, :], ones_b[:, 0:96], A_b[par][:], True, False, ["ones_b", f"A_b{par}"], ["ps_o2"])
                        mm(ps_o[0:96, 2, :], nbc_b[0:96, h, :], qs_b[par][0:96, :], False, True, ["nbc_b", f"qs_b{par}"], ["ps_o2"])
                        act(rd[0:96, :], ps_o[0:96, 2, :], AF.Abs, ["ps_o2"], ["rd"])
                        ts1(rd[0:96, :], rd[0:96, :], 1.0, ALU.max, ["rd"], ["rd"])
                        DVE(lambda e: e.reciprocal(out=rd[0:96, :], in_=rd[0:96, :]), ["rd"], ["rd"])
                        for e2 in range(2):
                            tt(hn[0:96, e2, :], ps_o[0:96, e2, :], rd[0:96, :], ALU.mult, [f"ps_o{e2}", "rd"], ["hn"])
                        tt(sqm[0:96, :, :], hn[0:96, :, :], hn[0:96, :, :], ALU.mult, ["hn"], ["sqm"])
                        for e2 in range(2):
                            mm(ps_s[0:96, 3, :], ones_b[0:96, 0:96], sqm[0:96, e2, :], e2 == 0, e2 == 1, ["ones_b", "sqm"], ["ps_s3"])
                        act(rs[0:96, :], ps_s[0:96, 3, :], AF.Sqrt, ["ps_s3"], ["rs"], bias=EPS, scale=1.0 / DV)
                        DVE(lambda e: e.reciprocal(out=rs[0:96, :], in_=rs[0:96, :]), ["rs"], ["rs"])
                        for e2 in range(2):
                            stt(yv[e2][0:96, :], hn[0:96, e2, :], pk("mlg", 2 * h + e2, 1, 96), rs[0:96, :], ALU.mult, ALU.mult, ["hn", "pp", "rs"], [f"yv{e2}"])
                            tt(mx_ml[0:96, 2 * h + e2, cs], yv[e2][0:96, :], oz[0:96, 2 * h + e2, cs], ALU.mult, [f"yv{e2}", "oz"], ["mx_ml"])
                        mm(ps_s[:, 2, 0:96], k_b[0:96, h, cs], ident_b[0:96, 0:96], True, True, ["k_b", "ident_b"], ["ps_s2"])
                        tt(wcol[par][:, 0:1], ps_bc[:, slot, 127:128], bml[:, cc, h:h + 1], ALU.add, [bck, "bml"], [f"wcol{par}"])
                        act(wcol[par][:, 1:2], wcol[par][:, 0:1], AF.Exp, [f"wcol{par}"], [f"wcolb{par}"])
                        ts1(kw_b[par][:], ps_s[:, 2, 0:96], wcol[par][:, 1:2], ALU.mult, ["ps_s2", f"wcolb{par}"], [f"kw_b{par}"])
                        mm(ps_st[0:96, 0:193], kw_b[par][:], v_b[:, cc, h, :], True, True, [f"kw_b{par}", "v_b"], ["ps_st"])
                        stt(cst_f[0:96, h, :], cst_f[0:96, h, :], ebc[par][0:96, 127:128], ps_st[0:96, 0:193], ALU.mult, ALU.add, ["cst_f", f"ebc{par}", "ps_st"], ["cst_f"])
                        cp(cst_b[0:96, h, :], cst_f[0:96, h, 0:192], ["cst_f"], ["cst_b"])
                        cp(nbc_b[0:96, h, :], cst_f[0:96, h, 192:193].to_broadcast([96, 96]), ["cst_f"], ["nbc_b"])
                    for g in range(4):
                        mm(ps_s[:, 0, :], B_b[:, g, cs], C_b[:, g, cs], True, True, ["B_b", "C_b"], ["ps_s0"])
                        tt(cbm[:], ps_s[:, 0, :], tri_f[:], ALU.mult, ["ps_s0", "tri_f"], ["cbm"])
                        mm(ps_s[:, 2, :], B_b[:, g, cs], ident_b[:], True, True, ["B_b", "ident_b"], ["ps_s2"])
                        act(btok[:], ps_s[:, 2, :], AF.Copy, ["ps_s2"], ["btok"])
                        for r in range(3):
                            hh = 3 * g + r
                            par, slot = get_bc(cc, 4 + hh)
                            bck = f"ps_bc{slot}"
                            act(ebc[par][:], ps_bc[:, slot, :], AF.Exp, [bck], [f"ebc{par}"])
                            ts(tf[par][:], ps_bc[:, slot, :], ncum[:, cc, 4 + hh:5 + hh], 0.0, ALU.add, ALU.min, [bck, "ncum"], [f"tf{par}"])
                            act(dT[par][:], tf[par][:], AF.Exp, [f"tf{par}"], [f"dT{par}"])
                            tt(A_b[par][:], dT[par][:], cbm[:], ALU.mult, [f"dT{par}", "cbm"], [f"A_b{par}"])
                            cp(xh[par][0:64, :], x_f[0:64, hh, cs], ["x_f"], [f"xh{par}"])
                            tt(yv[par][0:64, :], x_f[0:64, hh, cs], xh[par][0:64, :], ALU.subtract, ["x_f", f"xh{par}"], [f"yv{par}"])
                            cp(xl[par][0:64, :], yv[par][0:64, :], [f"yv{par}"], [f"xl{par}"])
                            mm(ps_s[:, 1, 64:128], xh[par][0:64, :], ident_b[0:64, 0:64], True, False, [f"xh{par}", "ident_b"], ["ps_s1c"])
                            mm(ps_s[:, 1, 64:128], xl[par][0:64, :], ident_b[0:64, 0:64], False, True, [f"xl{par}", "ident_b"], ["ps_s1c"])
                            ts1(dtx_b[par][:], ps_s[:, 1, 64:128], dtv[:, cc, hh:hh + 1], ALU.mult, ["ps_s1c", "dtv"], [f"dtx_b{par}"])
                            tt(Cs_b[par][:], C_b[:, g, cs], ebc[par][:], ALU.mult, ["C_b", f"ebc{par}"], [f"Cs_b{par}"])
                            mm(ps_o[0:64, r, :], dtx_b[par][:], A_b[par][:], True, False, [f"dtx_b{par}", f"A_b{par}"], [f"ps_o{r}"])
                            mm(ps_o[0:64, r, :], sst_b[:, hh, :], Cs_b[par][:], False, True, ["sst_b", f"Cs_b{par}"], [f"ps_o{r}"])
                            stt(yv[par][0:64, :], x_f[0:64, hh, cs], pk("sd", hh, 1, 64), ps_o[0:64, r, :], ALU.mult, ALU.add, ["x_f", "pp", f"ps_o{r}"], [f"yv{par}"])
                            tt(yz[0:64, hh, :], yv[par][0:64, :], zssd[0:64, hh, cs], ALU.mult, [f"yv{par}", "zssd"], ["yz"])
                            tt(wcol[par][:, 0:1], ps_bc[:, slot, 127:128], ncum[:, cc, 4 + hh:5 + hh], ALU.add, [bck, "ncum"], [f"wcol{par}"])
                            act(wcol[par][:, 1:2], wcol[par][:, 0:1], AF.Exp, [f"wcol{par}"], [f"wcolb{par}"])
                            ts1(dtxw_b[par][:], dtx_b[par][:], wcol[par][:, 1:2], ALU.mult, [f"dtx_b{par}", f"wcolb{par}"], [f"dtxw_b{par}"])
                            mm(ps_st[:, 256 + r * 64:256 + (r + 1) * 64], btok[:], dtxw_b[par][:], True, True, ["btok", f"dtxw_b{par}"], [f"ps_stb{r}"])
                            stt(sst_f[:, hh, :], sst_f[:, hh, :], ebc[par][:, 127:128], ps_st[:, 256 + r * 64:256 + (r + 1) * 64], ALU.mult, ALU.add, ["sst_f", f"ebc{par}", f"ps_stb{r}"], ["sst_f"])
                            cp(sst_b[:, hh, :], sst_f[:, hh, :], ["sst_f"], ["sst_b"])
                    act(sqb[0:64, :, :], yz[0:64, :, :], AF.Square, ["yz"], ["sqb"])
                    for hh in range(12):
                        mm(ps_s[0:64, 3, :], ones_b[0:64, 0:64], sqb[0:64, hh, :], hh == 0, hh == 11, ["ones_b", "sqb"], ["ps_s3"])
                    act(rs[0:64, :], ps_s[0:64, 3, :], AF.Sqrt, ["ps_s3"], ["rs"], bias=EPS, scale=1.0 / 768.0)
                    DVE(lambda e: e.reciprocal(out=rs[0:64, :], in_=rs[0:64, :]), ["rs"], ["rs"])
                    for hh in range(12):
                        stt(mx_ssd[0:64, hh, cs], yz[0:64, hh, :], pk("sg", hh, 1, 64), rs[0:64, :], ALU.mult, ALU.mult, ["yz", "pp", "rs"], ["mx_ssd"])

                _phase(5)
                if debug and li == 0:
                    S.dma("pool", lambda e, t0=t0: e.dma_start(out=dbg_d[0:4, :, t0:t0 + TT].rearrange("k p t -> p k t"), in_=mx_s5[:]), "dbg0", ["mx_s5"], [])
                    S.dma("pool", lambda e, t0=t0: e.dma_start(out=dbg_d[4:12, 0:96, t0:t0 + TT].rearrange("k p t -> p k t"), in_=mx_ml[0:96]), "dbg1", ["mx_ml"], [])
                    S.dma("pool", lambda e, t0=t0: e.dma_start(out=dbg_d[12:24, 0:64, t0:t0 + TT].rearrange("k p t -> p k t"), in_=mx_ssd[0:64]), "dbg2", ["mx_ssd"], [])
                for m in range(KT):
                    b = state["wb"]; state["wb"] = 1 - b
                    wv = wfl[b][:, 0:24 * 128].rearrange("p (k c) -> p k c", k=24)
                    wk = f"wfl{b}"
                    mc = slice(m * 128, (m + 1) * 128)
                    S.dma("sp", lambda e, wv=wv, mc=mc, li=li: e.dma_start(out=wv[:, 0:4, :], in_=woutb_d[li][0:512, mc].rearrange("(k p) c -> p k c", p=128)), wk + "a", WKEYS[li], [wk])
                    S.dma("sp", lambda e, wv=wv, mc=mc, li=li: e.dma_start(out=wv[0:96, 4:12, :], in_=woutb_d[li][512:1280, mc].rearrange("(k p) c -> p k c", p=96)), wk + "b", WKEYS[li], [wk + "_b"])
                    S.dma("sp", lambda e, wv=wv, mc=mc, li=li: e.dma_start(out=wv[0:64, 12:24, :], in_=woutb_d[li][1280:2048, mc].rearrange("(k p) c -> p k c", p=64)), wk + "c", WKEYS[li], [wk + "_c"])
                    i = next_pa()
                    n = 0
                    for k in range(4):
                        mm(pa[:, i, :], wv[:, k, :], mx_s5[:, k, :], n == 0, False, [wk, "mx_s5"], [f"pa{i}"]); n += 1
                    for k in range(8):
                        mm(pa[:, i, :], wv[0:96, 4 + k, :], mx_ml[0:96, k, :], False, False, [wk + "_b", "mx_ml"], [f"pa{i}"]); n += 1
                    for k in range(12):
                        mm(pa[:, i, :], wv[0:64, 12 + k, :], mx_ssd[0:64, k, :], False, k == 11, [wk + "_c", "mx_ssd"], [f"pa{i}"]); n += 1
                    S.readers.setdefault(wk, []).extend(S.readers.get(wk + "_b", []) + S.readers.get(wk + "_c", []))
                    S.last_write[wk + "_b"] = None; S.last_write[wk + "_c"] = None
                    S.readers[wk + "_b"] = []; S.readers[wk + "_c"] = []
                    stt(xf[:, m, :], xf[:, m, :], ALPHA, pa[:, i, :], ALU.mult, ALU.add, ["xf", f"pa{i}"], ["xf"])
                    sq = sqz[m % 2]
                    act(sq[:, 0:TT // 2].bitcast(BF16) if False else zb[m % 2][:], xf[:, m, :], AF.Copy, ["xf"], [f"zb{m % 2}"])
                    act(sqzb[m % 2][:], xf[:, m, :], AF.Square, ["xf"], [f"sqzb{m % 2}"])
                    mm(ps_b[:, 0, :], ones_b[:], zb[m % 2][:], m == 0, m == KT - 1, ["ones_b", f"zb{m % 2}"], ["ps_b0"])
                    mm(ps_y[:, 0, :], ones_b[:], sqzb[m % 2][:], m == 0, m == KT - 1, ["ones_b", f"sqzb{m % 2}"], ["ps_y0"])
                act(mean[:], ps_b[:, 0, :], AF.Copy, ["ps_b0"], ["mean"], scale=1.0 / D)
                tt(m2[:], mean[:], mean[:], ALU.mult, ["mean"], ["m2"])
                stt(m2[:], ps_y[:, 0, :], 1.0 / D, m2[:], ALU.mult, ALU.subtract, ["ps_y0", "m2"], ["m2"])
                act(rstd[:], m2[:], AF.Sqrt, ["m2"], ["rstd"], bias=EPS, scale=1.0)
                DVE(lambda e: e.reciprocal(out=rstd[:], in_=rstd[:]), ["rstd"], ["rstd"])
                out_toks = []
                for m in range(KT):
                    l_ = lt[m % 2]; lk = f"lt{m % 2}"
                    yb = state["yb"]; state["yb"] = (yb + 1) % 4
                    tt(l_[:], xf[:, m, :], mean[:], ALU.subtract, ["xf", "mean"], [lk])
                    tt(l_[:], l_[:], rstd[:], ALU.mult, [lk, "rstd"], [lk])
                    act(ybuf[yb][:], l_[:], AF.Identity, [lk, "pp"], [f"ybuf{yb}"], bias=pk("lnb", m), scale=pk("lng", m))
                    out_toks.append(S.dma("sp", lambda e, yb=yb, m=m, t0=t0, xdst=xdst: e.dma_start(out=xdst[m * 128:(m + 1) * 128, t0:t0 + TT], in_=ybuf[yb][:]), f"out{yb}", [f"ybuf{yb}"], [xdst_key]))
                state["out_toks"] = out_toks
    except _Stop:
        pass
    for t in state.get("out_toks", []):
        S.wait_tok("sp", t)
    for dk in ("dbg0", "dbg1", "dbg2"):
        ent = S.dsem.get(dk)
        if ent:
            S.wait_tok("pool", (id(ent[0]), ent[1], "dma"))
    for yb in range(4):
        ent = S.dsem.get(f"out{yb}")
        if ent:
            S.wait_tok("sp", (id(ent[0]), ent[1], "dma"))
    S.replay()
    return S


def _pack_layer_params(inp, l):
    f = np.float32
    pp = np.zeros((P, NPK), f)
    pp[:, PK["lng"]:PK["lng"] + 16] = inp["ln_g"][l].reshape(16, 128).T
    pp[:, PK["lnb"]:PK["lnb"] + 16] = inp["ln_b"][l].reshape(16, 128).T
    lre = inp["s5_lambda_re"][l].reshape(16, 2, 64).reshape(16, 128).T
    lim = inp["s5_lambda_im"][l].reshape(16, 2, 64).reshape(16, 128).T
    lst = np.repeat(inp["s5_log_step"][l].reshape(16, 2, 1), 64, axis=2).reshape(16, 128).T
    pp[:, PK["lre"]:PK["lre"] + 16] = lre
    pp[:, PK["lim"]:PK["lim"] + 16] = lim
    pp[:, PK["lst"]:PK["lst"] + 16] = lst
    pp[:, PK["s5d"]:PK["s5d"] + 4] = inp["s5_d"][l].reshape(4, 128).T
    pp[:, PK["bglu"]:PK["bglu"] + 4] = inp["s5_b_glu"][l].reshape(4, 128).T
    mcw = inp["ml_conv_w"][l]; mcb = inp["ml_conv_b"][l]
    scw = inp["ssd_conv_w"][l]; scb = inp["ssd_conv_b"][l]
    tiles = []
    for h in range(4):
        tiles.append((mcw[:, h * 96:(h + 1) * 96], mcb[h * 96:(h + 1) * 96]))
    for h in range(4):
        tiles.append((mcw[:, 384 + h * 96:384 + (h + 1) * 96], mcb[384 + h * 96:384 + (h + 1) * 96]))
    for g in range(4):
        for r in range(3):
            o = (3 * g + r) * 64
            tiles.append((scw[:, o:o + 64], scb[o:o + 64]))
        o = 768 + g * 128
        tiles.append((scw[:, o:o + 128], scb[o:o + 128]))
        o = 1280 + g * 128
        tiles.append((scw[:, o:o + 128], scb[o:o + 128]))
    for t, (w, b) in enumerate(tiles):
        m = w.shape[1]
        pp[0:m, PK["cw"] + 4 * t:PK["cw"] + 4 * t + 4] = w.T
        pp[0:m, PK["cb"] + t] = b
    pp[0:96, PK["mlg"]:PK["mlg"] + 8] = inp["ml_norm_g"][l].reshape(8, 96).T
    pp[0:64, PK["sd"]:PK["sd"] + 12] = np.repeat(inp["ssd_d"][l][None, :], 64, axis=0)
    pp[0:64, PK["sg"]:PK["sg"] + 12] = inp["ssd_norm_g"][l].reshape(12, 64).T
    gb = np.concatenate([inp["ml_i_bias"][l], inp["ml_f_bias"][l], inp["ssd_dt_bias"][l]])
    pp[:, PK["gb"]:PK["gb"] + 20] = np.repeat(gb[None, :], P, axis=0)
    pp[:, PK["al"]:PK["al"] + 12] = np.repeat(inp["ssd_a_log"][l][None, :], P, axis=0)
    bt = np.zeros((P, 2, 16, 128), f)
    ct = np.zeros((P, 2, 16, 128), f)
    for ri, (bsrc, csrc) in enumerate([(inp["s5_b_re"][l], inp["s5_c_re"][l]), (inp["s5_b_im"][l], inp["s5_c_im"][l])]):
        for s in range(16):
            for gg in range(2):
                g = 2 * s + gg
                r0 = (g % 8) * 16
                bt[r0:r0 + 16, ri, s, gg * 64:(gg + 1) * 64] = bsrc[g].T
                ct[gg * 64:(gg + 1) * 64, ri, s, r0:r0 + 16] = csrc[g].T
    return pp, bt.reshape(P, -1), ct


def _prep(inputs):
    inp = {k: np.asarray(v) for k, v in inputs.items()}
    L = inp["w_in"].shape[0]
    win = np.ascontiguousarray(inp["w_in"][:, :, PERM])
    pps, bts, cts = [], [], []
    for l in range(L):
        a, b, c_ = _pack_layer_params(inp, l)
        pps.append(a); bts.append(b); cts.append(c_)
    common = {
        "win": win,
        "wout": np.ascontiguousarray(inp["w_out"]),
        "pp": np.stack(pps), "bt": np.stack(bts), "ct": np.stack(cts),
        "wglu": np.ascontiguousarray(inp["s5_w_glu"]),
        "jrow": np.repeat(np.arange(1, 129, dtype=np.float32)[None, :], P, axis=0),
    }
    return inp, common


_CACHE = {}


def _get_prog(L, T):
    key = (L, T)
    if key not in _CACHE:
        nc = bass.Bass("TRN2", target_bir_lowering=False)
        es = ExitStack()
        build(nc, es, L, T)
        _CACHE[key] = (nc, es)
    return _CACHE[key][0]


FUSED = True


def kernel(**inputs):
    inp, common = _prep(inputs)
    x = inp["x"]
    B, T, _ = x.shape
    L = inp["w_in"].shape[0]
    n_cores = 8
    xT = [np.ascontiguousarray(x[b].T) for b in range(B)]
    if FUSED:
        nc = _get_prog(L, T)
        in_maps = [dict(common, xT=xT[c % B]) for c in range(n_cores)]
        res = run_bass_kernel_spmd(nc, in_maps, core_ids=list(range(n_cores)))
        outs = [res.results[b]["yT"] for b in range(B)]
    else:
        nc = _get_prog(1, T)
        cur = xT
        for l in range(L):
            cl = {k: (v[l:l + 1] if k in ("win", "wout", "pp", "bt", "ct", "wglu") else v) for k, v in common.items()}
            in_maps = [dict(cl, xT=cur[c % B]) for c in range(n_cores)]
            res = run_bass_kernel_spmd(nc, in_maps, core_ids=list(range(n_cores)))
            cur = [res.results[b]["yT"] for b in range(B)]
        outs = cur
    return np.stack([o.T for o in outs]).astype(np.float32)
```

```python
import math
from contextlib import ExitStack

import numpy as np
import concourse.bass as bass
import concourse.mybir as mybir
from concourse.bass_utils import run_bass_kernel_spmd

F32 = mybir.dt.float32
BF16 = mybir.dt.bfloat16
ALU = mybir.AluOpType
AF = mybir.ActivationFunctionType

ENGS = ("pe", "act", "dve", "pool", "sp")

P = 128
D = 2048
KT = 16
TT = 256
CH = 128
NCH = TT // CH
DEPTH = 4
N_IN = 6676
ALPHA = (2.0 * DEPTH) ** 0.25
EPS = 1e-5
DQK = 96
DV = 192
TWO_PI = 2.0 * math.pi
CW1 = 6.28125
CW2 = TWO_PI - 6.28125
MAGIC = 12582912.0

O_S5U, O_S5Z, O_MLQ, O_MLK, O_MLV = 0, 512, 1024, 1408, 1792
O_MLI, O_MLF, O_MLO, O_MLZ = 2560, 2564, 2568, 3336
O_SX, O_SB, O_SC, O_SDT, O_SZ = 4104, 4872, 5384, 5896, 5908


def _chunks():
    ch = []
    ar = np.arange
    ch.append(("gates", np.concatenate([ar(O_MLI, O_MLI + 4), ar(O_MLF, O_MLF + 4), ar(O_SDT, O_SDT + 12)])))
    for kt in range(4):
        ch.append((f"s5_{kt}", np.concatenate([ar(O_S5U + kt * 128, O_S5U + kt * 128 + 128), ar(O_S5Z + kt * 128, O_S5Z + kt * 128 + 128)])))
    for h in range(4):
        ch.append((f"mlA_{h}", np.concatenate([ar(O_MLQ + h * 96, O_MLQ + h * 96 + 96), ar(O_MLK + h * 96, O_MLK + h * 96 + 96)])))
        ch.append((f"mlB_{h}", ar(O_MLO + h * 192, O_MLO + h * 192 + 192)))
        ch.append((f"mlC_{h}", ar(O_MLZ + h * 192, O_MLZ + h * 192 + 192)))
        ch.append((f"mlV_{h}", ar(O_MLV + h * 192, O_MLV + h * 192 + 192)))
    for g in range(4):
        ch.append((f"sX_{g}", ar(O_SX + g * 192, O_SX + g * 192 + 192)))
        ch.append((f"sBC_{g}", np.concatenate([ar(O_SB + g * 128, O_SB + g * 128 + 128), ar(O_SC + g * 128, O_SC + g * 128 + 128)])))
        ch.append((f"sZ_{g}", ar(O_SZ + g * 192, O_SZ + g * 192 + 192)))
    return ch


CHUNKS = _chunks()
CH_OFF = {}
_o = 0
for _n, _c in CHUNKS:
    CH_OFF[_n] = (_o, len(_c))
    _o += len(_c)
assert _o == N_IN
PERM = np.concatenate([c for _, c in CHUNKS])

PK = {}
_o = 0
for _n, _w in [("lng", 16), ("lnb", 16), ("lre", 16), ("lim", 16), ("lst", 16), ("s5d", 4), ("bglu", 4),
               ("cw", 28 * 4), ("cb", 28), ("mlg", 8), ("sd", 12), ("sg", 12), ("gb", 20), ("al", 12)]:
    PK[_n] = _o
    _o += _w
NPK = _o


class Sched:
    SEM_EPOCH = 30000

    def __init__(self, nc, es, same_engine_sync=True):
        self.nc = nc
        self.es = es
        self.ops = {e: [] for e in ENGS}
        self.sems = {}
        self.esem = {}
        self.eseq = {}
        self.nsem = 0
        for e in ENGS:
            self._new_esem(e)
        self.dsem = {}
        self.last_write = {}
        self.readers = {}
        self.waited = {e: {} for e in ENGS}
        self.same_engine_sync = same_engine_sync
        self.n_ins = 0

    def _mksem(self, name):
        s = self.es.enter_context(self.nc.semaphore(name))
        self.nsem += 1
        self.sems[id(s)] = s
        return s

    def _new_esem(self, e):
        self.esem[e] = self._mksem(f"s_{e}_{self.nsem}")
        self.eseq[e] = 0

    ALIAS = {}

    def _deps(self, eng, reads, writes):
        need = {}
        reads = [self.ALIAS.get(k, k) for k in reads]
        writes = [self.ALIAS.get(k, k) for k in writes] + [k for k in reads if k.startswith("B_")]

        def add(tok):
            if tok is None:
                return
            sid, val, src = tok
            if src == eng and (eng == "pe" or not self.same_engine_sync):
                return
            if need.get(sid, 0) < val:
                need[sid] = val

        for k in reads:
            add(self.last_write.get(k))
        for k in writes:
            add(self.last_write.get(k))
            for t in self.readers.get(k, ()):
                add(t)
        w = self.waited[eng]
        for sid, val in need.items():
            if w.get(sid, 0) >= val:
                continue
            w[sid] = val
            self.ops[eng].append(("wait", self.sems[sid], val))

    def _commit(self, tok, reads, writes):
        reads = [self.ALIAS.get(k, k) for k in reads]
        writes = [self.ALIAS.get(k, k) for k in writes] + [k for k in reads if k.startswith("B_")]
        for k in writes:
            self.last_write[k] = tok
            self.readers[k] = []
        for k in reads:
            lst = self.readers.setdefault(k, [])
            lst.append(tok)
            if len(lst) > 8:
                best = {}
                for t in lst:
                    if best.get(t[0], (0, 0, 0))[1] < t[1]:
                        best[t[0]] = t
                self.readers[k] = list(best.values())

    def op(self, eng, fn, reads=(), writes=()):
        self._deps(eng, reads, writes)
        if self.eseq[eng] >= self.SEM_EPOCH:
            self._new_esem(eng)
        self.eseq[eng] += 1
        sem = self.esem[eng]
        tok = (id(sem), self.eseq[eng], eng)
        self.ops[eng].append(("ins", fn, sem, 1))
        self._commit(tok, reads, writes)
        self.n_ins += 1
        return tok

    def dma(self, q, fn, key, reads=(), writes=()):
        self._deps(q, reads, writes)
        if key not in self.dsem:
            self.dsem[key] = [self._mksem(f"d_{self.nsem}"), 0]
        ent = self.dsem[key]
        ent[1] += 16
        tok = (id(ent[0]), ent[1], "dma")
        self.ops[q].append(("ins", fn, ent[0], 16))
        self._commit(tok, reads, writes)
        self.n_ins += 1
        return tok

    def wait_tok(self, eng, tok):
        sid, val, _ = tok
        if self.waited[eng].get(sid, 0) >= val:
            return
        self.waited[eng][sid] = val
        self.ops[eng].append(("wait", self.sems[sid], val))

    def replay(self):
        nc = self.nc
        ops = self.ops

        def run(engobj, lst):
            for it in lst:
                if it[0] == "wait":
                    engobj.wait_ge(it[1], it[2])
                else:
                    it[1](engobj).then_inc(it[2], it[3])

        with nc.Block() as block:

            @block.tensor
            def _(e):
                run(e, ops["pe"])

            @block.scalar
            def _(e):
                run(e, ops["act"])

            @block.vector
            def _(e):
                run(e, ops["dve"])

            @block.gpsimd
            def _(e):
                run(e, ops["pool"])

            @block.sync
            def _(e):
                run(e, ops["sp"])


class _Stop(Exception):
    pass


def build(nc, es, L, T, layer0=0, debug=False, phase=99):
    NT = T // TT
    import os as _os
    S = Sched(nc, es, same_engine_sync=(_os.environ.get('SES', '1') == '1'))

    def dram(name, shape, kind, dt=F32):
        return nc.dram_tensor(name, shape, dt, kind=kind).ap()

    xT_d = dram("xT", [D, T], "ExternalInput")
    win_d = dram("win", [L, P, 16 * N_IN], "ExternalInput")
    wout_d = dram("wout", [L, P, 16 * 3072], "ExternalInput")
    pp_d = dram("pp", [L, P, NPK], "ExternalInput")
    bt_d = dram("bt", [L, P, 2 * 16 * 128], "ExternalInput")
    ct_d = dram("ct", [L, P, 2, 16, 128], "ExternalInput")
    wglu_d = dram("wglu", [L, 512, 512], "ExternalInput")
    jrow_d = dram("jrow", [P, 128], "ExternalInput")
    yT_d = dram("yT", [D, T], "ExternalOutput")
    xs_d = [dram(f"xs{i}", [D, T], "Internal") for i in range(2)] if L > 1 else []
    dbg_d = dram("dbg", [24, P, T], "ExternalOutput") if debug else None
    winb_d = [dram(f"winb{l}", [P, 16 * N_IN], "Internal", BF16) for l in range(L)]
    woutb_d = [dram(f"woutb{l}", [P, 16 * 3072], "Internal", BF16) for l in range(L)]

    def sb(name, shape, dt=F32):
        return es.enter_context(nc.sbuf_tensor(name, shape, dt))

    def psum(name, shape, dt=F32):
        return es.enter_context(nc.psum_tensor(name, shape, dt))

    ident_f = sb("ident_f", [P, 128]); ident_b = sb("ident_b", [P, 128], BF16)
    tri_f = sb("tri_f", [P, 128]); ones_f = sb("ones_f", [P, 128]); ones_b = sb("ones_b", [P, 128], BF16)
    jrow = sb("jrow_s", [P, 128])
    wfl = [sb(f"wfl{i}", [P, 16 * 256], BF16) for i in range(2)]
    xb = sb("xb", [P, KT, TT], BF16)
    xf = sb("xf", [P, KT, TT])
    mx_s5 = sb("mx_s5", [P, 4, TT], BF16); mx_ml = sb("mx_ml", [P, 8, TT], BF16); mx_ssd = sb("mx_ssd", [P, 12, TT], BF16)
    pp = sb("pp_s", [P, NPK])
    btb = sb("btb", [P, 2, 16, 128], BF16); ckb = sb("ckb", [P, 2, 16, 128], BF16)
    ctst = [sb(f"ctst{i}", [P, 2, 128]) for i in range(2)]
    wglub = sb("wglub", [P, 4, 512], BF16)
    E_re = sb("E_re", [P, 16, 128]); E_im = sb("E_im", [P, 16, 128])
    s5c = sb("s5c", [P, 24, 16])
    hs = sb("hs", [P, 2, 16])
    u_f = sb("u_f", [P, TT]); u_b = sb("u_b", [P, TT], BF16); zs5 = sb("zs5", [P, 4, TT], BF16)
    bp = sb("bp", [P, 2, TT]); kk = sb("kk", [P, 2, TT]); tq = [sb(f"tq{i}", [P, TT]) for i in range(3)]
    hb = [sb(f"hb{i}", [P, 2, TT], BF16) for i in range(2)]
    g_f = sb("g_f", [P, 4, TT]); g_b = sb("g_b", [P, 4, TT], BF16)
    tiny = sb("tiny", [P, 8])
    tp = [sb(f"tp{i}", [P, TT]) for i in range(2)]
    cbuf = [sb(f"cbuf{i}", [P, 3 + TT], BF16) for i in range(2)]
    halo = sb("halo", [P, 28, 3], BF16)
    dg = [sb(f"dg{i}", [P, 4, 128], BF16) for i in range(2)]
    q_b = sb("q_b", [P, 4, TT], BF16); k_b = sb("k_b", [P, 4, TT], BF16)
    oz = sb("oz", [P, 8, TT], BF16); osig = sb("osig", [P, TT], BF16)
    v_b = sb("v_b", [P, NCH, 4, 193], BF16)
    cst_f = sb("cst_f", [P, 4, 193]); cst_b = sb("cst_b", [P, 4, 192], BF16); nbc_b = sb("nbc_b", [P, 4, 96], BF16)
    x_f = sb("x_f", [P, 12, TT]); B_b = sb("B_b", [P, 4, TT], BF16); C_b = sb("C_b", [P, 4, TT], BF16)
    zssd = sb("zssd", [P, 12, TT], BF16)
    sst_f = sb("sst_f", [P, 12, 64]); sst_b = sb("sst_b", [P, 12, 64], BF16)
    yz = sb("yz", [P, 12, CH]); sqb = sb("sqb", [P, 12, CH], BF16)
    btok = sb("btok", [P, 128], BF16); cbm = sb("cbm", [P, 128])
    _xff = xf[:].rearrange("p k t -> p (k t)")
    _xsf = x_f[:].rearrange("p k t -> p (k t)")
    etmp = [_xff[:, 0:2048].rearrange("p (s j) -> p s j", s=16), _xff[:, 2048:4096].rearrange("p (s j) -> p s j", s=16),
            _xsf[:, 0:2048].rearrange("p (s j) -> p s j", s=16)]
    gsb = sb("gsb", [P, NCH, 20]); tmpg = sb("tmpg", [P, NCH, 16]); vals = sb("vals", [P, NCH, 16])
    dtv = sb("dtv", [P, NCH, 12]); cum = sb("cum", [P, NCH, 16]); bml = sb("bml", [P, NCH, 4]); ncum = sb("ncum", [P, NCH, 16])
    arow = sb("arow", [P, 12])
    lbh = [sb(f"lbh{i}", [P, 128], BF16) for i in range(2)]
    lbl = [sb(f"lbl{i}", [P, 128], BF16) for i in range(2)]
    vh = sb("vh", [P, NCH, 16], BF16); vl = sb("vl", [P, NCH, 16], BF16); vtmp = sb("vtmp", [P, 16])
    tri_b = sb("tri_b", [P, 128], BF16)
    xh = [sb(f"xh{i}", [P, CH], BF16) for i in range(2)]; xl = [sb(f"xl{i}", [P, CH], BF16) for i in range(2)]
    zb = [sb(f"zb{i}", [P, TT], BF16) for i in range(2)]
    ebc = [sb(f"ebc{i}", [P, 128]) for i in range(2)]
    tf = [sb(f"tf{i}", [P, 128]) for i in range(2)]
    dT = [sb(f"dT{i}", [P, 128]) for i in range(2)]
    A_b = [sb(f"A_b{i}", [P, 128], BF16) for i in range(2)]
    qs_b = [sb(f"qs_b{i}", [P, 128], BF16) for i in range(2)]
    Cs_b = [sb(f"Cs_b{i}", [P, 128], BF16) for i in range(2)]
    dtx_b = [sb(f"dtx_b{i}", [P, 64], BF16) for i in range(2)]
    dtxw_b = [sb(f"dtxw_b{i}", [P, 64], BF16) for i in range(2)]
    kw_b = [sb(f"kw_b{i}", [P, 96], BF16) for i in range(2)]
    wcol = [sb(f"wcol{i}", [P, 2]) for i in range(2)]
    hn = sb("hn", [P, 2, CH]); sqm = sb("sqm", [P, 2, CH], BF16); rd = sb("rd", [P, CH]); rs = sb("rs", [P, CH])
    yv = [sb(f"yv{i}", [P, CH]) for i in range(2)]
    mean = sb("mean", [P, TT]); m2 = sb("m2", [P, TT]); rstd = sb("rstd", [P, TT]); sqz = [None, None]; sqzb = [sb(f"sqzb{i}", [P, TT], BF16) for i in range(2)]
    ybuf = [sb(f"ybuf{i}", [P, TT]) for i in range(4)]
    lt = [sb(f"lt{i}", [P, TT]) for i in range(2)]

    pa = psum("pa", [P, 4, 256])
    ps_b = psum("ps_b", [P, 2, 256])
    ps_y = psum("ps_y", [P, 2, 256])
    ps_bc = psum("ps_bc", [P, 4, 128])
    ps_s = psum("ps_s", [P, 4, 128])
    ps_o = psum("ps_o", [P, 4, 128])
    ps_st = psum("ps_st", [P, 512])

    btok1 = sb("btok1", [P, 128], BF16); cbm1 = sb("cbm1", [P, 128])
    bp1 = sb("bp1", [P, 2, TT]); kk1 = sb("kk1", [P, 2, TT]); tq1x = [sb(f"tq1x{i}", [P, TT]) for i in range(2)]; tp1x = [sb(f"tp1x{i}", [P, TT]) for i in range(2)]
    S5L = [{"psb": ps_b, "k": "ps_b", "bp": bp, "kk": kk, "tq": (tq[0], tq[1]), "tp": (tp[0], tp[1])},
           {"psb": ps_st[:].rearrange("p (a j) -> p a j", a=2), "k": "S5b", "bp": bp1, "kk": kk1, "tq": (tq1x[0], tq1x[1]), "tp": (tp1x[0], tp1x[1])}]
    yvm = [sb(f"yvm{i}", [P, CH]) for i in range(2)]
    LN0 = {"par": 0, "k": "ps_", "bc": ps_bc[:, 0, :], "s": ps_s, "o": ps_o, "st": ps_st, "btok": btok, "cbm": cbm}
    LN1 = {"par": 1, "k": "L1_", "bc": pa[:, 0, 0:128], "s": pa[:, 2:4, :].rearrange("p a (b j) -> p (a b) j", j=128),
           "o": ps_b[:].rearrange("p a (b j) -> p (a b) j", j=128), "st": ps_y[:].rearrange("p a j -> p (a j)"), "btok": btok1, "cbm": cbm1}
    state = {"pa": 0, "wb": 0, "cb": 0, "yb": 0, "par": 0}
    al = {"pa0": "B_pa0", "pa1": "B_pa0", "pa2": "B_pa1", "pa3": "B_pa1", "ps_b0": "B_b", "ps_b1": "B_b",
          "ps_y0": "B_y", "ps_y1": "B_y", "ps_bc0": "B_bc", "ps_bc1": "B_bc",
          "ps_s0": "B_s", "ps_s1": "B_s", "ps_s1b": "B_s", "ps_s1c": "B_s", "ps_s2": "B_s", "ps_s3": "B_s",
          "ps_o0": "B_o", "ps_o1": "B_o", "ps_o2": "B_o", "ps_st": "B_st", "ps_stb0": "B_st", "ps_stb1": "B_st", "ps_stb2": "B_st",
          "ps_bc": "B_bc", "S5b0": "B_st", "S5b1": "B_st", "L1_bc": "B_pa0", "L1_s0": "B_pa1", "L1_s1c": "B_pa1", "L1_s2": "B_pa1", "L1_s3": "B_pa1",
          "L1_o0": "B_b", "L1_o1": "B_b", "L1_o2": "B_b", "L1_st": "B_y", "L1_stb0": "B_y", "L1_stb1": "B_y", "L1_stb2": "B_y"}
    S.ALIAS = al

    def next_pa():
        i = state["pa"]; state["pa"] = (i + 1) % 4
        return i

    def DVE(fn, r, w): return S.op("dve", fn, r, w)
    def ACT(fn, r, w): return S.op("act", fn, r, w)
    def PE(fn, r, w): return S.op("pe", fn, r, w)

    def mm(out, lhsT, rhs, start, stop, r, w):
        return PE(lambda e: e.matmul(out, lhsT=lhsT, rhs=rhs, start=start, stop=stop), r, w)

    def tt(out, a, b, op, r, w): return DVE(lambda e: e.tensor_tensor(out=out, in0=a, in1=b, op=op), r, w)
    def ts(out, a, s1, s2, op0, op1, r, w): return DVE(lambda e: e.tensor_scalar(out=out, in0=a, scalar1=s1, scalar2=s2, op0=op0, op1=op1), r, w)
    def ts1(out, a, s1, op0, r, w): return DVE(lambda e: e.tensor_single_scalar(out=out, in_=a, scalar=s1, op=op0), r, w)
    def stt(out, a, sc, b, op0, op1, r, w): return DVE(lambda e: e.scalar_tensor_tensor(out=out, in0=a, scalar=sc, in1=b, op0=op0, op1=op1), r, w)
    def cp(out, a, r, w): return DVE(lambda e: e.tensor_copy(out=out, in_=a), r, w)
    def act(out, a, func, r, w, bias=None, scale=None):
        kw = {}
        if bias is not None: kw["bias"] = bias
        if scale is not None: kw["scale"] = scale
        return ACT(lambda e: e.activation(out=out, in_=a, func=func, **kw), r, w)

    S.op("pool", lambda e: e.memset(ident_f[:], 1.0), [], ["ident_f"])
    S.op("pool", lambda e: e.affine_select(out=ident_f[:], in_=ident_f[:], pattern=[[-1, 128]], compare_op=ALU.is_equal, fill=0.0, base=0, channel_multiplier=1), ["ident_f"], ["ident_f"])
    S.op("pool", lambda e: e.memset(tri_f[:], 1.0), [], ["tri_f"])
    S.op("pool", lambda e: e.affine_select(out=tri_f[:], in_=tri_f[:], pattern=[[1, 128]], compare_op=ALU.is_ge, fill=0.0, base=0, channel_multiplier=-1), ["tri_f"], ["tri_f"])
    S.op("pool", lambda e: e.memset(ones_f[:], 1.0), [], ["ones_f"])
    S.op("pool", lambda e: e.memset(ones_b[:], 1.0), [], ["ones_b"])
    S.op("pool", lambda e: e.memset(v_b[:], 1.0), [], ["v_b"])
    cp(ident_b[:], ident_f[:], ["ident_f"], ["ident_b"])
    cp(tri_b[:], tri_f[:], ["tri_f"], ["tri_b"])
    S.dma("sp", lambda e: e.dma_start(out=jrow[:], in_=jrow_d), "jrow", [], ["jrow"])

    WKEYS = {}
    for l in range(L):
        keys = []
        WI = 16 * N_IN // 8
        for part in range(8):
            cs_ = slice(part * WI, (part + 1) * WI)
            k_ = f"winb{l}_{part}"
            S.dma("pool", lambda e, l=l, cs_=cs_: e.dma_start(out=winb_d[l][:, cs_], in_=win_d[l][:, cs_]), f"wcast{l}", [], [k_])
            keys.append(k_)
        WO = 16 * 3072 // 4
        for part in range(4):
            cs_ = slice(part * WO, (part + 1) * WO)
            k_ = f"woutb{l}_{part}"
            S.dma("pool", lambda e, l=l, cs_=cs_: e.dma_start(out=woutb_d[l][:, cs_], in_=wout_d[l][:, cs_]), f"wcast{l}", [], [k_])
            keys.append(k_)
        WKEYS[l] = keys

    def range_reduce_sin(out, ang, quarter, neg, keys_r, key_w, t0, t1, k0, k1):
        ts(t0, ang, 1.0 / TWO_PI, 0.25 * quarter, ALU.mult, ALU.add, keys_r, [k0])
        ts1(t0, t0, MAGIC, ALU.add, [k0], [k0])
        ts1(t0, t0, -MAGIC, ALU.add, [k0], [k0])
        stt(t1, t0, -CW1, ang, ALU.mult, ALU.add, keys_r + [k0], [k1])
        stt(t1, t0, -CW2, t1, ALU.mult, ALU.add, [k0, k1], [k1])
        lo = -math.pi - quarter * math.pi / 2
        ts(t1, t1, lo, lo + TWO_PI, ALU.max, ALU.min, [k1], [k1])
        if neg:
            act(out, t1, AF.Sin, [k1], [key_w], bias=-quarter * math.pi / 2, scale=-1.0)
        else:
            act(out, t1, AF.Sin, [k1], [key_w], bias=quarter * math.pi / 2, scale=1.0)

    def c(i):
        return s5c[:, i, :]

    def _phase(n):
        if phase <= n:
            raise _Stop()

    try:
        for li in range(L):
            xsrc = xT_d if li == 0 else xs_d[(li - 1) % 2]
            xdst = yT_d if li == L - 1 else xs_d[li % 2]
            xsrc_key = "xT" if li == 0 else f"xs{(li - 1) % 2}"
            xdst_key = "yT" if li == L - 1 else f"xs{li % 2}"

            if li > 0:
                for yb_ in range(4):
                    ent_ = S.dsem.get(f"out{yb_}")
                    if ent_:
                        S.wait_tok("pool", (id(ent_[0]), ent_[1], "dma"))
                        S.wait_tok("sp", (id(ent_[0]), ent_[1], "dma"))
            S.dma("sp", lambda e, li=li: e.dma_start(out=pp[:], in_=pp_d[li]), "pp", [], ["pp"])
            S.dma("pool", lambda e, li=li: e.dma_start(out=btb[:].rearrange("p a s c -> p (a s c)"), in_=bt_d[li]), "btb", [], ["btb"])
            S.dma("pool", lambda e, li=li: e.dma_start(out=wglub[:], in_=wglu_d[li].rearrange("(k p) c -> p k c", p=P)), "wglub", [], ["wglub"])

            def pk(name, i=0, n=1, rows=P):
                o = PK[name] + i
                return pp[0:rows, o:o + n]

            lre = pp[:, PK["lre"]:PK["lre"] + 16]; lim = pp[:, PK["lim"]:PK["lim"] + 16]; lst = pp[:, PK["lst"]:PK["lst"] + 16]
            K = "s5c"
            act(c(0), lst, AF.Exp, ["pp"], [K])
            tt(c(1), lre, c(0), ALU.mult, ["pp", K], [K])
            tt(c(2), lim, c(0), ALU.mult, ["pp", K], [K])
            act(c(3), c(1), AF.Exp, [K], [K])
            range_reduce_sin(c(4), c(2), 0, False, [K], K, c(20), c(21), K, K)
            range_reduce_sin(c(5), c(2), 1, False, [K], K, c(20), c(21), K, K)
            tt(c(6), c(3), c(5), ALU.mult, [K], [K])
            tt(c(7), c(3), c(4), ALU.mult, [K], [K])
            ts1(c(8), c(6), -1.0, ALU.add, [K], [K])
            tt(c(9), lre, lre, ALU.mult, ["pp"], [K])
            tt(c(10), lim, lim, ALU.mult, ["pp"], [K])
            tt(c(9), c(9), c(10), ALU.add, [K], [K])
            DVE(lambda e: e.reciprocal(out=c(9), in_=c(9)), [K], [K])
            tt(c(10), c(8), lre, ALU.mult, [K, "pp"], [K])
            tt(c(11), c(7), lim, ALU.mult, [K, "pp"], [K])
            tt(c(10), c(10), c(11), ALU.add, [K], [K])
            tt(c(12), c(10), c(9), ALU.mult, [K], [K])
            tt(c(10), c(7), lre, ALU.mult, [K, "pp"], [K])
            tt(c(11), c(8), lim, ALU.mult, [K, "pp"], [K])
            tt(c(10), c(10), c(11), ALU.subtract, [K], [K])
            tt(c(13), c(10), c(9), ALU.mult, [K], [K])
            ts1(c(14), c(12), -1.0, ALU.mult, [K], [K])
            ts1(c(15), c(13), -1.0, ALU.mult, [K], [K])
            for s in range(16):
                ts1(etmp[0][:, s, :], jrow[:], c(2)[:, s:s + 1], ALU.mult, ["jrow", K], ["xf"])
            E_re_fl = E_re[:].rearrange("p s j -> p (s j)"); E_im_fl = E_im[:].rearrange("p s j -> p (s j)")
            range_reduce_sin(E_re_fl, _xff[:, 0:2048], 1, False, ["xf"], "E_re", _xff[:, 2048:4096], _xsf[:, 0:2048], "xf", "x_f")
            range_reduce_sin(E_im_fl, _xff[:, 0:2048], 0, True, ["xf"], "E_im", _xff[:, 2048:4096], _xsf[:, 0:2048], "xf", "x_f")
            for s in range(16):
                b = s % 2
                S.dma("sp", lambda e, li=li, s=s, b=b: e.dma_start(out=ctst[b][:], in_=ct_d[li][:, :, s, :]), f"ctst{b}", [], [f"ctst{b}"])
                ts1(tq[0][:, 0:128], ctst[b][:, 0, :], c(12)[:, s:s + 1], ALU.mult, [f"ctst{b}", K], ["tq0"])
                stt(ckb[:, 0, s, :], ctst[b][:, 1, :], c(15)[:, s:s + 1], tq[0][:, 0:128], ALU.mult, ALU.add, [f"ctst{b}", K, "tq0"], ["ckb"])
                ts1(tq[1][:, 0:128], ctst[b][:, 0, :], c(15)[:, s:s + 1], ALU.mult, [f"ctst{b}", K], ["tq1"])
                stt(ckb[:, 1, s, :], ctst[b][:, 1, :], c(14)[:, s:s + 1], tq[1][:, 0:128], ALU.mult, ALU.add, [f"ctst{b}", K, "tq1"], ["ckb"])
            act(arow[:], pp[:, PK["al"]:PK["al"] + 12], AF.Exp, ["pp"], ["arow"])
            ts1(arow[:], arow[:], -1.0, ALU.mult, ["arow"], ["arow"])
            S.op("pool", lambda e: e.memset(hs[:], 0.0), [], [f"hs{i_}" for i_ in range(16)])
            S.op("pool", lambda e: e.memset(halo[:], 0.0), [], ["halo"])
            S.op("pool", lambda e: e.memset(cst_f[:], 0.0), [], ["cst_f"])
            S.op("pool", lambda e: e.memset(cst_b[:], 0.0), [], ["cst_b"])
            S.op("pool", lambda e: e.memset(nbc_b[:], 0.0), [], ["nbc_b"])
            S.op("pool", lambda e: e.memset(sst_f[:], 0.0), [], [f"sst_f{i_}" for i_ in range(12)])
            S.op("pool", lambda e: e.memset(sst_b[:], 0.0), [], [f"sst_b{i_}" for i_ in range(12)])

            _phase(0)
            win_v = winb_d[li]

            def load_chunk(name):
                c0, w = CH_OFF[name]
                b = state["wb"]; state["wb"] = 1 - b
                view = wfl[b][:, 0:16 * w].rearrange("p (k c) -> p k c", k=16)
                src = win_v[:, 16 * c0:16 * c0 + 16 * w]
                dst = wfl[b][:, 0:16 * w]
                S.dma("sp", lambda e, dst=dst, src=src: e.dma_start(out=dst, in_=src), f"wfl{b}", WKEYS[li], [f"wfl{b}"])
                return view, f"wfl{b}"

            pend = []

            def flush():
                for f_ in list(pend):
                    f_()
                pend.clear()

            def proj_fm(wv, wkey, c0, M):
                i = next_pa()
                for k in range(KT):
                    mm(pa[0:M, i, :], wv[:, k, c0:c0 + M], xb[:, k, :], k == 0, k == KT - 1, [wkey, "xb"], [f"pa{i}"])
                flush()
                return i

            def conv_tile(i, M, tile, out_ap, out_key):
                b = state["cb"]; state["cb"] = 1 - b
                ck = f"cbuf{b}"
                act(cbuf[b][0:M, 3:3 + TT], pa[0:M, i, :], AF.Copy, [f"pa{i}"], [ck])
                cp(cbuf[b][0:M, 0:3], halo[0:M, tile, :], ["halo"], [ck])
                cp(halo[0:M, tile, :], cbuf[b][0:M, TT:TT + 3], [ck], ["halo"])
                for k in range(4):
                    act(dg[b][0:M, k, 0:M], ident_f[0:M, 0:M], AF.Copy, ["ident_f", "pp"], [f"dg{b}"],
                        scale=pp[0:M, PK["cw"] + tile * 4 + k:PK["cw"] + tile * 4 + k + 1])
                def fin():
                    j = next_pa()
                    for k in range(4):
                        mm(pa[0:M, j, :], dg[b][0:M, k, 0:M], cbuf[b][0:M, k:k + TT], k == 0, k == 3, [f"dg{b}", ck], [f"pa{j}"])
                    act(out_ap, pa[0:M, j, :], AF.Silu, [f"pa{j}", "pp"], [out_key], bias=pp[0:M, PK["cb"] + tile:PK["cb"] + tile + 1])
                pend.append(fin)

            for ti in range(NT):
                t0 = ti * TT
                S.dma("pool", lambda e, t0=t0, xsrc=xsrc: e.dma_start(out=xb[:], in_=xsrc.rearrange("(k p) t -> p k t", p=P)[:, :, t0:t0 + TT]), "xb", [xsrc_key], ["xb"])
                S.dma("sp", lambda e, t0=t0, xsrc=xsrc: e.dma_start(out=xf[:], in_=xsrc.rearrange("(k p) t -> p k t", p=P)[:, :, t0:t0 + TT]), "xf", [xsrc_key], ["xf"])

                _phase(0.3)
                wv, wkey = load_chunk("gates")
                _phase(0.4)
                for cc in range(NCH):
                    for k in range(KT):
                        mm(ps_st[:, 0:20], xb[:, k, cc * CH:(cc + 1) * CH], wv[:, k, 0:20], k == 0, k == KT - 1, ["xb", wkey], ["ps_st"])
                    _phase(0.5 if cc == 0 else 0.96 + (0.5 - 0.5) * 0.06)
                    tt(gsb[:, cc, :], ps_st[:, 0:20], pp[:, PK["gb"]:PK["gb"] + 20], ALU.add, ["ps_st", "pp"], ["gsb"])
                    _phase(0.6 if cc == 0 else 0.96 + (0.6 - 0.5) * 0.06)
                    act(tmpg[:, cc, 0:4], gsb[:, cc, 4:8], AF.Exp, ["gsb"], ["tmpg"], scale=-1.0)
                    act(tmpg[:, cc, 4:16], gsb[:, cc, 8:20], AF.Exp, ["gsb"], ["tmpg"])
                    act(vals[:, cc, 0:4], tmpg[:, cc, 0:4], AF.Ln, ["tmpg"], ["vals"], bias=1.0)
                    act(dtv[:, cc, :], tmpg[:, cc, 4:16], AF.Ln, ["tmpg"], ["dtv"], bias=1.0)
                    ts1(vals[:, cc, 0:4], vals[:, cc, 0:4], -1.0, ALU.mult, ["vals"], ["vals"])
                    tt(vals[:, cc, 4:16], dtv[:, cc, :], arow[:], ALU.mult, ["dtv", "arow"], ["vals"])
                    _phase(0.7 if cc == 0 else 0.96 + (0.7 - 0.5) * 0.06)
                    cp(vh[:, cc, :], vals[:, cc, :], ["vals"], ["vh"])
                    tt(vtmp[:], vals[:, cc, :], vh[:, cc, :], ALU.subtract, ["vals", "vh"], ["vtmp"])
                    cp(vl[:, cc, :], vtmp[:], ["vtmp"], ["vl"])
                    mm(ps_s[:, 1, 32:48], tri_b[:], vh[:, cc, :], True, False, ["tri_b", "vh"], ["ps_s1b"])
                    mm(ps_s[:, 1, 32:48], tri_b[:], vl[:, cc, :], False, True, ["tri_b", "vl"], ["ps_s1b"])
                    _phase(0.8 if cc == 0 else 0.96 + (0.8 - 0.5) * 0.06)
                    cp(cum[:, cc, :], ps_s[:, 1, 32:48], ["ps_s1b"], ["cum"])
                    _phase(0.85 if cc == 0 else 0.96 + (0.85 - 0.5) * 0.06)
                    tt(bml[:, cc, :], gsb[:, cc, 0:4], cum[:, cc, 0:4], ALU.subtract, ["gsb", "cum"], ["bml"])
                    _phase(0.9 if cc == 0 else 0.96 + (0.9 - 0.5) * 0.06)
                    ts1(ncum[:, cc, :], cum[:, cc, :], -1.0, ALU.mult, ["cum"], ["ncum"])
                    _phase(0.95 if cc == 0 else 0.96 + (0.95 - 0.5) * 0.06)

                _phase(1)
                for kt in range(4):
                    wv, wkey = load_chunk(f"s5_{kt}")
                    iu = proj_fm(wv, wkey, 0, 128)
                    iz = proj_fm(wv, wkey, 128, 128)
                    act(u_f[:], pa[:, iu, :], AF.Copy, [f"pa{iu}"], ["u_f"])
                    cp(u_b[:], pa[:, iu, :], [f"pa{iu}"], ["u_b"])
                    act(zs5[:, kt, :], pa[:, iz, :], AF.Silu, [f"pa{iz}"], ["zs5"])
                    ycnt = [0]

                    def s5_tile(j, s):
                        psb = S5L[j]["psb"]; kb = S5L[j]["k"]; bp_ = S5L[j]["bp"]; kk_ = S5L[j]["kk"]; tqa, tqb = S5L[j]["tq"]; tpa, tpb = S5L[j]["tp"]
                        hbuf = hb[j]; hk = f"hb{j}"
                        L_ = f"s5l{j}_"
                        mm(psb[:, 0, :], btb[:, 0, s, :], u_b[:], True, True, ["btb", "u_b"], [kb + "0"])
                        mm(psb[:, 1, :], btb[:, 1, s, :], u_b[:], True, True, ["btb", "u_b"], [kb + "1"])
                        yield
                        Er = E_re[:, s:s + 1, :].to_broadcast([P, 2, 128]); Ei = E_im[:, s:s + 1, :].to_broadcast([P, 2, 128])
                        v2 = lambda ap: ap.rearrange("p (a j) -> p a j", a=2)
                        tt(v2(tqa[:]), v2(psb[:, 0, :]), Er, ALU.mult, [kb + "0", "E_re"], [L_ + "tqa"])
                        tt(v2(tqb[:]), v2(psb[:, 1, :]), Ei, ALU.mult, [kb + "1", "E_im"], [L_ + "tqb"])
                        tt(bp_[:, 0, :], tqa[:], tqb[:], ALU.subtract, [L_ + "tqa", L_ + "tqb"], [L_ + "bp0"])
                        tt(v2(tqa[:]), v2(psb[:, 1, :]), Er, ALU.mult, [kb + "1", "E_re"], [L_ + "tqa"])
                        tt(v2(tqb[:]), v2(psb[:, 0, :]), Ei, ALU.mult, [kb + "0", "E_im"], [L_ + "tqb"])
                        tt(bp_[:, 1, :], tqa[:], tqb[:], ALU.add, [L_ + "tqa", L_ + "tqb"], [L_ + "bp1"])
                        yield
                        for a in range(2):
                            sl_ = slice(a * 128, (a + 1) * 128)
                            rbc = c(3)[:, s:s + 1].to_broadcast([P, 128])
                            for ri in range(2):
                                DVE(lambda e, ri=ri, sl_=sl_, rbc=rbc, s=s, kk_=kk_, bp_=bp_: e.tensor_tensor_scan(out=kk_[:, ri, sl_], data0=rbc, data1=bp_[:, ri, sl_], initial=hs[:, ri, s:s + 1], op0=ALU.mult, op1=ALU.add),
                                    [K, L_ + f"bp{ri}", f"hs{s}"], [L_ + f"kk{ri}"])
                            last = a * 128 + 127
                            erl = E_re[:, s, 127:128]; eil = E_im[:, s, 127:128]
                            t0_ = tiny[:, 2 * j:2 * j + 1]; t1_ = tiny[:, 2 * j + 1:2 * j + 2]
                            tt(t0_, kk_[:, 1, last:last + 1], eil, ALU.mult, [L_ + "kk1", "E_im"], [L_ + "tiny0"])
                            tt(t1_, kk_[:, 0, last:last + 1], eil, ALU.mult, [L_ + "kk0", "E_im"], [L_ + "tiny1"])
                            stt(hs[:, 0, s:s + 1], kk_[:, 0, last:last + 1], erl, t0_, ALU.mult, ALU.add, [L_ + "kk0", "E_re", L_ + "tiny0"], [f"hs{s}"])
                            stt(hs[:, 1, s:s + 1], kk_[:, 1, last:last + 1], erl, t1_, ALU.mult, ALU.subtract, [L_ + "kk1", "E_re", L_ + "tiny1"], [f"hs{s}"])
                            yield

                        def PL(out, a_, b_, op, r, w):
                            return S.op("pool", lambda e: e.tensor_tensor(out=out, in0=a_, in1=b_, op=op), r, w)
                        PL(v2(tpa[:]), v2(kk_[:, 0, :]), Er, ALU.mult, [L_ + "kk0", "E_re"], [L_ + "tpa"])
                        PL(v2(tpb[:]), v2(kk_[:, 1, :]), Ei, ALU.mult, [L_ + "kk1", "E_im"], [L_ + "tpb"])
                        PL(hbuf[:, 0, :], tpa[:], tpb[:], ALU.add, [L_ + "tpa", L_ + "tpb"], [hk])
                        PL(v2(tpa[:]), v2(kk_[:, 1, :]), Er, ALU.mult, [L_ + "kk1", "E_re"], [L_ + "tpa"])
                        PL(v2(tpb[:]), v2(kk_[:, 0, :]), Ei, ALU.mult, [L_ + "kk0", "E_im"], [L_ + "tpb"])
                        PL(hbuf[:, 1, :], tpa[:], tpb[:], ALU.subtract, [L_ + "tpa", L_ + "tpb"], [hk])
                        yield
                        n_ = ycnt[0]; ycnt[0] += 2
                        mm(ps_y[:, 0, :], ckb[:, 0, s, :], hbuf[:, 0, :], n_ == 0, False, ["ckb", hk], ["ps_y0"])
                        mm(ps_y[:, 0, :], ckb[:, 1, s, :], hbuf[:, 1, :], False, n_ == 6, ["ckb", hk], ["ps_y0"])
                        yield

                    def _chain(*gs):
                        for g_ in gs:
                            yield from g_

                    def _run_lanes(gens):
                        gens = list(gens)
                        while gens:
                            for g_ in list(gens):
                                try:
                                    next(g_)
                                except StopIteration:
                                    gens.remove(g_)

                    _run_lanes([_chain(s5_tile(0, kt * 4 + 0), s5_tile(0, kt * 4 + 2)), _chain(s5_tile(1, kt * 4 + 1), s5_tile(1, kt * 4 + 3))])
                    stt(tq[2][:], u_f[:], pk("s5d", kt), ps_y[:, 0, :], ALU.mult, ALU.add, ["u_f", "pp", "ps_y0"], ["tq2"])
                    tt(tq[0][:], tq[2][:], tq[2][:], ALU.mult, ["tq2"], ["tq0"])
                    ts(tq[0][:], tq[0][:], 0.044715, 1.0, ALU.mult, ALU.add, ["tq0"], ["tq0"])
                    tt(tq[0][:], tq[0][:], tq[2][:], ALU.mult, ["tq0", "tq2"], ["tq0"])
                    act(tq[1][:], tq[0][:], AF.Sigmoid, ["tq0"], ["tq1"], scale=2.0 * math.sqrt(2.0 / math.pi))
                    tt(g_f[:, kt, :], tq[2][:], tq[1][:], ALU.mult, ["tq2", "tq1"], ["g_f"])
                    cp(g_b[:, kt, :], g_f[:, kt, :], ["g_f"], ["g_b"])
                for mt in range(4):
                    for k in range(4):
                        mm(ps_y[:, 1, :], wglub[:, k, mt * 128:(mt + 1) * 128], g_b[:, k, :], k == 0, k == 3, ["wglub", "g_b"], ["ps_y1"])
                    act(tq[1][:], ps_y[:, 1, :], AF.Sigmoid, ["ps_y1", "pp"], ["tq1"], bias=pk("bglu", mt))
                    tt(tq[0][:], g_f[:, mt, :], tq[1][:], ALU.mult, ["g_f", "tq1"], ["tq0"])
                    tt(mx_s5[:, mt, :], tq[0][:], zs5[:, mt, :], ALU.mult, ["tq0", "zs5"], ["mx_s5"])

                _phase(2)
                for h in range(4):
                    wv, wkey = load_chunk(f"mlA_{h}")
                    i = proj_fm(wv, wkey, 0, 96)
                    conv_tile(i, 96, h, q_b[0:96, h, :], "q_b")
                    i = proj_fm(wv, wkey, 96, 96)
                    conv_tile(i, 96, 4 + h, k_b[0:96, h, :], "k_b")
                    wvB, wkB = load_chunk(f"mlB_{h}")
                    wvC, wkC = load_chunk(f"mlC_{h}")
                    for e2 in range(2):
                        i = proj_fm(wvB, wkB, e2 * 96, 96)
                        act(osig[0:96, :], pa[0:96, i, :], AF.Sigmoid, [f"pa{i}"], ["osig"])
                        i = proj_fm(wvC, wkC, e2 * 96, 96)
                        act(tq[0][0:96, :], pa[0:96, i, :], AF.Silu, [f"pa{i}"], ["tq0"])
                        tt(oz[0:96, 2 * h + e2, :], tq[0][0:96, :], osig[0:96, :], ALU.mult, ["tq0", "osig"], ["oz"])
                    wv, wkey = load_chunk(f"mlV_{h}")
                    for cc in range(NCH):
                        i = next_pa()
                        for k in range(KT):
                            mm(pa[:, i, 0:192], xb[:, k, cc * CH:(cc + 1) * CH], wv[:, k, 0:192], k == 0, k == KT - 1, ["xb", wkey], [f"pa{i}"])
                        act(v_b[:, cc, h, 0:192], pa[:, i, 0:192], AF.Copy, [f"pa{i}"], ["v_b"])
                flush()
                _phase(3)
                for g in range(4):
                    wv, wkey = load_chunk(f"sX_{g}")
                    for r in range(3):
                        i = proj_fm(wv, wkey, r * 64, 64)
                        conv_tile(i, 64, 8 + g * 5 + r, x_f[0:64, 3 * g + r, :], "x_f")
                    wv, wkey = load_chunk(f"sBC_{g}")
                    i = proj_fm(wv, wkey, 0, 128)
                    conv_tile(i, 128, 8 + g * 5 + 3, B_b[:, g, :], "B_b")
                    i = proj_fm(wv, wkey, 128, 128)
                    conv_tile(i, 128, 8 + g * 5 + 4, C_b[:, g, :], "C_b")
                    wv, wkey = load_chunk(f"sZ_{g}")
                    for r in range(3):
                        i = proj_fm(wv, wkey, r * 64, 64)
                        act(zssd[0:64, 3 * g + r, :], pa[0:64, i, :], AF.Silu, [f"pa{i}"], ["zssd"])

                flush()
                _phase(4)
                def get_bc(ln, cc, col):
                    par = ln["par"]
                    S.op("pool", lambda e, par=par, cc=cc, col=col: e.tensor_copy(out=lbh[par][:], in_=vh[:, cc, col:col + 1].to_broadcast([P, 128])), ["vh"], [f"lbh{par}"])
                    S.op("pool", lambda e, par=par, cc=cc, col=col: e.tensor_copy(out=lbl[par][:], in_=vl[:, cc, col:col + 1].to_broadcast([P, 128])), ["vl"], [f"lbl{par}"])
                    mm(ln["bc"], lbh[par][:], tri_b[:], True, False, [f"lbh{par}", "tri_b"], [ln["k"] + "bc"])
                    mm(ln["bc"], lbl[par][:], tri_b[:], False, True, [f"lbl{par}", "tri_b"], [ln["k"] + "bc"])

                def ml_head(ln, cc, h):
                    cs = slice(cc * CH, (cc + 1) * CH)
                    par = ln["par"]; kp = ln["k"]; bc = ln["bc"]; ps_s = ln["s"]; ps_o = ln["o"]; ps_st = ln["st"]
                    bck = kp + "bc"
                    get_bc(ln, cc, h)
                    mm(ps_s[:, 0, :], k_b[0:96, h, cs], q_b[0:96, h, cs], True, True, ["k_b", "q_b"], [kp + "s0"])
                    yield
                    act(ebc[par][:], bc, AF.Exp, [bck], [f"ebc{par}"])
                    ts(tf[par][:], bc, ncum[:, cc, h:h + 1], 0.0, ALU.add, ALU.min, [bck, "ncum"], [f"tf{par}"])
                    tt(wcol[par][:, 0:1], bc[:, 127:128], bml[:, cc, h:h + 1], ALU.add, [bck, "bml"], [f"wcol{par}"])
                    yield
                    act(dT[par][:], tf[par][:], AF.Exp, [f"tf{par}", "gsb"], [f"dT{par}"], bias=gsb[:, cc, h:h + 1])
                    act(wcol[par][:, 1:2], wcol[par][:, 0:1], AF.Exp, [f"wcol{par}"], [f"wcolb{par}"])
                    yield
                    tt(dT[par][:], dT[par][:], tri_f[:], ALU.mult, [f"dT{par}", "tri_f"], [f"dT{par}"])
                    stt(A_b[par][:], ps_s[:, 0, :], DQK ** -0.5, dT[par][:], ALU.mult, ALU.mult, [kp + "s0", f"dT{par}"], [f"A_b{par}"])
                    stt(qs_b[par][0:96, :], q_b[0:96, h, cs], DQK ** -0.5, ebc[par][0:96, :], ALU.mult, ALU.mult, ["q_b", f"ebc{par}"], [f"qs_b{par}"])
                    yield
                    for e2 in range(2):
                        mm(ps_o[0:96, e2, :], v_b[:, cc, h, e2 * 96:(e2 + 1) * 96], A_b[par][:], True, False, ["v_b", f"A_b{par}"], [kp + f"o{e2}"])
                        mm(ps_o[0:96, e2, :], cst_b[0:96, h, e2 * 96:(e2 + 1) * 96], qs_b[par][0:96, :], False, True, ["cst_b", f"qs_b{par}"], [kp + f"o{e2}"])
                    mm(ps_o[0:96, 2, :], ones_b[:, 0:96], A_b[par][:], True, False, ["ones_b", f"A_b{par}"], [kp + "o2"])
                    mm(ps_o[0:96, 2, :], nbc_b[0:96, h, :], qs_b[par][0:96, :], False, True, ["nbc_b", f"qs_b{par}"], [kp + "o2"])
                    mm(ps_s[:, 2, 0:96], k_b[0:96, h, cs], ident_b[0:96, 0:96], True, True, ["k_b", "ident_b"], [kp + "s2"])
                    yield
                    act(rd[0:96, :], ps_o[0:96, 2, :], AF.Abs, [kp + "o2"], ["rd"])
                    ts1(kw_b[par][:], ps_s[:, 2, 0:96], wcol[par][:, 1:2], ALU.mult, [kp + "s2", f"wcolb{par}"], [f"kw_b{par}"])
                    yield
                    mm(ps_st[0:96, 0:193], kw_b[par][:], v_b[:, cc, h, :], True, True, [f"kw_b{par}", "v_b"], [kp + "st"])
                    ts1(rd[0:96, :], rd[0:96, :], 1.0, ALU.max, ["rd"], ["rd"])
                    DVE(lambda e: e.reciprocal(out=rd[0:96, :], in_=rd[0:96, :]), ["rd"], ["rd"])
                    for e2 in range(2):
                        tt(hn[0:96, e2, :], ps_o[0:96, e2, :], rd[0:96, :], ALU.mult, [kp + f"o{e2}", "rd"], ["hn"])
                    tt(sqm[0:96, :, :], hn[0:96, :, :], hn[0:96, :, :], ALU.mult, ["hn"], ["sqm"])
                    yield
                    for e2 in range(2):
                        mm(ps_s[0:96, 3, :], ones_b[0:96, 0:96], sqm[0:96, e2, :], e2 == 0, e2 == 1, ["ones_b", "sqm"], [kp + "s3"])
                    stt(cst_f[0:96, h, :], cst_f[0:96, h, :], ebc[par][0:96, 127:128], ps_st[0:96, 0:193], ALU.mult, ALU.add, ["cst_f", f"ebc{par}", kp + "st"], ["cst_f"])
                    cp(cst_b[0:96, h, :], cst_f[0:96, h, 0:192], ["cst_f"], ["cst_b"])
                    cp(nbc_b[0:96, h, :], cst_f[0:96, h, 192:193].to_broadcast([96, 96]), ["cst_f"], ["nbc_b"])
                    yield
                    act(rs[0:96, :], ps_s[0:96, 3, :], AF.Sqrt, [kp + "s3"], ["rs"], bias=EPS, scale=1.0 / DV)
                    yield
                    DVE(lambda e: e.reciprocal(out=rs[0:96, :], in_=rs[0:96, :]), ["rs"], ["rs"])
                    for e2 in range(2):
                        stt(yvm[e2][0:96, :], hn[0:96, e2, :], pk("mlg", 2 * h + e2, 1, 96), rs[0:96, :], ALU.mult, ALU.mult, ["hn", "pp", "rs"], [f"yvm{e2}"])
                        tt(mx_ml[0:96, 2 * h + e2, cs], yvm[e2][0:96, :], oz[0:96, 2 * h + e2, cs], ALU.mult, [f"yvm{e2}", "oz"], ["mx_ml"])
                    yield

                def ssd_group(ln, cc, g):
                    cs = slice(cc * CH, (cc + 1) * CH)
                    par = ln["par"]; kp = ln["k"]; bc = ln["bc"]; ps_s = ln["s"]; ps_o = ln["o"]; ps_st = ln["st"]
                    btok_ = ln["btok"]; cbm_ = ln["cbm"]; bk = f"btok{par}"; ck = f"cbm{par}"
                    bck = kp + "bc"
                    mm(ps_s[:, 0, :], B_b[:, g, cs], C_b[:, g, cs], True, True, ["B_b", "C_b"], [kp + "s0"])
                    mm(ps_s[:, 2, :], B_b[:, g, cs], ident_b[:], True, True, ["B_b", "ident_b"], [kp + "s2"])
                    yield
                    tt(cbm_[:], ps_s[:, 0, :], tri_f[:], ALU.mult, [kp + "s0", "tri_f"], [ck])
                    act(btok_[:], ps_s[:, 2, :], AF.Copy, [kp + "s2"], [bk])
                    yield
                    for r in range(3):
                        hh = 3 * g + r
                        get_bc(ln, cc, 4 + hh)
                        cp(xh[par][0:64, :], x_f[0:64, hh, cs], ["x_f"], [f"xh{par}"])
                        tt(yv[par][0:64, :], x_f[0:64, hh, cs], xh[par][0:64, :], ALU.subtract, ["x_f", f"xh{par}"], [f"yv{par}"])
                        cp(xl[par][0:64, :], yv[par][0:64, :], [f"yv{par}"], [f"xl{par}"])
                        yield
                        mm(ps_s[:, 1, 64:128], xh[par][0:64, :], ident_b[0:64, 0:64], True, False, [f"xh{par}", "ident_b"], [kp + "s1c"])
                        mm(ps_s[:, 1, 64:128], xl[par][0:64, :], ident_b[0:64, 0:64], False, True, [f"xl{par}", "ident_b"], [kp + "s1c"])
                        act(ebc[par][:], bc, AF.Exp, [bck], [f"ebc{par}"])
                        ts(tf[par][:], bc, ncum[:, cc, 4 + hh:5 + hh], 0.0, ALU.add, ALU.min, [bck, "ncum"], [f"tf{par}"])
                        tt(wcol[par][:, 0:1], bc[:, 127:128], ncum[:, cc, 4 + hh:5 + hh], ALU.add, [bck, "ncum"], [f"wcol{par}"])
                        yield
                        act(dT[par][:], tf[par][:], AF.Exp, [f"tf{par}"], [f"dT{par}"])
                        act(wcol[par][:, 1:2], wcol[par][:, 0:1], AF.Exp, [f"wcol{par}"], [f"wcolb{par}"])
                        ts1(dtx_b[par][:], ps_s[:, 1, 64:128], dtv[:, cc, hh:hh + 1], ALU.mult, [kp + "s1c", "dtv"], [f"dtx_b{par}"])
                        tt(Cs_b[par][:], C_b[:, g, cs], ebc[par][:], ALU.mult, ["C_b", f"ebc{par}"], [f"Cs_b{par}"])
                        yield
                        tt(A_b[par][:], dT[par][:], cbm_[:], ALU.mult, [f"dT{par}", ck], [f"A_b{par}"])
                        ts1(dtxw_b[par][:], dtx_b[par][:], wcol[par][:, 1:2], ALU.mult, [f"dtx_b{par}", f"wcolb{par}"], [f"dtxw_b{par}"])
                        yield
                        mm(ps_o[0:64, r, :], dtx_b[par][:], A_b[par][:], True, False, [f"dtx_b{par}", f"A_b{par}"], [kp + f"o{r}"])
                        mm(ps_o[0:64, r, :], sst_b[:, hh, :], Cs_b[par][:], False, True, [f"sst_b{hh}", f"Cs_b{par}"], [kp + f"o{r}"])
                        mm(ps_st[:, 256 + r * 64:256 + (r + 1) * 64], btok_[:], dtxw_b[par][:], True, True, [bk, f"dtxw_b{par}"], [kp + f"stb{r}"])
                        yield
                        stt(yv[par][0:64, :], x_f[0:64, hh, cs], pk("sd", hh, 1, 64), ps_o[0:64, r, :], ALU.mult, ALU.add, ["x_f", "pp", kp + f"o{r}"], [f"yv{par}"])
                        tt(yz[0:64, hh, :], yv[par][0:64, :], zssd[0:64, hh, cs], ALU.mult, [f"yv{par}", "zssd"], [f"yz{hh}"])
                        stt(sst_f[:, hh, :], sst_f[:, hh, :], ebc[par][:, 127:128], ps_st[:, 256 + r * 64:256 + (r + 1) * 64], ALU.mult, ALU.add, [f"sst_f{hh}", f"ebc{par}", kp + f"stb{r}"], [f"sst_f{hh}"])
                        cp(sst_b[:, hh, :], sst_f[:, hh, :], [f"sst_f{hh}"], [f"sst_b{hh}"])
                        yield

                def run_lanes(gens):
                    gens = list(gens)
                    while gens:
                        for g_ in list(gens):
                            try:
                                next(g_)
                            except StopIteration:
                                gens.remove(g_)

                def chain(*gs):
                    for g_ in gs:
                        yield from g_

                for cc in range(NCH):
                    cs = slice(cc * CH, (cc + 1) * CH)
                    run_lanes([chain(*[ml_head(LN0, cc, h) for h in range(4)], ssd_group(LN0, cc, 3)),
                               chain(*[ssd_group(LN1, cc, g) for g in range(3)])])
                    act(sqb[0:64, :, :], yz[0:64, :, :], AF.Square, [f"yz{i_}" for i_ in range(12)], ["sqb"])
                    for hh in range(12):
                        mm(ps_s[0:64, 3, :], ones_b[0:64, 0:64], sqb[0:64, hh, :], hh == 0, hh == 11, ["ones_b", "sqb"], ["ps_s3"])
                    act(rs[0:64, :], ps_s[0:64, 3, :], AF.Sqrt, ["ps_s3"], ["rs"], bias=EPS, scale=1.0 / 768.0)
                    DVE(lambda e: e.reciprocal(out=rs[0:64, :], in_=rs[0:64, :]), ["rs"], ["rs"])
                    for hh in range(12):
                        stt(mx_ssd[0:64, hh, cs], yz[0:64, hh, :], pk("sg", hh, 1, 64), rs[0:64, :], ALU.mult, ALU.mult, [f"yz{hh}", "pp", "rs"], ["mx_ssd"])

                _phase(5)
                if debug and li == 0:
                    S.dma("pool", lambda e, t0=t0: e.dma_start(out=dbg_d[0:4, :, t0:t0 + TT].rearrange("k p t -> p k t"), in_=mx_s5[:]), "dbg0", ["mx_s5"], [])
                    S.dma("pool", lambda e, t0=t0: e.dma_start(out=dbg_d[4:12, 0:96, t0:t0 + TT].rearrange("k p t -> p k t"), in_=mx_ml[0:96]), "dbg1", ["mx_ml"], [])
                    S.dma("pool", lambda e, t0=t0: e.dma_start(out=dbg_d[12:24, 0:64, t0:t0 + TT].rearrange("k p t -> p k t"), in_=mx_ssd[0:64]), "dbg2", ["mx_ssd"], [])
                for m in range(KT):
                    b = state["wb"]; state["wb"] = 1 - b
                    wv = wfl[b][:, 0:24 * 128].rearrange("p (k c) -> p k c", k=24)
                    wk = f"wfl{b}"
                    mc = slice(m * 128, (m + 1) * 128)
                    S.dma("sp", lambda e, b=b, m=m, li=li: e.dma_start(out=wfl[b][:, 0:3072], in_=woutb_d[li][:, m * 3072:(m + 1) * 3072]), wk, WKEYS[li], [wk])
                    i = next_pa()
                    n = 0
                    for k in range(4):
                        mm(pa[:, i, :], wv[:, k, :], mx_s5[:, k, :], n == 0, False, [wk, "mx_s5"], [f"pa{i}"]); n += 1
                    for k in range(8):
                        mm(pa[:, i, :], wv[0:96, 4 + k, :], mx_ml[0:96, k, :], False, False, [wk, "mx_ml"], [f"pa{i}"]); n += 1
                    for k in range(12):
                        mm(pa[:, i, :], wv[0:64, 12 + k, :], mx_ssd[0:64, k, :], False, k == 11, [wk, "mx_ssd"], [f"pa{i}"]); n += 1
                    stt(xf[:, m, :], xf[:, m, :], ALPHA, pa[:, i, :], ALU.mult, ALU.add, ["xf", f"pa{i}"], ["xf"])
                    sq = sqz[m % 2]
                    act(sq[:, 0:TT // 2].bitcast(BF16) if False else zb[m % 2][:], xf[:, m, :], AF.Copy, ["xf"], [f"zb{m % 2}"])
                    act(sqzb[m % 2][:], xf[:, m, :], AF.Square, ["xf"], [f"sqzb{m % 2}"])
                    mm(ps_b[:, 0, :], ones_b[:], zb[m % 2][:], m == 0, m == KT - 1, ["ones_b", f"zb{m % 2}"], ["ps_b0"])
                    mm(ps_y[:, 0, :], ones_b[:], sqzb[m % 2][:], m == 0, m == KT - 1, ["ones_b", f"sqzb{m % 2}"], ["ps_y0"])
                act(mean[:], ps_b[:, 0, :], AF.Copy, ["ps_b0"], ["mean"], scale=1.0 / D)
                tt(m2[:], mean[:], mean[:], ALU.mult, ["mean"], ["m2"])
                stt(m2[:], ps_y[:, 0, :], 1.0 / D, m2[:], ALU.mult, ALU.subtract, ["ps_y0", "m2"], ["m2"])
                act(rstd[:], m2[:], AF.Sqrt, ["m2"], ["rstd"], bias=EPS, scale=1.0)
                DVE(lambda e: e.reciprocal(out=rstd[:], in_=rstd[:]), ["rstd"], ["rstd"])
                out_toks = []
                for m in range(KT):
                    l_ = lt[m % 2]; lk = f"lt{m % 2}"
                    yb = state["yb"]; state["yb"] = (yb + 1) % 4
                    tt(l_[:], xf[:, m, :], mean[:], ALU.subtract, ["xf", "mean"], [lk])
                    tt(l_[:], l_[:], rstd[:], ALU.mult, [lk, "rstd"], [lk])
                    act(ybuf[yb][:], l_[:], AF.Identity, [lk, "pp"], [f"ybuf{yb}"], bias=pk("lnb", m), scale=pk("lng", m))
                    out_toks.append(S.dma("sp", lambda e, yb=yb, m=m, t0=t0, xdst=xdst: e.dma_start(out=xdst[m * 128:(m + 1) * 128, t0:t0 + TT], in_=ybuf[yb][:]), f"out{yb}", [f"ybuf{yb}"], [xdst_key]))
                state["out_toks"] = out_toks
    except _Stop:
        pass
    for t in state.get("out_toks", []):
        S.wait_tok("sp", t)
    for dk in ("dbg0", "dbg1", "dbg2"):
        ent = S.dsem.get(dk)
        if ent:
            S.wait_tok("pool", (id(ent[0]), ent[1], "dma"))
    for yb in range(4):
        ent = S.dsem.get(f"out{yb}")
        if ent:
            S.wait_tok("sp", (id(ent[0]), ent[1], "dma"))
    S.replay()
    return S


def _pack_layer_params(inp, l):
    f = np.float32
    pp = np.zeros((P, NPK), f)
    pp[:, PK["lng"]:PK["lng"] + 16] = inp["ln_g"][l].reshape(16, 128).T
    pp[:, PK["lnb"]:PK["lnb"] + 16] = inp["ln_b"][l].reshape(16, 128).T
    lre = inp["s5_lambda_re"][l].reshape(16, 2, 64).reshape(16, 128).T
    lim = inp["s5_lambda_im"][l].reshape(16, 2, 64).reshape(16, 128).T
    lst = np.repeat(inp["s5_log_step"][l].reshape(16, 2, 1), 64, axis=2).reshape(16, 128).T
    pp[:, PK["lre"]:PK["lre"] + 16] = lre
    pp[:, PK["lim"]:PK["lim"] + 16] = lim
    pp[:, PK["lst"]:PK["lst"] + 16] = lst
    pp[:, PK["s5d"]:PK["s5d"] + 4] = inp["s5_d"][l].reshape(4, 128).T
    pp[:, PK["bglu"]:PK["bglu"] + 4] = inp["s5_b_glu"][l].reshape(4, 128).T
    mcw = inp["ml_conv_w"][l]; mcb = inp["ml_conv_b"][l]
    scw = inp["ssd_conv_w"][l]; scb = inp["ssd_conv_b"][l]
    tiles = []
    for h in range(4):
        tiles.append((mcw[:, h * 96:(h + 1) * 96], mcb[h * 96:(h + 1) * 96]))
    for h in range(4):
        tiles.append((mcw[:, 384 + h * 96:384 + (h + 1) * 96], mcb[384 + h * 96:384 + (h + 1) * 96]))
    for g in range(4):
        for r in range(3):
            o = (3 * g + r) * 64
            tiles.append((scw[:, o:o + 64], scb[o:o + 64]))
        o = 768 + g * 128
        tiles.append((scw[:, o:o + 128], scb[o:o + 128]))
        o = 1280 + g * 128
        tiles.append((scw[:, o:o + 128], scb[o:o + 128]))
    for t, (w, b) in enumerate(tiles):
        m = w.shape[1]
        pp[0:m, PK["cw"] + 4 * t:PK["cw"] + 4 * t + 4] = w.T
        pp[0:m, PK["cb"] + t] = b
    pp[0:96, PK["mlg"]:PK["mlg"] + 8] = inp["ml_norm_g"][l].reshape(8, 96).T
    pp[0:64, PK["sd"]:PK["sd"] + 12] = np.repeat(inp["ssd_d"][l][None, :], 64, axis=0)
    pp[0:64, PK["sg"]:PK["sg"] + 12] = inp["ssd_norm_g"][l].reshape(12, 64).T
    gb = np.concatenate([inp["ml_i_bias"][l], inp["ml_f_bias"][l], inp["ssd_dt_bias"][l]])
    pp[:, PK["gb"]:PK["gb"] + 20] = np.repeat(gb[None, :], P, axis=0)
    pp[:, PK["al"]:PK["al"] + 12] = np.repeat(inp["ssd_a_log"][l][None, :], P, axis=0)
    bt = np.zeros((P, 2, 16, 128), f)
    ct = np.zeros((P, 2, 16, 128), f)
    for ri, (bsrc, csrc) in enumerate([(inp["s5_b_re"][l], inp["s5_c_re"][l]), (inp["s5_b_im"][l], inp["s5_c_im"][l])]):
        for s in range(16):
            for gg in range(2):
                g = 2 * s + gg
                r0 = (g % 8) * 16
                bt[r0:r0 + 16, ri, s, gg * 64:(gg + 1) * 64] = bsrc[g].T
                ct[gg * 64:(gg + 1) * 64, ri, s, r0:r0 + 16] = csrc[g].T
    return pp, bt.reshape(P, -1), ct


def _prep(inputs):
    inp = {k: np.asarray(v) for k, v in inputs.items()}
    L = inp["w_in"].shape[0]
    win = np.empty((L, P, 16 * N_IN), np.float32)
    wout = np.zeros((L, P, 16 * 3072), np.float32)
    for l in range(L):
        wl = inp["w_in"][l]
        for name, cols in CHUNKS:
            c0, w = CH_OFF[name]
            blk = wl[:, cols].reshape(16, P, w).transpose(1, 0, 2).reshape(P, 16 * w)
            win[l, :, 16 * c0:16 * c0 + 16 * w] = blk
        wo = inp["w_out"][l]
        dst = wout[l].reshape(P, 16, 24, 128)
        src = wo.reshape(D, 16, 128)
        dst[:, :, 0:4, :] = src[0:512].reshape(4, 128, 16, 128).transpose(1, 2, 0, 3)
        dst[0:96, :, 4:12, :] = src[512:1280].reshape(8, 96, 16, 128).transpose(1, 2, 0, 3)
        dst[0:64, :, 12:24, :] = src[1280:2048].reshape(12, 64, 16, 128).transpose(1, 2, 0, 3)
    pps, bts, cts = [], [], []
    for l in range(L):
        a, b, c_ = _pack_layer_params(inp, l)
        pps.append(a); bts.append(b); cts.append(c_)
    common = {
        "win": win,
        "wout": wout,
        "pp": np.stack(pps), "bt": np.stack(bts), "ct": np.stack(cts),
        "wglu": np.ascontiguousarray(inp["s5_w_glu"]),
        "jrow": np.repeat(np.arange(1, 129, dtype=np.float32)[None, :], P, axis=0),
    }
    return inp, common


_CACHE = {}


def _get_prog(L, T):
    key = (L, T)
    if key not in _CACHE:
        nc = bass.Bass("TRN2", target_bir_lowering=False)
        es = ExitStack()
        build(nc, es, L, T)
        _CACHE[key] = (nc, es)
    return _CACHE[key][0]


FUSED = True


def kernel(**inputs):
    inp, common = _prep(inputs)
    x = inp["x"]
    B, T, _ = x.shape
    L = inp["w_in"].shape[0]
    n_cores = 8
    xT = [np.ascontiguousarray(x[b].T) for b in range(B)]
    if FUSED:
        nc = _get_prog(L, T)
        in_maps = [dict(common, xT=xT[c % B]) for c in range(n_cores)]
        res = run_bass_kernel_spmd(nc, in_maps, core_ids=list(range(n_cores)))
        outs = [res.results[b]["yT"] for b in range(B)]
    else:
        nc = _get_prog(1, T)
        cur = xT
        for l in range(L):
            cl = {k: (v[l:l + 1] if k in ("win", "wout", "pp", "bt", "ct", "wglu") else v) for k, v in common.items()}
            in_maps = [dict(cl, xT=cur[c % B]) for c in range(n_cores)]
            res = run_bass_kernel_spmd(nc, in_maps, core_ids=list(range(n_cores)))
            cur = [res.results[b]["yT"] for b in range(B)]
        outs = cur
    return np.stack([o.T for o in outs]).astype(np.float32)
```

```python
import math
from contextlib import ExitStack

import numpy as np
import concourse.bass as bass
import concourse.mybir as mybir
from concourse.bass_utils import run_bass_kernel_spmd

F32 = mybir.dt.float32
BF16 = mybir.dt.bfloat16
ALU = mybir.AluOpType
AF = mybir.ActivationFunctionType

ENGS = ("pe", "act", "dve", "pool", "sp")

P = 128
D = 2048
KT = 16
TT = 256
CH = 128
NCH = TT // CH
DEPTH = 4
N_IN = 6676
ALPHA = (2.0 * DEPTH) ** 0.25
EPS = 1e-5
DQK = 96
DV = 192
TWO_PI = 2.0 * math.pi
CW1 = 6.28125
CW2 = TWO_PI - 6.28125
MAGIC = 12582912.0

O_S5U, O_S5Z, O_MLQ, O_MLK, O_MLV = 0, 512, 1024, 1408, 1792
O_MLI, O_MLF, O_MLO, O_MLZ = 2560, 2564, 2568, 3336
O_SX, O_SB, O_SC, O_SDT, O_SZ = 4104, 4872, 5384, 5896, 5908


def _chunks():
    ch = []
    ar = np.arange
    ch.append(("gates", np.concatenate([ar(O_MLI, O_MLI + 4), ar(O_MLF, O_MLF + 4), ar(O_SDT, O_SDT + 12)])))
    for kt in range(4):
        ch.append((f"s5_{kt}", np.concatenate([ar(O_S5U + kt * 128, O_S5U + kt * 128 + 128), ar(O_S5Z + kt * 128, O_S5Z + kt * 128 + 128)])))
    for h in range(4):
        ch.append((f"mlA_{h}", np.concatenate([ar(O_MLQ + h * 96, O_MLQ + h * 96 + 96), ar(O_MLK + h * 96, O_MLK + h * 96 + 96)])))
        ch.append((f"mlB_{h}", ar(O_MLO + h * 192, O_MLO + h * 192 + 192)))
        ch.append((f"mlC_{h}", ar(O_MLZ + h * 192, O_MLZ + h * 192 + 192)))
        ch.append((f"mlV_{h}", ar(O_MLV + h * 192, O_MLV + h * 192 + 192)))
    for g in range(4):
        ch.append((f"sX_{g}", ar(O_SX + g * 192, O_SX + g * 192 + 192)))
        ch.append((f"sBC_{g}", np.concatenate([ar(O_SB + g * 128, O_SB + g * 128 + 128), ar(O_SC + g * 128, O_SC + g * 128 + 128)])))
        ch.append((f"sZ_{g}", ar(O_SZ + g * 192, O_SZ + g * 192 + 192)))
    return ch


CHUNKS = _chunks()
CH_OFF = {}
_o = 0
for _n, _c in CHUNKS:
    CH_OFF[_n] = (_o, len(_c))
    _o += len(_c)
assert _o == N_IN
PERM = np.concatenate([c for _, c in CHUNKS])

PK = {}
_o = 0
for _n, _w in [("lng", 16), ("lnb", 16), ("lre", 16), ("lim", 16), ("lst", 16), ("s5d", 4), ("bglu", 4),
               ("cw", 28 * 4), ("cb", 28), ("mlg", 8), ("sd", 12), ("sg", 12), ("gb", 20), ("al", 12)]:
    PK[_n] = _o
    _o += _w
NPK = _o


class Sched:
    SEM_EPOCH = 30000

    def __init__(self, nc, es, same_engine_sync=True):
        self.nc = nc
        self.es = es
        self.ops = {e: [] for e in ENGS}
        self.sems = {}
        self.esem = {}
        self.eseq = {}
        self.nsem = 0
        for e in ENGS:
            self._new_esem(e)
        self.dsem = {}
        self.last_write = {}
        self.readers = {}
        self.waited = {e: {} for e in ENGS}
        self.same_engine_sync = same_engine_sync
        self.n_ins = 0

    def _mksem(self, name):
        s = self.es.enter_context(self.nc.semaphore(name))
        self.nsem += 1
        self.sems[id(s)] = s
        return s

    def _new_esem(self, e):
        self.esem[e] = self._mksem(f"s_{e}_{self.nsem}")
        self.eseq[e] = 0

    ALIAS = {}
    NOSELF = ()

    def _deps(self, eng, reads, writes):
        need = {}
        reads = [self.ALIAS.get(k, k) for k in reads]
        writes = [self.ALIAS.get(k, k) for k in writes] + [k for k in reads if k.startswith("B_")]

        def add(tok):
            if tok is None:
                return
            sid, val, src = tok
            if src == eng and (eng == "pe" or not self.same_engine_sync or eng in self.NOSELF):
                return
            if need.get(sid, 0) < val:
                need[sid] = val

        for k in reads:
            add(self.last_write.get(k))
        for k in writes:
            add(self.last_write.get(k))
            for t in self.readers.get(k, ()):
                add(t)
        w = self.waited[eng]
        for sid, val in need.items():
            if w.get(sid, 0) >= val:
                continue
            w[sid] = val
            self.ops[eng].append(("wait", self.sems[sid], val))

    def _commit(self, tok, reads, writes):
        reads = [self.ALIAS.get(k, k) for k in reads]
        writes = [self.ALIAS.get(k, k) for k in writes] + [k for k in reads if k.startswith("B_")]
        for k in writes:
            self.last_write[k] = tok
            self.readers[k] = []
        for k in reads:
            lst = self.readers.setdefault(k, [])
            lst.append(tok)
            if len(lst) > 8:
                best = {}
                for t in lst:
                    if best.get(t[0], (0, 0, 0))[1] < t[1]:
                        best[t[0]] = t
                self.readers[k] = list(best.values())

    def op(self, eng, fn, reads=(), writes=()):
        self._deps(eng, reads, writes)
        if self.eseq[eng] >= self.SEM_EPOCH:
            self._new_esem(eng)
        self.eseq[eng] += 1
        sem = self.esem[eng]
        tok = (id(sem), self.eseq[eng], eng)
        self.ops[eng].append(("ins", fn, sem, 1))
        self._commit(tok, reads, writes)
        self.n_ins += 1
        return tok

    def dma(self, q, fn, key, reads=(), writes=()):
        self._deps(q, reads, writes)
        if key not in self.dsem:
            self.dsem[key] = [self._mksem(f"d_{self.nsem}"), 0]
        ent = self.dsem[key]
        ent[1] += 16
        tok = (id(ent[0]), ent[1], "dma")
        self.ops[q].append(("ins", fn, ent[0], 16))
        self._commit(tok, reads, writes)
        self.n_ins += 1
        return tok

    def wait_tok(self, eng, tok):
        sid, val, _ = tok
        if self.waited[eng].get(sid, 0) >= val:
            return
        self.waited[eng][sid] = val
        self.ops[eng].append(("wait", self.sems[sid], val))

    def replay(self):
        nc = self.nc
        ops = self.ops

        def run(engobj, lst):
            for it in lst:
                if it[0] == "wait":
                    engobj.wait_ge(it[1], it[2])
                else:
                    it[1](engobj).then_inc(it[2], it[3])

        with nc.Block() as block:

            @block.tensor
            def _(e):
                run(e, ops["pe"])

            @block.scalar
            def _(e):
                run(e, ops["act"])

            @block.vector
            def _(e):
                run(e, ops["dve"])

            @block.gpsimd
            def _(e):
                run(e, ops["pool"])

            @block.sync
            def _(e):
                run(e, ops["sp"])


class _Stop(Exception):
    pass


def build(nc, es, L, T, layer0=0, debug=False, phase=99):
    NT = T // TT
    import os as _os
    S = Sched(nc, es, same_engine_sync=(_os.environ.get('SES', '1') == '1'))
    S.NOSELF = tuple(x for x in _os.environ.get('NOSELF', '').split(',') if x)

    def dram(name, shape, kind, dt=F32):
        return nc.dram_tensor(name, shape, dt, kind=kind).ap()

    xT_d = dram("xT", [D, T], "ExternalInput")
    win_d = dram("win", [L, P, 16 * N_IN], "ExternalInput")
    wout_d = dram("wout", [L, P, 16 * 3072], "ExternalInput")
    pp_d = dram("pp", [L, P, NPK], "ExternalInput")
    bt_d = dram("bt", [L, P, 2 * 16 * 128], "ExternalInput")
    ct_d = dram("ct", [L, P, 2, 16, 128], "ExternalInput")
    wglu_d = dram("wglu", [L, 512, 512], "ExternalInput")
    jrow_d = dram("jrow", [P, 128], "ExternalInput")
    yT_d = dram("yT", [D, T], "ExternalOutput")
    xs_d = [dram(f"xs{i}", [D, T], "Internal") for i in range(2)] if L > 1 else []
    dbg_d = dram("dbg", [24, P, T], "ExternalOutput") if debug else None
    winb_d = [dram(f"winb{l}", [P, 16 * N_IN], "Internal", BF16) for l in range(L)]
    woutb_d = [dram(f"woutb{l}", [P, 16 * 3072], "Internal", BF16) for l in range(L)]

    def sb(name, shape, dt=F32):
        return es.enter_context(nc.sbuf_tensor(name, shape, dt))

    def psum(name, shape, dt=F32):
        return es.enter_context(nc.psum_tensor(name, shape, dt))

    ident_f = sb("ident_f", [P, 128]); ident_b = sb("ident_b", [P, 128], BF16)
    tri_f = sb("tri_f", [P, 128]); ones_f = sb("ones_f", [P, 128]); ones_b = sb("ones_b", [P, 128], BF16)
    jrow = sb("jrow_s", [P, 128])
    wfl = [sb(f"wfl{i}", [P, 16 * 256], BF16) for i in range(2)]
    xb = sb("xb", [P, KT, TT], BF16)
    xf = sb("xf", [P, KT, TT])
    mx_s5 = sb("mx_s5", [P, 4, TT], BF16); mx_ml = sb("mx_ml", [P, 8, TT], BF16); mx_ssd = sb("mx_ssd", [P, 12, TT], BF16)
    pp = sb("pp_s", [P, NPK])
    btb = sb("btb", [P, 2, 16, 128], BF16); ckb = sb("ckb", [P, 2, 16, 128], BF16)
    ctst = [sb(f"ctst{i}", [P, 2, 128]) for i in range(2)]
    wglub = sb("wglub", [P, 4, 512], BF16)
    E_re = sb("E_re", [P, 16, 128]); E_im = sb("E_im", [P, 16, 128])
    s5c = sb("s5c", [P, 24, 16])
    hs = sb("hs", [P, 2, 16])
    u_f = sb("u_f", [P, TT]); u_b = sb("u_b", [P, TT], BF16); zs5 = sb("zs5", [P, 4, TT], BF16)
    bp = sb("bp", [P, 2, TT]); kk = sb("kk", [P, 2, TT]); tq = [sb(f"tq{i}", [P, TT]) for i in range(3)]
    hb = [sb(f"hb{i}", [P, 2, TT], BF16) for i in range(2)]
    g_f = sb("g_f", [P, 4, TT]); g_b = sb("g_b", [P, 4, TT], BF16)
    tiny = sb("tiny", [P, 8])
    tp = [sb(f"tp{i}", [P, TT]) for i in range(2)]
    cbuf = [sb(f"cbuf{i}", [P, 3 + TT], BF16) for i in range(2)]
    halo = sb("halo", [P, 28, 3], BF16)
    dg = [sb(f"dg{i}", [P, 4, 128], BF16) for i in range(2)]
    q_b = sb("q_b", [P, 4, TT], BF16); k_b = sb("k_b", [P, 4, TT], BF16)
    oz = sb("oz", [P, 8, TT], BF16); osig = sb("osig", [P, TT], BF16)
    v_b = sb("v_b", [P, NCH, 4, 193], BF16)
    cst_f = sb("cst_f", [P, 4, 193]); cst_b = sb("cst_b", [P, 4, 192], BF16); nbc_b = sb("nbc_b", [P, 4, 96], BF16)
    x_f = sb("x_f", [P, 12, TT]); B_b = sb("B_b", [P, 4, TT], BF16); C_b = sb("C_b", [P, 4, TT], BF16)
    zssd = sb("zssd", [P, 12, TT], BF16)
    sst_f = sb("sst_f", [P, 12, 64]); sst_b = sb("sst_b", [P, 12, 64], BF16)
    yz = sb("yz", [P, 12, CH]); sqb = sb("sqb", [P, 12, CH], BF16)
    btok = sb("btok", [P, 128], BF16); cbm = sb("cbm", [P, 128])
    _xff = xf[:].rearrange("p k t -> p (k t)")
    _xsf = x_f[:].rearrange("p k t -> p (k t)")
    etmp = [_xff[:, 0:2048].rearrange("p (s j) -> p s j", s=16), _xff[:, 2048:4096].rearrange("p (s j) -> p s j", s=16),
            _xsf[:, 0:2048].rearrange("p (s j) -> p s j", s=16)]
    gsb = sb("gsb", [P, NCH, 20]); tmpg = sb("tmpg", [P, NCH, 16]); vals = sb("vals", [P, NCH, 16])
    dtv = sb("dtv", [P, NCH, 12]); cum = sb("cum", [P, NCH, 16]); bml = sb("bml", [P, NCH, 4]); ncum = sb("ncum", [P, NCH, 16])
    arow = sb("arow", [P, 12])
    lbh = [sb(f"lbh{i}", [P, 128], BF16) for i in range(2)]
    lbl = [sb(f"lbl{i}", [P, 128], BF16) for i in range(2)]
    vh = sb("vh", [P, NCH, 16], BF16); vl = sb("vl", [P, NCH, 16], BF16); vtmp = sb("vtmp", [P, 16])
    tri_b = sb("tri_b", [P, 128], BF16)
    xh = [sb(f"xh{i}", [P, CH], BF16) for i in range(2)]; xl = [sb(f"xl{i}", [P, CH], BF16) for i in range(2)]
    zb = [sb(f"zb{i}", [P, TT], BF16) for i in range(2)]
    ebc = [sb(f"ebc{i}", [P, 128]) for i in range(2)]
    tf = [sb(f"tf{i}", [P, 128]) for i in range(2)]
    dT = [sb(f"dT{i}", [P, 128]) for i in range(2)]
    A_b = [sb(f"A_b{i}", [P, 128], BF16) for i in range(2)]
    qs_b = [sb(f"qs_b{i}", [P, 128], BF16) for i in range(2)]
    Cs_b = [sb(f"Cs_b{i}", [P, 128], BF16) for i in range(2)]
    dtx_b = [sb(f"dtx_b{i}", [P, 64], BF16) for i in range(2)]
    dtxw_b = [sb(f"dtxw_b{i}", [P, 64], BF16) for i in range(2)]
    kw_b = [sb(f"kw_b{i}", [P, 96], BF16) for i in range(2)]
    wcol = [sb(f"wcol{i}", [P, 2]) for i in range(2)]
    hn = sb("hn", [P, 2, CH]); sqm = sb("sqm", [P, 2, CH], BF16); rd = sb("rd", [P, CH]); rs = sb("rs", [P, CH])
    yv = [sb(f"yv{i}", [P, CH]) for i in range(2)]
    mean = sb("mean", [P, TT]); m2 = sb("m2", [P, TT]); rstd = sb("rstd", [P, TT]); sqz = [None, None]; sqzb = [sb(f"sqzb{i}", [P, TT], BF16) for i in range(2)]
    ybuf = [sb(f"ybuf{i}", [P, TT]) for i in range(4)]
    lt = [sb(f"lt{i}", [P, TT]) for i in range(2)]

    pa = psum("pa", [P, 4, 256])
    ps_b = psum("ps_b", [P, 2, 256])
    ps_y = psum("ps_y", [P, 2, 256])
    ps_bc = psum("ps_bc", [P, 4, 128])
    ps_s = psum("ps_s", [P, 4, 128])
    ps_o = psum("ps_o", [P, 4, 128])
    ps_st = psum("ps_st", [P, 512])

    btok1 = sb("btok1", [P, 128], BF16); cbm1 = sb("cbm1", [P, 128])
    bp1 = sb("bp1", [P, 2, TT]); kk1 = sb("kk1", [P, 2, TT]); tq1x = [sb(f"tq1x{i}", [P, TT]) for i in range(2)]; tp1x = [sb(f"tp1x{i}", [P, TT]) for i in range(2)]
    S5L = [{"psb": ps_b, "k": "ps_b", "bp": bp, "kk": kk, "tq": (tq[0], tq[1]), "tp": (tp[0], tp[1])},
           {"psb": ps_st[:].rearrange("p (a j) -> p a j", a=2), "k": "S5b", "bp": bp1, "kk": kk1, "tq": (tq1x[0], tq1x[1]), "tp": (tp1x[0], tp1x[1])}]
    yvm = [sb(f"yvm{i}", [P, CH]) for i in range(2)]
    LN0 = {"par": 0, "k": "ps_", "bc": ps_bc[:, 0, :], "s": ps_s, "o": ps_o, "st": ps_st, "btok": btok, "cbm": cbm}
    LN1 = {"par": 1, "k": "L1_", "bc": pa[:, 0, 0:128], "s": pa[:, 2:4, :].rearrange("p a (b j) -> p (a b) j", j=128),
           "o": ps_b[:].rearrange("p a (b j) -> p (a b) j", j=128), "st": ps_y[:].rearrange("p a j -> p (a j)"), "btok": btok1, "cbm": cbm1}
    state = {"pa": 0, "wb": 0, "cb": 0, "yb": 0, "par": 0}
    al = {"pa0": "B_pa0", "pa1": "B_pa0", "pa2": "B_pa1", "pa3": "B_pa1", "ps_b0": "B_b", "ps_b1": "B_b",
          "ps_y0": "B_y", "ps_y1": "B_y", "ps_bc0": "B_bc", "ps_bc1": "B_bc",
          "ps_s0": "B_s", "ps_s1": "B_s", "ps_s1b": "B_s", "ps_s1c": "B_s", "ps_s2": "B_s", "ps_s3": "B_s",
          "ps_o0": "B_o", "ps_o1": "B_o", "ps_o2": "B_o", "ps_st": "B_st", "ps_stb0": "B_st", "ps_stb1": "B_st", "ps_stb2": "B_st",
          "ps_bc": "B_bc", "S5b0": "B_st", "S5b1": "B_st", "L1_bc": "B_pa0", "L1_s0": "B_pa1", "L1_s1c": "B_pa1", "L1_s2": "B_pa1", "L1_s3": "B_pa1",
          "L1_o0": "B_b", "L1_o1": "B_b", "L1_o2": "B_b", "L1_st": "B_y", "L1_stb0": "B_y", "L1_stb1": "B_y", "L1_stb2": "B_y"}
    S.ALIAS = al

    def next_pa():
        i = state["pa"]; state["pa"] = (i + 1) % 4
        return i

    def DVE(fn, r, w): return S.op("dve", fn, r, w)
    def ACT(fn, r, w): return S.op("act", fn, r, w)
    def PE(fn, r, w): return S.op("pe", fn, r, w)

    def mm(out, lhsT, rhs, start, stop, r, w):
        return PE(lambda e: e.matmul(out, lhsT=lhsT, rhs=rhs, start=start, stop=stop), r, w)

    def tt(out, a, b, op, r, w): return DVE(lambda e: e.tensor_tensor(out=out, in0=a, in1=b, op=op), r, w)
    def ts(out, a, s1, s2, op0, op1, r, w): return DVE(lambda e: e.tensor_scalar(out=out, in0=a, scalar1=s1, scalar2=s2, op0=op0, op1=op1), r, w)
    def ts1(out, a, s1, op0, r, w): return DVE(lambda e: e.tensor_single_scalar(out=out, in_=a, scalar=s1, op=op0), r, w)
    def stt(out, a, sc, b, op0, op1, r, w): return DVE(lambda e: e.scalar_tensor_tensor(out=out, in0=a, scalar=sc, in1=b, op0=op0, op1=op1), r, w)
    def cp(out, a, r, w): return DVE(lambda e: e.tensor_copy(out=out, in_=a), r, w)
    def act(out, a, func, r, w, bias=None, scale=None):
        kw = {}
        if bias is not None: kw["bias"] = bias
        if scale is not None: kw["scale"] = scale
        return ACT(lambda e: e.activation(out=out, in_=a, func=func, **kw), r, w)

    S.op("pool", lambda e: e.memset(ident_f[:], 1.0), [], ["ident_f"])
    S.op("pool", lambda e: e.affine_select(out=ident_f[:], in_=ident_f[:], pattern=[[-1, 128]], compare_op=ALU.is_equal, fill=0.0, base=0, channel_multiplier=1), ["ident_f"], ["ident_f"])
    S.op("pool", lambda e: e.memset(tri_f[:], 1.0), [], ["tri_f"])
    S.op("pool", lambda e: e.affine_select(out=tri_f[:], in_=tri_f[:], pattern=[[1, 128]], compare_op=ALU.is_ge, fill=0.0, base=0, channel_multiplier=-1), ["tri_f"], ["tri_f"])
    S.op("pool", lambda e: e.memset(ones_f[:], 1.0), [], ["ones_f"])
    S.op("pool", lambda e: e.memset(ones_b[:], 1.0), [], ["ones_b"])
    S.op("pool", lambda e: e.memset(v_b[:], 1.0), [], ["v_b"])
    cp(ident_b[:], ident_f[:], ["ident_f"], ["ident_b"])
    cp(tri_b[:], tri_f[:], ["tri_f"], ["tri_b"])
    S.dma("sp", lambda e: e.dma_start(out=jrow[:], in_=jrow_d), "jrow", [], ["jrow"])

    WKEYS = {}
    for l in range(L):
        keys = []
        WI = 16 * N_IN // 8
        for part in range(8):
            cs_ = slice(part * WI, (part + 1) * WI)
            k_ = f"winb{l}_{part}"
            S.dma("pool", lambda e, l=l, cs_=cs_: e.dma_start(out=winb_d[l][:, cs_], in_=win_d[l][:, cs_]), f"wcast{l}", [], [k_])
            keys.append(k_)
        WO = 16 * 3072 // 4
        for part in range(4):
            cs_ = slice(part * WO, (part + 1) * WO)
            k_ = f"woutb{l}_{part}"
            S.dma("pool", lambda e, l=l, cs_=cs_: e.dma_start(out=woutb_d[l][:, cs_], in_=wout_d[l][:, cs_]), f"wcast{l}", [], [k_])
            keys.append(k_)
        WKEYS[l] = keys

    def range_reduce_sin(out, ang, quarter, neg, keys_r, key_w, t0, t1, k0, k1):
        ts(t0, ang, 1.0 / TWO_PI, 0.25 * quarter, ALU.mult, ALU.add, keys_r, [k0])
        ts1(t0, t0, MAGIC, ALU.add, [k0], [k0])
        ts1(t0, t0, -MAGIC, ALU.add, [k0], [k0])
        stt(t1, t0, -CW1, ang, ALU.mult, ALU.add, keys_r + [k0], [k1])
        stt(t1, t0, -CW2, t1, ALU.mult, ALU.add, [k0, k1], [k1])
        lo = -math.pi - quarter * math.pi / 2
        ts(t1, t1, lo, lo + TWO_PI, ALU.max, ALU.min, [k1], [k1])
        if neg:
            act(out, t1, AF.Sin, [k1], [key_w], bias=-quarter * math.pi / 2, scale=-1.0)
        else:
            act(out, t1, AF.Sin, [k1], [key_w], bias=quarter * math.pi / 2, scale=1.0)

    def c(i):
        return s5c[:, i, :]

    def _phase(n):
        if phase <= n:
            raise _Stop()

    try:
        for li in range(L):
            xsrc = xT_d if li == 0 else xs_d[(li - 1) % 2]
            xdst = yT_d if li == L - 1 else xs_d[li % 2]
            xsrc_key = "xT" if li == 0 else f"xs{(li - 1) % 2}"
            xdst_key = "yT" if li == L - 1 else f"xs{li % 2}"

            if li > 0:
                for yb_ in range(4):
                    ent_ = S.dsem.get(f"out{yb_}")
                    if ent_:
                        S.wait_tok("pool", (id(ent_[0]), ent_[1], "dma"))
                        S.wait_tok("sp", (id(ent_[0]), ent_[1], "dma"))
            S.dma("sp", lambda e, li=li: e.dma_start(out=pp[:], in_=pp_d[li]), "pp", [], ["pp"])
            S.dma("pool", lambda e, li=li: e.dma_start(out=btb[:].rearrange("p a s c -> p (a s c)"), in_=bt_d[li]), "btb", [], ["btb"])
            S.dma("pool", lambda e, li=li: e.dma_start(out=wglub[:], in_=wglu_d[li].rearrange("(k p) c -> p k c", p=P)), "wglub", [], ["wglub"])

            def pk(name, i=0, n=1, rows=P):
                o = PK[name] + i
                return pp[0:rows, o:o + n]

            lre = pp[:, PK["lre"]:PK["lre"] + 16]; lim = pp[:, PK["lim"]:PK["lim"] + 16]; lst = pp[:, PK["lst"]:PK["lst"] + 16]
            K = "s5c"
            act(c(0), lst, AF.Exp, ["pp"], [K])
            tt(c(1), lre, c(0), ALU.mult, ["pp", K], [K])
            tt(c(2), lim, c(0), ALU.mult, ["pp", K], [K])
            act(c(3), c(1), AF.Exp, [K], [K])
            range_reduce_sin(c(4), c(2), 0, False, [K], K, c(20), c(21), K, K)
            range_reduce_sin(c(5), c(2), 1, False, [K], K, c(20), c(21), K, K)
            tt(c(6), c(3), c(5), ALU.mult, [K], [K])
            tt(c(7), c(3), c(4), ALU.mult, [K], [K])
            ts1(c(8), c(6), -1.0, ALU.add, [K], [K])
            tt(c(9), lre, lre, ALU.mult, ["pp"], [K])
            tt(c(10), lim, lim, ALU.mult, ["pp"], [K])
            tt(c(9), c(9), c(10), ALU.add, [K], [K])
            DVE(lambda e: e.reciprocal(out=c(9), in_=c(9)), [K], [K])
            tt(c(10), c(8), lre, ALU.mult, [K, "pp"], [K])
            tt(c(11), c(7), lim, ALU.mult, [K, "pp"], [K])
            tt(c(10), c(10), c(11), ALU.add, [K], [K])
            tt(c(12), c(10), c(9), ALU.mult, [K], [K])
            tt(c(10), c(7), lre, ALU.mult, [K, "pp"], [K])
            tt(c(11), c(8), lim, ALU.mult, [K, "pp"], [K])
            tt(c(10), c(10), c(11), ALU.subtract, [K], [K])
            tt(c(13), c(10), c(9), ALU.mult, [K], [K])
            ts1(c(14), c(12), -1.0, ALU.mult, [K], [K])
            ts1(c(15), c(13), -1.0, ALU.mult, [K], [K])
            for s in range(16):
                ts1(etmp[0][:, s, :], jrow[:], c(2)[:, s:s + 1], ALU.mult, ["jrow", K], ["xf"])
            E_re_fl = E_re[:].rearrange("p s j -> p (s j)"); E_im_fl = E_im[:].rearrange("p s j -> p (s j)")
            range_reduce_sin(E_re_fl, _xff[:, 0:2048], 1, False, ["xf"], "E_re", _xff[:, 2048:4096], _xsf[:, 0:2048], "xf", "x_f")
            range_reduce_sin(E_im_fl, _xff[:, 0:2048], 0, True, ["xf"], "E_im", _xff[:, 2048:4096], _xsf[:, 0:2048], "xf", "x_f")
            for s in range(16):
                b = s % 2
                S.dma("sp", lambda e, li=li, s=s, b=b: e.dma_start(out=ctst[b][:], in_=ct_d[li][:, :, s, :]), f"ctst{b}", [], [f"ctst{b}"])
                ts1(tq[0][:, 0:128], ctst[b][:, 0, :], c(12)[:, s:s + 1], ALU.mult, [f"ctst{b}", K], ["tq0"])
                stt(ckb[:, 0, s, :], ctst[b][:, 1, :], c(15)[:, s:s + 1], tq[0][:, 0:128], ALU.mult, ALU.add, [f"ctst{b}", K, "tq0"], ["ckb"])
                ts1(tq[1][:, 0:128], ctst[b][:, 0, :], c(15)[:, s:s + 1], ALU.mult, [f"ctst{b}", K], ["tq1"])
                stt(ckb[:, 1, s, :], ctst[b][:, 1, :], c(14)[:, s:s + 1], tq[1][:, 0:128], ALU.mult, ALU.add, [f"ctst{b}", K, "tq1"], ["ckb"])
            act(arow[:], pp[:, PK["al"]:PK["al"] + 12], AF.Exp, ["pp"], ["arow"])
            ts1(arow[:], arow[:], -1.0, ALU.mult, ["arow"], ["arow"])
            S.op("pool", lambda e: e.memset(hs[:], 0.0), [], [f"hs{i_}" for i_ in range(16)])
            S.op("pool", lambda e: e.memset(halo[:], 0.0), [], ["halo"])
            S.op("pool", lambda e: e.memset(cst_f[:], 0.0), [], ["cst_f"])
            S.op("pool", lambda e: e.memset(cst_b[:], 0.0), [], ["cst_b"])
            S.op("pool", lambda e: e.memset(nbc_b[:], 0.0), [], ["nbc_b"])
            S.op("pool", lambda e: e.memset(sst_f[:], 0.0), [], [f"sst_f{i_}" for i_ in range(12)])
            S.op("pool", lambda e: e.memset(sst_b[:], 0.0), [], [f"sst_b{i_}" for i_ in range(12)])

            _phase(0)
            win_v = winb_d[li]

            def load_chunk(name):
                c0, w = CH_OFF[name]
                b = state["wb"]; state["wb"] = 1 - b
                view = wfl[b][:, 0:16 * w].rearrange("p (k c) -> p k c", k=16)
                src = win_v[:, 16 * c0:16 * c0 + 16 * w]
                dst = wfl[b][:, 0:16 * w]
                S.dma("sp", lambda e, dst=dst, src=src: e.dma_start(out=dst, in_=src), f"wfl{b}", WKEYS[li], [f"wfl{b}"])
                return view, f"wfl{b}"

            pend = []

            def flush():
                for f_ in list(pend):
                    f_()
                pend.clear()

            def proj_fm(wv, wkey, c0, M):
                i = next_pa()
                for k in range(KT):
                    mm(pa[0:M, i, :], wv[:, k, c0:c0 + M], xb[:, k, :], k == 0, k == KT - 1, [wkey, "xb"], [f"pa{i}"])
                flush()
                return i

            def conv_tile(i, M, tile, out_ap, out_key):
                b = state["cb"]; state["cb"] = 1 - b
                ck = f"cbuf{b}"
                act(cbuf[b][0:M, 3:3 + TT], pa[0:M, i, :], AF.Copy, [f"pa{i}"], [ck])
                cp(cbuf[b][0:M, 0:3], halo[0:M, tile, :], ["halo"], [ck])
                cp(halo[0:M, tile, :], cbuf[b][0:M, TT:TT + 3], [ck], ["halo"])
                cw0 = PK["cw"] + tile * 4
                S.op("pool", lambda e, b=b, M=M, cw0=cw0: e.tensor_tensor(out=dg[b][0:M, :, 0:M], in0=ident_f[0:M, 0:M].unsqueeze(1).to_broadcast([M, 4, M]),
                                                                   in1=pp[0:M, cw0:cw0 + 4].unsqueeze(2).to_broadcast([M, 4, M]), op=ALU.mult),
                     ["ident_f", "pp"], [f"dg{b}"])
                def fin():
                    j = next_pa()
                    for k in range(4):
                        mm(pa[0:M, j, :], dg[b][0:M, k, 0:M], cbuf[b][0:M, k:k + TT], k == 0, k == 3, [f"dg{b}", ck], [f"pa{j}"])
                    act(out_ap, pa[0:M, j, :], AF.Silu, [f"pa{j}", "pp"], [out_key], bias=pp[0:M, PK["cb"] + tile:PK["cb"] + tile + 1])
                pend.append(fin)

            for ti in range(NT):
                t0 = ti * TT
                if ti == 0:
                    S.dma("pool", lambda e, t0=t0, xsrc=xsrc: e.dma_start(out=xb[:], in_=xsrc.rearrange("(k p) t -> p k t", p=P)[:, :, t0:t0 + TT]), "xb", [xsrc_key], ["xb"])
                S.dma("sp", lambda e, t0=t0, xsrc=xsrc: e.dma_start(out=xf[:], in_=xsrc.rearrange("(k p) t -> p k t", p=P)[:, :, t0:t0 + TT]), "xf", [xsrc_key], ["xf"])

                _phase(0.3)
                wv, wkey = load_chunk("gates")
                _phase(0.4)
                for cc in range(NCH):
                    for k in range(KT):
                        mm(ps_st[:, 0:20], xb[:, k, cc * CH:(cc + 1) * CH], wv[:, k, 0:20], k == 0, k == KT - 1, ["xb", wkey], ["ps_st"])
                    _phase(0.5 if cc == 0 else 0.96 + (0.5 - 0.5) * 0.06)
                    tt(gsb[:, cc, :], ps_st[:, 0:20], pp[:, PK["gb"]:PK["gb"] + 20], ALU.add, ["ps_st", "pp"], ["gsb"])
                    _phase(0.6 if cc == 0 else 0.96 + (0.6 - 0.5) * 0.06)
                    act(tmpg[:, cc, 0:4], gsb[:, cc, 4:8], AF.Exp, ["gsb"], ["tmpg"], scale=-1.0)
                    act(tmpg[:, cc, 4:16], gsb[:, cc, 8:20], AF.Exp, ["gsb"], ["tmpg"])
                    act(vals[:, cc, 0:4], tmpg[:, cc, 0:4], AF.Ln, ["tmpg"], ["vals"], bias=1.0)
                    act(dtv[:, cc, :], tmpg[:, cc, 4:16], AF.Ln, ["tmpg"], ["dtv"], bias=1.0)
                    ts1(vals[:, cc, 0:4], vals[:, cc, 0:4], -1.0, ALU.mult, ["vals"], ["vals"])
                    tt(vals[:, cc, 4:16], dtv[:, cc, :], arow[:], ALU.mult, ["dtv", "arow"], ["vals"])
                    _phase(0.7 if cc == 0 else 0.96 + (0.7 - 0.5) * 0.06)
                    cp(vh[:, cc, :], vals[:, cc, :], ["vals"], ["vh"])
                    tt(vtmp[:], vals[:, cc, :], vh[:, cc, :], ALU.subtract, ["vals", "vh"], ["vtmp"])
                    cp(vl[:, cc, :], vtmp[:], ["vtmp"], ["vl"])
                    mm(ps_s[:, 1, 32:48], tri_b[:], vh[:, cc, :], True, False, ["tri_b", "vh"], ["ps_s1b"])
                    mm(ps_s[:, 1, 32:48], tri_b[:], vl[:, cc, :], False, True, ["tri_b", "vl"], ["ps_s1b"])
                    _phase(0.8 if cc == 0 else 0.96 + (0.8 - 0.5) * 0.06)
                    cp(cum[:, cc, :], ps_s[:, 1, 32:48], ["ps_s1b"], ["cum"])
                    _phase(0.85 if cc == 0 else 0.96 + (0.85 - 0.5) * 0.06)
                    tt(bml[:, cc, :], gsb[:, cc, 0:4], cum[:, cc, 0:4], ALU.subtract, ["gsb", "cum"], ["bml"])
                    _phase(0.9 if cc == 0 else 0.96 + (0.9 - 0.5) * 0.06)
                    ts1(ncum[:, cc, :], cum[:, cc, :], -1.0, ALU.mult, ["cum"], ["ncum"])
                    _phase(0.95 if cc == 0 else 0.96 + (0.95 - 0.5) * 0.06)

                _phase(1)
                for kt in range(4):
                    wv, wkey = load_chunk(f"s5_{kt}")
                    iu = proj_fm(wv, wkey, 0, 128)
                    iz = proj_fm(wv, wkey, 128, 128)
                    act(u_f[:], pa[:, iu, :], AF.Copy, [f"pa{iu}"], ["u_f"])
                    cp(u_b[:], pa[:, iu, :], [f"pa{iu}"], ["u_b"])
                    act(zs5[:, kt, :], pa[:, iz, :], AF.Silu, [f"pa{iz}"], ["zs5"])
                    ycnt = [0]

                    def s5_tile(j, s):
                        psb = S5L[j]["psb"]; kb = S5L[j]["k"]; bp_ = S5L[j]["bp"]; kk_ = S5L[j]["kk"]; tqa, tqb = S5L[j]["tq"]; tpa, tpb = S5L[j]["tp"]
                        hbuf = hb[j]; hk = f"hb{j}"
                        L_ = f"s5l{j}_"
                        mm(psb[:, 0, :], btb[:, 0, s, :], u_b[:], True, True, ["btb", "u_b"], [kb + "0"])
                        mm(psb[:, 1, :], btb[:, 1, s, :], u_b[:], True, True, ["btb", "u_b"], [kb + "1"])
                        yield
                        Er = E_re[:, s:s + 1, :].to_broadcast([P, 2, 128]); Ei = E_im[:, s:s + 1, :].to_broadcast([P, 2, 128])
                        v2 = lambda ap: ap.rearrange("p (a j) -> p a j", a=2)
                        tt(v2(tqa[:]), v2(psb[:, 0, :]), Er, ALU.mult, [kb + "0", "E_re"], [L_ + "tqa"])
                        tt(v2(tqb[:]), v2(psb[:, 1, :]), Ei, ALU.mult, [kb + "1", "E_im"], [L_ + "tqb"])
                        tt(bp_[:, 0, :], tqa[:], tqb[:], ALU.subtract, [L_ + "tqa", L_ + "tqb"], [L_ + "bp0"])
                        tt(v2(tqa[:]), v2(psb[:, 1, :]), Er, ALU.mult, [kb + "1", "E_re"], [L_ + "tqa"])
                        tt(v2(tqb[:]), v2(psb[:, 0, :]), Ei, ALU.mult, [kb + "0", "E_im"], [L_ + "tqb"])
                        tt(bp_[:, 1, :], tqa[:], tqb[:], ALU.add, [L_ + "tqa", L_ + "tqb"], [L_ + "bp1"])
                        yield
                        for a in range(2):
                            sl_ = slice(a * 128, (a + 1) * 128)
                            rbc = c(3)[:, s:s + 1].to_broadcast([P, 128])
                            for ri in range(2):
                                DVE(lambda e, ri=ri, sl_=sl_, rbc=rbc, s=s, kk_=kk_, bp_=bp_: e.tensor_tensor_scan(out=kk_[:, ri, sl_], data0=rbc, data1=bp_[:, ri, sl_], initial=hs[:, ri, s:s + 1], op0=ALU.mult, op1=ALU.add),
                                    [K, L_ + f"bp{ri}", f"hs{s}"], [L_ + f"kk{ri}"])
                            last = a * 128 + 127
                            erl = E_re[:, s, 127:128]; eil = E_im[:, s, 127:128]
                            t0_ = tiny[:, 2 * j:2 * j + 1]; t1_ = tiny[:, 2 * j + 1:2 * j + 2]
                            tt(t0_, kk_[:, 1, last:last + 1], eil, ALU.mult, [L_ + "kk1", "E_im"], [L_ + "tiny0"])
                            tt(t1_, kk_[:, 0, last:last + 1], eil, ALU.mult, [L_ + "kk0", "E_im"], [L_ + "tiny1"])
                            stt(hs[:, 0, s:s + 1], kk_[:, 0, last:last + 1], erl, t0_, ALU.mult, ALU.add, [L_ + "kk0", "E_re", L_ + "tiny0"], [f"hs{s}"])
                            stt(hs[:, 1, s:s + 1], kk_[:, 1, last:last + 1], erl, t1_, ALU.mult, ALU.subtract, [L_ + "kk1", "E_re", L_ + "tiny1"], [f"hs{s}"])
                            yield

                        def PL(out, a_, b_, op, r, w):
                            return S.op("pool", lambda e: e.tensor_tensor(out=out, in0=a_, in1=b_, op=op), r, w)
                        PL(v2(tpa[:]), v2(kk_[:, 0, :]), Er, ALU.mult, [L_ + "kk0", "E_re"], [L_ + "tpa"])
                        PL(v2(tpb[:]), v2(kk_[:, 1, :]), Ei, ALU.mult, [L_ + "kk1", "E_im"], [L_ + "tpb"])
                        PL(hbuf[:, 0, :], tpa[:], tpb[:], ALU.add, [L_ + "tpa", L_ + "tpb"], [hk])
                        PL(v2(tpa[:]), v2(kk_[:, 1, :]), Er, ALU.mult, [L_ + "kk1", "E_re"], [L_ + "tpa"])
                        PL(v2(tpb[:]), v2(kk_[:, 0, :]), Ei, ALU.mult, [L_ + "kk0", "E_im"], [L_ + "tpb"])
                        PL(hbuf[:, 1, :], tpa[:], tpb[:], ALU.subtract, [L_ + "tpa", L_ + "tpb"], [hk])
                        yield
                        n_ = ycnt[0]; ycnt[0] += 2
                        mm(ps_y[:, 0, :], ckb[:, 0, s, :], hbuf[:, 0, :], n_ == 0, False, ["ckb", hk], ["ps_y0"])
                        mm(ps_y[:, 0, :], ckb[:, 1, s, :], hbuf[:, 1, :], False, n_ == 6, ["ckb", hk], ["ps_y0"])
                        yield

                    def _chain(*gs):
                        for g_ in gs:
                            yield from g_

                    def _run_lanes(gens):
                        gens = list(gens)
                        while gens:
                            for g_ in list(gens):
                                try:
                                    next(g_)
                                except StopIteration:
                                    gens.remove(g_)

                    _run_lanes([_chain(s5_tile(0, kt * 4 + 0), s5_tile(0, kt * 4 + 2)), _chain(s5_tile(1, kt * 4 + 1), s5_tile(1, kt * 4 + 3))])
                    stt(tq[2][:], u_f[:], pk("s5d", kt), ps_y[:, 0, :], ALU.mult, ALU.add, ["u_f", "pp", "ps_y0"], ["tq2"])
                    tt(tq[0][:], tq[2][:], tq[2][:], ALU.mult, ["tq2"], ["tq0"])
                    ts(tq[0][:], tq[0][:], 0.044715, 1.0, ALU.mult, ALU.add, ["tq0"], ["tq0"])
                    tt(tq[0][:], tq[0][:], tq[2][:], ALU.mult, ["tq0", "tq2"], ["tq0"])
                    act(tq[1][:], tq[0][:], AF.Sigmoid, ["tq0"], ["tq1"], scale=2.0 * math.sqrt(2.0 / math.pi))
                    tt(g_f[:, kt, :], tq[2][:], tq[1][:], ALU.mult, ["tq2", "tq1"], ["g_f"])
                    cp(g_b[:, kt, :], g_f[:, kt, :], ["g_f"], ["g_b"])
                for mt in range(4):
                    for k in range(4):
                        mm(ps_y[:, 1, :], wglub[:, k, mt * 128:(mt + 1) * 128], g_b[:, k, :], k == 0, k == 3, ["wglub", "g_b"], ["ps_y1"])
                    act(tq[1][:], ps_y[:, 1, :], AF.Sigmoid, ["ps_y1", "pp"], ["tq1"], bias=pk("bglu", mt))
                    tt(tq[0][:], g_f[:, mt, :], tq[1][:], ALU.mult, ["g_f", "tq1"], ["tq0"])
                    tt(mx_s5[:, mt, :], tq[0][:], zs5[:, mt, :], ALU.mult, ["tq0", "zs5"], ["mx_s5"])

                _phase(2)
                for h in range(4):
                    wv, wkey = load_chunk(f"mlA_{h}")
                    i = proj_fm(wv, wkey, 0, 96)
                    conv_tile(i, 96, h, q_b[0:96, h, :], "q_b")
                    i = proj_fm(wv, wkey, 96, 96)
                    conv_tile(i, 96, 4 + h, k_b[0:96, h, :], "k_b")
                    wvB, wkB = load_chunk(f"mlB_{h}")
                    wvC, wkC = load_chunk(f"mlC_{h}")
                    for e2 in range(2):
                        i = proj_fm(wvB, wkB, e2 * 96, 96)
                        act(osig[0:96, :], pa[0:96, i, :], AF.Sigmoid, [f"pa{i}"], ["osig"])
                        i = proj_fm(wvC, wkC, e2 * 96, 96)
                        act(tq[0][0:96, :], pa[0:96, i, :], AF.Silu, [f"pa{i}"], ["tq0"])
                        tt(oz[0:96, 2 * h + e2, :], tq[0][0:96, :], osig[0:96, :], ALU.mult, ["tq0", "osig"], ["oz"])
                    wv, wkey = load_chunk(f"mlV_{h}")
                    for cc in range(NCH):
                        i = next_pa()
                        for k in range(KT):
                            mm(pa[:, i, 0:192], xb[:, k, cc * CH:(cc + 1) * CH], wv[:, k, 0:192], k == 0, k == KT - 1, ["xb", wkey], [f"pa{i}"])
                        act(v_b[:, cc, h, 0:192], pa[:, i, 0:192], AF.Copy, [f"pa{i}"], ["v_b"])
                flush()
                _phase(3)
                for g in range(4):
                    wv, wkey = load_chunk(f"sX_{g}")
                    for r in range(3):
                        i = proj_fm(wv, wkey, r * 64, 64)
                        conv_tile(i, 64, 8 + g * 5 + r, x_f[0:64, 3 * g + r, :], "x_f")
                    wv, wkey = load_chunk(f"sBC_{g}")
                    i = proj_fm(wv, wkey, 0, 128)
                    conv_tile(i, 128, 8 + g * 5 + 3, B_b[:, g, :], "B_b")
                    i = proj_fm(wv, wkey, 128, 128)
                    conv_tile(i, 128, 8 + g * 5 + 4, C_b[:, g, :], "C_b")
                    wv, wkey = load_chunk(f"sZ_{g}")
                    for r in range(3):
                        i = proj_fm(wv, wkey, r * 64, 64)
                        act(zssd[0:64, 3 * g + r, :], pa[0:64, i, :], AF.Silu, [f"pa{i}"], ["zssd"])

                flush()
                if ti + 1 < NT:
                    S.dma("pool", lambda e, t1=t0 + TT, xsrc=xsrc: e.dma_start(out=xb[:], in_=xsrc.rearrange("(k p) t -> p k t", p=P)[:, :, t1:t1 + TT]), "xb", [xsrc_key], ["xb"])
                _phase(4)
                def get_bc(ln, cc, col):
                    par = ln["par"]
                    S.op("pool", lambda e, par=par, cc=cc, col=col: e.tensor_copy(out=lbh[par][:], in_=vh[:, cc, col:col + 1].to_broadcast([P, 128])), ["vh"], [f"lbh{par}"])
                    S.op("pool", lambda e, par=par, cc=cc, col=col: e.tensor_copy(out=lbl[par][:], in_=vl[:, cc, col:col + 1].to_broadcast([P, 128])), ["vl"], [f"lbl{par}"])
                    mm(ln["bc"], lbh[par][:], tri_b[:], True, False, [f"lbh{par}", "tri_b"], [ln["k"] + "bc"])
                    mm(ln["bc"], lbl[par][:], tri_b[:], False, True, [f"lbl{par}", "tri_b"], [ln["k"] + "bc"])

                def ml_head(ln, cc, h):
                    cs = slice(cc * CH, (cc + 1) * CH)
                    par = ln["par"]; kp = ln["k"]; bc = ln["bc"]; ps_s = ln["s"]; ps_o = ln["o"]; ps_st = ln["st"]
                    bck = kp + "bc"
                    get_bc(ln, cc, h)
                    mm(ps_s[:, 0, :], k_b[0:96, h, cs], q_b[0:96, h, cs], True, True, ["k_b", "q_b"], [kp + "s0"])
                    yield
                    act(ebc[par][:], bc, AF.Exp, [bck], [f"ebc{par}"])
                    ts(tf[par][:], bc, ncum[:, cc, h:h + 1], 0.0, ALU.add, ALU.min, [bck, "ncum"], [f"tf{par}"])
                    tt(wcol[par][:, 0:1], bc[:, 127:128], bml[:, cc, h:h + 1], ALU.add, [bck, "bml"], [f"wcol{par}"])
                    yield
                    act(dT[par][:], tf[par][:], AF.Exp, [f"tf{par}", "gsb"], [f"dT{par}"], bias=gsb[:, cc, h:h + 1])
                    act(wcol[par][:, 1:2], wcol[par][:, 0:1], AF.Exp, [f"wcol{par}"], [f"wcolb{par}"])
                    yield
                    tt(dT[par][:], dT[par][:], tri_f[:], ALU.mult, [f"dT{par}", "tri_f"], [f"dT{par}"])
                    stt(A_b[par][:], ps_s[:, 0, :], DQK ** -0.5, dT[par][:], ALU.mult, ALU.mult, [kp + "s0", f"dT{par}"], [f"A_b{par}"])
                    stt(qs_b[par][0:96, :], q_b[0:96, h, cs], DQK ** -0.5, ebc[par][0:96, :], ALU.mult, ALU.mult, ["q_b", f"ebc{par}"], [f"qs_b{par}"])
                    yield
                    for e2 in range(2):
                        mm(ps_o[0:96, e2, :], v_b[:, cc, h, e2 * 96:(e2 + 1) * 96], A_b[par][:], True, False, ["v_b", f"A_b{par}"], [kp + f"o{e2}"])
                        mm(ps_o[0:96, e2, :], cst_b[0:96, h, e2 * 96:(e2 + 1) * 96], qs_b[par][0:96, :], False, True, ["cst_b", f"qs_b{par}"], [kp + f"o{e2}"])
                    mm(ps_o[0:96, 2, :], ones_b[:, 0:96], A_b[par][:], True, False, ["ones_b", f"A_b{par}"], [kp + "o2"])
                    mm(ps_o[0:96, 2, :], nbc_b[0:96, h, :], qs_b[par][0:96, :], False, True, ["nbc_b", f"qs_b{par}"], [kp + "o2"])
                    mm(ps_s[:, 2, 0:96], k_b[0:96, h, cs], ident_b[0:96, 0:96], True, True, ["k_b", "ident_b"], [kp + "s2"])
                    yield
                    act(rd[0:96, :], ps_o[0:96, 2, :], AF.Abs, [kp + "o2"], ["rd"])
                    ts1(kw_b[par][:], ps_s[:, 2, 0:96], wcol[par][:, 1:2], ALU.mult, [kp + "s2", f"wcolb{par}"], [f"kw_b{par}"])
                    yield
                    mm(ps_st[0:96, 0:193], kw_b[par][:], v_b[:, cc, h, :], True, True, [f"kw_b{par}", "v_b"], [kp + "st"])
                    ts1(rd[0:96, :], rd[0:96, :], 1.0, ALU.max, ["rd"], ["rd"])
                    DVE(lambda e: e.reciprocal(out=rd[0:96, :], in_=rd[0:96, :]), ["rd"], ["rd"])
                    for e2 in range(2):
                        tt(hn[0:96, e2, :], ps_o[0:96, e2, :], rd[0:96, :], ALU.mult, [kp + f"o{e2}", "rd"], ["hn"])
                    tt(sqm[0:96, :, :], hn[0:96, :, :], hn[0:96, :, :], ALU.mult, ["hn"], ["sqm"])
                    yield
                    for e2 in range(2):
                        mm(ps_s[0:96, 3, :], ones_b[0:96, 0:96], sqm[0:96, e2, :], e2 == 0, e2 == 1, ["ones_b", "sqm"], [kp + "s3"])
                    stt(cst_f[0:96, h, :], cst_f[0:96, h, :], ebc[par][0:96, 127:128], ps_st[0:96, 0:193], ALU.mult, ALU.add, ["cst_f", f"ebc{par}", kp + "st"], ["cst_f"])
                    cp(cst_b[0:96, h, :], cst_f[0:96, h, 0:192], ["cst_f"], ["cst_b"])
                    cp(nbc_b[0:96, h, :], cst_f[0:96, h, 192:193].to_broadcast([96, 96]), ["cst_f"], ["nbc_b"])
                    yield
                    act(rs[0:96, :], ps_s[0:96, 3, :], AF.Sqrt, [kp + "s3"], ["rs"], bias=EPS, scale=1.0 / DV)
                    yield
                    DVE(lambda e: e.reciprocal(out=rs[0:96, :], in_=rs[0:96, :]), ["rs"], ["rs"])
                    for e2 in range(2):
                        stt(yvm[e2][0:96, :], hn[0:96, e2, :], pk("mlg", 2 * h + e2, 1, 96), rs[0:96, :], ALU.mult, ALU.mult, ["hn", "pp", "rs"], [f"yvm{e2}"])
                        tt(mx_ml[0:96, 2 * h + e2, cs], yvm[e2][0:96, :], oz[0:96, 2 * h + e2, cs], ALU.mult, [f"yvm{e2}", "oz"], ["mx_ml"])
                    yield

                def ssd_group(ln, cc, g):
                    cs = slice(cc * CH, (cc + 1) * CH)
                    par = ln["par"]; kp = ln["k"]; bc = ln["bc"]; ps_s = ln["s"]; ps_o = ln["o"]; ps_st = ln["st"]
                    btok_ = ln["btok"]; cbm_ = ln["cbm"]; bk = f"btok{par}"; ck = f"cbm{par}"
                    bck = kp + "bc"
                    mm(ps_s[:, 0, :], B_b[:, g, cs], C_b[:, g, cs], True, True, ["B_b", "C_b"], [kp + "s0"])
                    mm(ps_s[:, 2, :], B_b[:, g, cs], ident_b[:], True, True, ["B_b", "ident_b"], [kp + "s2"])
                    yield
                    tt(cbm_[:], ps_s[:, 0, :], tri_f[:], ALU.mult, [kp + "s0", "tri_f"], [ck])
                    act(btok_[:], ps_s[:, 2, :], AF.Copy, [kp + "s2"], [bk])
                    yield
                    for r in range(3):
                        hh = 3 * g + r
                        get_bc(ln, cc, 4 + hh)
                        cp(xh[par][0:64, :], x_f[0:64, hh, cs], ["x_f"], [f"xh{par}"])
                        tt(yv[par][0:64, :], x_f[0:64, hh, cs], xh[par][0:64, :], ALU.subtract, ["x_f", f"xh{par}"], [f"yv{par}"])
                        cp(xl[par][0:64, :], yv[par][0:64, :], [f"yv{par}"], [f"xl{par}"])
                        yield
                        mm(ps_s[:, 1, 64:128], xh[par][0:64, :], ident_b[0:64, 0:64], True, False, [f"xh{par}", "ident_b"], [kp + "s1c"])
                        mm(ps_s[:, 1, 64:128], xl[par][0:64, :], ident_b[0:64, 0:64], False, True, [f"xl{par}", "ident_b"], [kp + "s1c"])
                        act(ebc[par][:], bc, AF.Exp, [bck], [f"ebc{par}"])
                        ts(tf[par][:], bc, ncum[:, cc, 4 + hh:5 + hh], 0.0, ALU.add, ALU.min, [bck, "ncum"], [f"tf{par}"])
                        tt(wcol[par][:, 0:1], bc[:, 127:128], ncum[:, cc, 4 + hh:5 + hh], ALU.add, [bck, "ncum"], [f"wcol{par}"])
                        yield
                        act(dT[par][:], tf[par][:], AF.Exp, [f"tf{par}"], [f"dT{par}"])
                        act(wcol[par][:, 1:2], wcol[par][:, 0:1], AF.Exp, [f"wcol{par}"], [f"wcolb{par}"])
                        ts1(dtx_b[par][:], ps_s[:, 1, 64:128], dtv[:, cc, hh:hh + 1], ALU.mult, [kp + "s1c", "dtv"], [f"dtx_b{par}"])
                        tt(Cs_b[par][:], C_b[:, g, cs], ebc[par][:], ALU.mult, ["C_b", f"ebc{par}"], [f"Cs_b{par}"])
                        yield
                        tt(A_b[par][:], dT[par][:], cbm_[:], ALU.mult, [f"dT{par}", ck], [f"A_b{par}"])
                        ts1(dtxw_b[par][:], dtx_b[par][:], wcol[par][:, 1:2], ALU.mult, [f"dtx_b{par}", f"wcolb{par}"], [f"dtxw_b{par}"])
                        yield
                        mm(ps_o[0:64, r, :], dtx_b[par][:], A_b[par][:], True, False, [f"dtx_b{par}", f"A_b{par}"], [kp + f"o{r}"])
                        mm(ps_o[0:64, r, :], sst_b[:, hh, :], Cs_b[par][:], False, True, [f"sst_b{hh}", f"Cs_b{par}"], [kp + f"o{r}"])
                        mm(ps_st[:, 256 + r * 64:256 + (r + 1) * 64], btok_[:], dtxw_b[par][:], True, True, [bk, f"dtxw_b{par}"], [kp + f"stb{r}"])
                        yield
                        stt(yv[par][0:64, :], x_f[0:64, hh, cs], pk("sd", hh, 1, 64), ps_o[0:64, r, :], ALU.mult, ALU.add, ["x_f", "pp", kp + f"o{r}"], [f"yv{par}"])
                        tt(yz[0:64, hh, :], yv[par][0:64, :], zssd[0:64, hh, cs], ALU.mult, [f"yv{par}", "zssd"], [f"yz{hh}"])
                        stt(sst_f[:, hh, :], sst_f[:, hh, :], ebc[par][:, 127:128], ps_st[:, 256 + r * 64:256 + (r + 1) * 64], ALU.mult, ALU.add, [f"sst_f{hh}", f"ebc{par}", kp + f"stb{r}"], [f"sst_f{hh}"])
                        cp(sst_b[:, hh, :], sst_f[:, hh, :], [f"sst_f{hh}"], [f"sst_b{hh}"])
                        yield

                def run_lanes(gens):
                    gens = list(gens)
                    while gens:
                        for g_ in list(gens):
                            try:
                                next(g_)
                            except StopIteration:
                                gens.remove(g_)

                def chain(*gs):
                    for g_ in gs:
                        yield from g_

                for cc in range(NCH):
                    cs = slice(cc * CH, (cc + 1) * CH)
                    run_lanes([chain(*[ml_head(LN0, cc, h) for h in range(4)], ssd_group(LN0, cc, 3)),
                               chain(*[ssd_group(LN1, cc, g) for g in range(3)])])
                    act(sqb[0:64, :, :], yz[0:64, :, :], AF.Square, [f"yz{i_}" for i_ in range(12)], ["sqb"])
                    for hh in range(12):
                        mm(ps_s[0:64, 3, :], ones_b[0:64, 0:64], sqb[0:64, hh, :], hh == 0, hh == 11, ["ones_b", "sqb"], ["ps_s3"])
                    act(rs[0:64, :], ps_s[0:64, 3, :], AF.Sqrt, ["ps_s3"], ["rs"], bias=EPS, scale=1.0 / 768.0)
                    DVE(lambda e: e.reciprocal(out=rs[0:64, :], in_=rs[0:64, :]), ["rs"], ["rs"])
                    for hh in range(12):
                        stt(mx_ssd[0:64, hh, cs], yz[0:64, hh, :], pk("sg", hh, 1, 64), rs[0:64, :], ALU.mult, ALU.mult, [f"yz{hh}", "pp", "rs"], ["mx_ssd"])

                _phase(5)
                if debug and li == 0:
                    S.dma("pool", lambda e, t0=t0: e.dma_start(out=dbg_d[0:4, :, t0:t0 + TT].rearrange("k p t -> p k t"), in_=mx_s5[:]), "dbg0", ["mx_s5"], [])
                    S.dma("pool", lambda e, t0=t0: e.dma_start(out=dbg_d[4:12, 0:96, t0:t0 + TT].rearrange("k p t -> p k t"), in_=mx_ml[0:96]), "dbg1", ["mx_ml"], [])
                    S.dma("pool", lambda e, t0=t0: e.dma_start(out=dbg_d[12:24, 0:64, t0:t0 + TT].rearrange("k p t -> p k t"), in_=mx_ssd[0:64]), "dbg2", ["mx_ssd"], [])
                for m in range(KT):
                    b = state["wb"]; state["wb"] = 1 - b
                    wv = wfl[b][:, 0:24 * 128].rearrange("p (k c) -> p k c", k=24)
                    wk = f"wfl{b}"
                    mc = slice(m * 128, (m + 1) * 128)
                    S.dma("sp", lambda e, b=b, m=m, li=li: e.dma_start(out=wfl[b][:, 0:3072], in_=woutb_d[li][:, m * 3072:(m + 1) * 3072]), wk, WKEYS[li], [wk])
                    i = next_pa()
                    n = 0
                    for k in range(4):
                        mm(pa[:, i, :], wv[:, k, :], mx_s5[:, k, :], n == 0, False, [wk, "mx_s5"], [f"pa{i}"]); n += 1
                    for k in range(8):
                        mm(pa[:, i, :], wv[0:96, 4 + k, :], mx_ml[0:96, k, :], False, False, [wk, "mx_ml"], [f"pa{i}"]); n += 1
                    for k in range(12):
                        mm(pa[:, i, :], wv[0:64, 12 + k, :], mx_ssd[0:64, k, :], False, k == 11, [wk, "mx_ssd"], [f"pa{i}"]); n += 1
                    stt(xf[:, m, :], xf[:, m, :], ALPHA, pa[:, i, :], ALU.mult, ALU.add, ["xf", f"pa{i}"], ["xf"])
                    sq = sqz[m % 2]
                    act(sq[:, 0:TT // 2].bitcast(BF16) if False else zb[m % 2][:], xf[:, m, :], AF.Copy, ["xf"], [f"zb{m % 2}"])
                    act(sqzb[m % 2][:], xf[:, m, :], AF.Square, ["xf"], [f"sqzb{m % 2}"])
                    mm(ps_b[:, 0, :], ones_b[:], zb[m % 2][:], m == 0, m == KT - 1, ["ones_b", f"zb{m % 2}"], ["ps_b0"])
                    mm(ps_y[:, 0, :], ones_b[:], sqzb[m % 2][:], m == 0, m == KT - 1, ["ones_b", f"sqzb{m % 2}"], ["ps_y0"])
                act(mean[:], ps_b[:, 0, :], AF.Copy, ["ps_b0"], ["mean"], scale=1.0 / D)
                tt(m2[:], mean[:], mean[:], ALU.mult, ["mean"], ["m2"])
                stt(m2[:], ps_y[:, 0, :], 1.0 / D, m2[:], ALU.mult, ALU.subtract, ["ps_y0", "m2"], ["m2"])
                act(rstd[:], m2[:], AF.Sqrt, ["m2"], ["rstd"], bias=EPS, scale=1.0)
                DVE(lambda e: e.reciprocal(out=rstd[:], in_=rstd[:]), ["rstd"], ["rstd"])
                out_toks = []
                for m in range(KT):
                    l_ = lt[m % 2]; lk = f"lt{m % 2}"
                    yb = state["yb"]; state["yb"] = (yb + 1) % 4
                    tt(l_[:], xf[:, m, :], mean[:], ALU.subtract, ["xf", "mean"], [lk])
                    tt(l_[:], l_[:], rstd[:], ALU.mult, [lk, "rstd"], [lk])
                    act(ybuf[yb][:], l_[:], AF.Identity, [lk, "pp"], [f"ybuf{yb}"], bias=pk("lnb", m), scale=pk("lng", m))
                    out_toks.append(S.dma("sp", lambda e, yb=yb, m=m, t0=t0, xdst=xdst: e.dma_start(out=xdst[m * 128:(m + 1) * 128, t0:t0 + TT], in_=ybuf[yb][:]), f"out{yb}", [f"ybuf{yb}"], [xdst_key]))
                state["out_toks"] = out_toks
    except _Stop:
        pass
    for t in state.get("out_toks", []):
        S.wait_tok("sp", t)
    for dk in ("dbg0", "dbg1", "dbg2"):
        ent = S.dsem.get(dk)
        if ent:
            S.wait_tok("pool", (id(ent[0]), ent[1], "dma"))
    for yb in range(4):
        ent = S.dsem.get(f"out{yb}")
        if ent:
            S.wait_tok("sp", (id(ent[0]), ent[1], "dma"))
    S.replay()
    return S


def _pack_layer_params(inp, l):
    f = np.float32
    pp = np.zeros((P, NPK), f)
    pp[:, PK["lng"]:PK["lng"] + 16] = inp["ln_g"][l].reshape(16, 128).T
    pp[:, PK["lnb"]:PK["lnb"] + 16] = inp["ln_b"][l].reshape(16, 128).T
    lre = inp["s5_lambda_re"][l].reshape(16, 2, 64).reshape(16, 128).T
    lim = inp["s5_lambda_im"][l].reshape(16, 2, 64).reshape(16, 128).T
    lst = np.repeat(inp["s5_log_step"][l].reshape(16, 2, 1), 64, axis=2).reshape(16, 128).T
    pp[:, PK["lre"]:PK["lre"] + 16] = lre
    pp[:, PK["lim"]:PK["lim"] + 16] = lim
    pp[:, PK["lst"]:PK["lst"] + 16] = lst
    pp[:, PK["s5d"]:PK["s5d"] + 4] = inp["s5_d"][l].reshape(4, 128).T
    pp[:, PK["bglu"]:PK["bglu"] + 4] = inp["s5_b_glu"][l].reshape(4, 128).T
    mcw = inp["ml_conv_w"][l]; mcb = inp["ml_conv_b"][l]
    scw = inp["ssd_conv_w"][l]; scb = inp["ssd_conv_b"][l]
    tiles = []
    for h in range(4):
        tiles.append((mcw[:, h * 96:(h + 1) * 96], mcb[h * 96:(h + 1) * 96]))
    for h in range(4):
        tiles.append((mcw[:, 384 + h * 96:384 + (h + 1) * 96], mcb[384 + h * 96:384 + (h + 1) * 96]))
    for g in range(4):
        for r in range(3):
            o = (3 * g + r) * 64
            tiles.append((scw[:, o:o + 64], scb[o:o + 64]))
        o = 768 + g * 128
        tiles.append((scw[:, o:o + 128], scb[o:o + 128]))
        o = 1280 + g * 128
        tiles.append((scw[:, o:o + 128], scb[o:o + 128]))
    for t, (w, b) in enumerate(tiles):
        m = w.shape[1]
        pp[0:m, PK["cw"] + 4 * t:PK["cw"] + 4 * t + 4] = w.T
        pp[0:m, PK["cb"] + t] = b
    pp[0:96, PK["mlg"]:PK["mlg"] + 8] = inp["ml_norm_g"][l].reshape(8, 96).T
    pp[0:64, PK["sd"]:PK["sd"] + 12] = np.repeat(inp["ssd_d"][l][None, :], 64, axis=0)
    pp[0:64, PK["sg"]:PK["sg"] + 12] = inp["ssd_norm_g"][l].reshape(12, 64).T
    gb = np.concatenate([inp["ml_i_bias"][l], inp["ml_f_bias"][l], inp["ssd_dt_bias"][l]])
    pp[:, PK["gb"]:PK["gb"] + 20] = np.repeat(gb[None, :], P, axis=0)
    pp[:, PK["al"]:PK["al"] + 12] = np.repeat(inp["ssd_a_log"][l][None, :], P, axis=0)
    bt = np.zeros((P, 2, 16, 128), f)
    ct = np.zeros((P, 2, 16, 128), f)
    for ri, (bsrc, csrc) in enumerate([(inp["s5_b_re"][l], inp["s5_c_re"][l]), (inp["s5_b_im"][l], inp["s5_c_im"][l])]):
        for s in range(16):
            for gg in range(2):
                g = 2 * s + gg
                r0 = (g % 8) * 16
                bt[r0:r0 + 16, ri, s, gg * 64:(gg + 1) * 64] = bsrc[g].T
                ct[gg * 64:(gg + 1) * 64, ri, s, r0:r0 + 16] = csrc[g].T
    return pp, bt.reshape(P, -1), ct


def _prep(inputs):
    inp = {k: np.asarray(v) for k, v in inputs.items()}
    L = inp["w_in"].shape[0]
    win = np.empty((L, P, 16 * N_IN), np.float32)
    wout = np.zeros((L, P, 16 * 3072), np.float32)
    for l in range(L):
        wl = inp["w_in"][l]
        for name, cols in CHUNKS:
            c0, w = CH_OFF[name]
            blk = wl[:, cols].reshape(16, P, w).transpose(1, 0, 2).reshape(P, 16 * w)
            win[l, :, 16 * c0:16 * c0 + 16 * w] = blk
        wo = inp["w_out"][l]
        dst = wout[l].reshape(P, 16, 24, 128)
        src = wo.reshape(D, 16, 128)
        dst[:, :, 0:4, :] = src[0:512].reshape(4, 128, 16, 128).transpose(1, 2, 0, 3)
        dst[0:96, :, 4:12, :] = src[512:1280].reshape(8, 96, 16, 128).transpose(1, 2, 0, 3)
        dst[0:64, :, 12:24, :] = src[1280:2048].reshape(12, 64, 16, 128).transpose(1, 2, 0, 3)
    pps, bts, cts = [], [], []
    for l in range(L):
        a, b, c_ = _pack_layer_params(inp, l)
        pps.append(a); bts.append(b); cts.append(c_)
    common = {
        "win": win,
        "wout": wout,
        "pp": np.stack(pps), "bt": np.stack(bts), "ct": np.stack(cts),
        "wglu": np.ascontiguousarray(inp["s5_w_glu"]),
        "jrow": np.repeat(np.arange(1, 129, dtype=np.float32)[None, :], P, axis=0),
    }
    return inp, common


_CACHE = {}


def _get_prog(L, T):
    key = (L, T)
    if key not in _CACHE:
        nc = bass.Bass("TRN2", target_bir_lowering=False)
        es = ExitStack()
        build(nc, es, L, T)
        _CACHE[key] = (nc, es)
    return _CACHE[key][0]


FUSED = True


def kernel(**inputs):
    inp, common = _prep(inputs)
    x = inp["x"]
    B, T, _ = x.shape
    L = inp["w_in"].shape[0]
    n_cores = 8
    xT = [np.ascontiguousarray(x[b].T) for b in range(B)]
    if FUSED:
        nc = _get_prog(L, T)
        in_maps = [dict(common, xT=xT[c % B]) for c in range(n_cores)]
        res = run_bass_kernel_spmd(nc, in_maps, core_ids=list(range(n_cores)))
        outs = [res.results[b]["yT"] for b in range(B)]
    else:
        nc = _get_prog(1, T)
        cur = xT
        for l in range(L):
            cl = {k: (v[l:l + 1] if k in ("win", "wout", "pp", "bt", "ct", "wglu") else v) for k, v in common.items()}
            in_maps = [dict(cl, xT=cur[c % B]) for c in range(n_cores)]
            res = run_bass_kernel_spmd(nc, in_maps, core_ids=list(range(n_cores)))
            cur = [res.results[b]["yT"] for b in range(B)]
        outs = cur
    return np.stack([o.T for o in outs]).astype(np.float32)
```

```python
import math
from contextlib import ExitStack

import numpy as np
import concourse.bass as bass
import concourse.mybir as mybir
from concourse.bass_utils import run_bass_kernel_spmd

F32 = mybir.dt.float32
BF16 = mybir.dt.bfloat16
ALU = mybir.AluOpType
AF = mybir.ActivationFunctionType

ENGS = ("pe", "act", "dve", "pool", "sp")

P = 128
D = 2048
KT = 16
TT = 256
CH = 128
NCH = TT // CH
DEPTH = 4
N_IN = 6676
ALPHA = (2.0 * DEPTH) ** 0.25
EPS = 1e-5
DQK = 96
DV = 192
TWO_PI = 2.0 * math.pi
CW1 = 6.28125
CW2 = TWO_PI - 6.28125
MAGIC = 12582912.0

O_S5U, O_S5Z, O_MLQ, O_MLK, O_MLV = 0, 512, 1024, 1408, 1792
O_MLI, O_MLF, O_MLO, O_MLZ = 2560, 2564, 2568, 3336
O_SX, O_SB, O_SC, O_SDT, O_SZ = 4104, 4872, 5384, 5896, 5908


def _chunks():
    ch = []
    ar = np.arange
    ch.append(("gates", np.concatenate([ar(O_MLI, O_MLI + 4), ar(O_MLF, O_MLF + 4), ar(O_SDT, O_SDT + 12)])))
    for kt in range(4):
        ch.append((f"s5_{kt}", np.concatenate([ar(O_S5U + kt * 128, O_S5U + kt * 128 + 128), ar(O_S5Z + kt * 128, O_S5Z + kt * 128 + 128)])))
    for h in range(4):
        ch.append((f"mlA_{h}", np.concatenate([ar(O_MLQ + h * 96, O_MLQ + h * 96 + 96), ar(O_MLK + h * 96, O_MLK + h * 96 + 96)])))
        ch.append((f"mlB_{h}", ar(O_MLO + h * 192, O_MLO + h * 192 + 192)))
        ch.append((f"mlC_{h}", ar(O_MLZ + h * 192, O_MLZ + h * 192 + 192)))
        ch.append((f"mlV_{h}", ar(O_MLV + h * 192, O_MLV + h * 192 + 192)))
    for g in range(4):
        ch.append((f"sX_{g}", ar(O_SX + g * 192, O_SX + g * 192 + 192)))
        ch.append((f"sBC_{g}", np.concatenate([ar(O_SB + g * 128, O_SB + g * 128 + 128), ar(O_SC + g * 128, O_SC + g * 128 + 128)])))
        ch.append((f"sZ_{g}", ar(O_SZ + g * 192, O_SZ + g * 192 + 192)))
    return ch


CHUNKS = _chunks()
CH_OFF = {}
_o = 0
for _n, _c in CHUNKS:
    CH_OFF[_n] = (_o, len(_c))
    _o += len(_c)
assert _o == N_IN
PERM = np.concatenate([c for _, c in CHUNKS])

PK = {}
_o = 0
for _n, _w in [("lng", 16), ("lnb", 16), ("lre", 16), ("lim", 16), ("lst", 16), ("s5d", 4), ("bglu", 4),
               ("cw", 28 * 4), ("cb", 28), ("mlg", 8), ("sd", 12), ("sg", 12), ("gb", 20), ("al", 12)]:
    PK[_n] = _o
    _o += _w
NPK = _o


class Sched:
    SEM_EPOCH = 30000

    def __init__(self, nc, es, same_engine_sync=True):
        self.nc = nc
        self.es = es
        self.ops = {e: [] for e in ENGS}
        self.sems = {}
        self.esem = {}
        self.eseq = {}
        self.nsem = 0
        for e in ENGS:
            self._new_esem(e)
        self.dsem = {}
        self.last_write = {}
        self.readers = {}
        self.waited = {e: {} for e in ENGS}
        self.same_engine_sync = same_engine_sync
        self.n_ins = 0

    def _mksem(self, name):
        s = self.es.enter_context(self.nc.semaphore(name))
        self.nsem += 1
        self.sems[id(s)] = s
        return s

    def _new_esem(self, e):
        self.esem[e] = self._mksem(f"s_{e}_{self.nsem}")
        self.eseq[e] = 0

    ALIAS = {}
    NOSELF = ()

    def _deps(self, eng, reads, writes):
        need = {}
        reads = [self.ALIAS.get(k, k) for k in reads]
        writes = [self.ALIAS.get(k, k) for k in writes] + [k for k in reads if k.startswith("B_")]

        def add(tok):
            if tok is None:
                return
            sid, val, src = tok
            if src == eng and (eng == "pe" or not self.same_engine_sync or eng in self.NOSELF):
                return
            if need.get(sid, 0) < val:
                need[sid] = val

        for k in reads:
            add(self.last_write.get(k))
        for k in writes:
            add(self.last_write.get(k))
            for t in self.readers.get(k, ()):
                add(t)
        w = self.waited[eng]
        for sid, val in need.items():
            if w.get(sid, 0) >= val:
                continue
            w[sid] = val
            self.ops[eng].append(("wait", self.sems[sid], val))

    def _commit(self, tok, reads, writes):
        reads = [self.ALIAS.get(k, k) for k in reads]
        writes = [self.ALIAS.get(k, k) for k in writes] + [k for k in reads if k.startswith("B_")]
        for k in writes:
            self.last_write[k] = tok
            self.readers[k] = []
        for k in reads:
            lst = self.readers.setdefault(k, [])
            lst.append(tok)
            if len(lst) > 8:
                best = {}
                for t in lst:
                    if best.get(t[0], (0, 0, 0))[1] < t[1]:
                        best[t[0]] = t
                self.readers[k] = list(best.values())

    def op(self, eng, fn, reads=(), writes=()):
        self._deps(eng, reads, writes)
        if self.eseq[eng] >= self.SEM_EPOCH:
            self._new_esem(eng)
        self.eseq[eng] += 1
        sem = self.esem[eng]
        tok = (id(sem), self.eseq[eng], eng)
        self.ops[eng].append(("ins", fn, sem, 1))
        self._commit(tok, reads, writes)
        self.n_ins += 1
        return tok

    def dma(self, q, fn, key, reads=(), writes=()):
        self._deps(q, reads, writes)
        if key not in self.dsem:
            self.dsem[key] = [self._mksem(f"d_{self.nsem}"), 0]
        ent = self.dsem[key]
        ent[1] += 16
        tok = (id(ent[0]), ent[1], "dma")
        self.ops[q].append(("ins", fn, ent[0], 16))
        self._commit(tok, reads, writes)
        self.n_ins += 1
        return tok

    def wait_tok(self, eng, tok):
        sid, val, _ = tok
        if self.waited[eng].get(sid, 0) >= val:
            return
        self.waited[eng][sid] = val
        self.ops[eng].append(("wait", self.sems[sid], val))

    def replay(self):
        nc = self.nc
        ops = self.ops

        def run(engobj, lst):
            for it in lst:
                if it[0] == "wait":
                    engobj.wait_ge(it[1], it[2])
                else:
                    it[1](engobj).then_inc(it[2], it[3])

        with nc.Block() as block:

            @block.tensor
            def _(e):
                run(e, ops["pe"])

            @block.scalar
            def _(e):
                run(e, ops["act"])

            @block.vector
            def _(e):
                run(e, ops["dve"])

            @block.gpsimd
            def _(e):
                run(e, ops["pool"])

            @block.sync
            def _(e):
                run(e, ops["sp"])


class _Stop(Exception):
    pass


def build(nc, es, L, T, layer0=0, debug=False, phase=99):
    NT = T // TT
    import os as _os
    S = Sched(nc, es, same_engine_sync=(_os.environ.get('SES', '1') == '1'))
    S.NOSELF = tuple(x for x in _os.environ.get('NOSELF', 'act,pool').split(',') if x)

    def dram(name, shape, kind, dt=F32):
        return nc.dram_tensor(name, shape, dt, kind=kind).ap()

    xT_d = dram("xT", [D, T], "ExternalInput")
    win_d = dram("win", [L, P, 16 * N_IN], "ExternalInput")
    wout_d = dram("wout", [L, P, 16 * 3072], "ExternalInput")
    pp_d = dram("pp", [L, P, NPK], "ExternalInput")
    bt_d = dram("bt", [L, P, 2 * 16 * 128], "ExternalInput")
    ct_d = dram("ct", [L, P, 2, 16, 128], "ExternalInput")
    wglu_d = dram("wglu", [L, 512, 512], "ExternalInput")
    jrow_d = dram("jrow", [P, 128], "ExternalInput")
    yT_d = dram("yT", [D, T], "ExternalOutput")
    xs_d = [dram(f"xs{i}", [D, T], "Internal") for i in range(2)] if L > 1 else []
    dbg_d = dram("dbg", [24, P, T], "ExternalOutput") if debug else None
    winb_d = [dram(f"winb{l}", [P, 16 * N_IN], "Internal", BF16) for l in range(L)]
    woutb_d = [dram(f"woutb{l}", [P, 16 * 3072], "Internal", BF16) for l in range(L)]

    def sb(name, shape, dt=F32):
        return es.enter_context(nc.sbuf_tensor(name, shape, dt))

    def psum(name, shape, dt=F32):
        return es.enter_context(nc.psum_tensor(name, shape, dt))

    ident_f = sb("ident_f", [P, 128]); ident_b = sb("ident_b", [P, 128], BF16)
    tri_f = sb("tri_f", [P, 128]); ones_f = sb("ones_f", [P, 128]); ones_b = sb("ones_b", [P, 128], BF16)
    jrow = sb("jrow_s", [P, 128])
    wfl = [sb(f"wfl{i}", [P, 16 * 256], BF16) for i in range(2)]
    xb = sb("xb", [P, KT, TT], BF16)
    xf = sb("xf", [P, KT, TT])
    mx_s5 = sb("mx_s5", [P, 4, TT], BF16); mx_ml = sb("mx_ml", [P, 8, TT], BF16); mx_ssd = sb("mx_ssd", [P, 12, TT], BF16)
    pp = sb("pp_s", [P, NPK])
    btb = sb("btb", [P, 2, 16, 128], BF16); ckb = sb("ckb", [P, 2, 16, 128], BF16)
    ctst = [sb(f"ctst{i}", [P, 2, 128]) for i in range(2)]
    wglub = sb("wglub", [P, 4, 512], BF16)
    E_re = sb("E_re", [P, 16, 128]); E_im = sb("E_im", [P, 16, 128])
    s5c = sb("s5c", [P, 24, 16])
    hs = sb("hs", [P, 2, 16])
    u_f = sb("u_f", [P, TT]); u_b = sb("u_b", [P, TT], BF16); zs5 = sb("zs5", [P, 4, TT], BF16)
    bp = sb("bp", [P, 2, TT]); kk = sb("kk", [P, 2, TT]); tq = [sb(f"tq{i}", [P, TT]) for i in range(3)]
    hb = [sb(f"hb{i}", [P, 2, TT], BF16) for i in range(2)]
    g_f = sb("g_f", [P, 4, TT]); g_b = sb("g_b", [P, 4, TT], BF16)
    tiny = sb("tiny", [P, 8])
    tp = [sb(f"tp{i}", [P, TT]) for i in range(2)]
    cbuf = [sb(f"cbuf{i}", [P, 3 + TT], BF16) for i in range(2)]
    halo = sb("halo", [P, 28, 3], BF16)
    dg = [sb(f"dg{i}", [P, 4, 128], BF16) for i in range(2)]
    q_b = sb("q_b", [P, 4, TT], BF16); k_b = sb("k_b", [P, 4, TT], BF16)
    oz = sb("oz", [P, 8, TT], BF16); osig = sb("osig", [P, TT], BF16)
    v_b = sb("v_b", [P, NCH, 4, 193], BF16)
    cst_f = sb("cst_f", [P, 4, 193]); cst_b = sb("cst_b", [P, 4, 192], BF16); nbc_b = sb("nbc_b", [P, 4, 96], BF16)
    x_f = sb("x_f", [P, 12, TT]); B_b = sb("B_b", [P, 4, TT], BF16); C_b = sb("C_b", [P, 4, TT], BF16)
    zssd = sb("zssd", [P, 12, TT], BF16)
    sst_f = sb("sst_f", [P, 12, 64]); sst_b = sb("sst_b", [P, 12, 64], BF16)
    yz = sb("yz", [P, 12, CH]); sqb = sb("sqb", [P, 12, CH], BF16)
    btok = sb("btok", [P, 128], BF16); cbm = sb("cbm", [P, 128])
    _xff = xf[:].rearrange("p k t -> p (k t)")
    _xsf = x_f[:].rearrange("p k t -> p (k t)")
    etmp = [_xff[:, 0:2048].rearrange("p (s j) -> p s j", s=16), _xff[:, 2048:4096].rearrange("p (s j) -> p s j", s=16),
            _xsf[:, 0:2048].rearrange("p (s j) -> p s j", s=16)]
    gsb = sb("gsb", [P, NCH, 20]); tmpg = sb("tmpg", [P, NCH, 16]); vals = sb("vals", [P, NCH, 16])
    dtv = sb("dtv", [P, NCH, 12]); cum = sb("cum", [P, NCH, 16]); bml = sb("bml", [P, NCH, 4]); ncum = sb("ncum", [P, NCH, 16])
    arow = sb("arow", [P, 12])
    lbh = [sb(f"lbh{i}", [P, 128], BF16) for i in range(2)]
    lbl = [sb(f"lbl{i}", [P, 128], BF16) for i in range(2)]
    vh = sb("vh", [P, NCH, 16], BF16); vl = sb("vl", [P, NCH, 16], BF16); vtmp = sb("vtmp", [P, 16])
    tri_b = sb("tri_b", [P, 128], BF16)
    xh = [sb(f"xh{i}", [P, CH], BF16) for i in range(2)]; xl = [sb(f"xl{i}", [P, CH], BF16) for i in range(2)]
    zb = [sb(f"zb{i}", [P, TT], BF16) for i in range(2)]
    ebc = [sb(f"ebc{i}", [P, 128]) for i in range(2)]
    tf = [sb(f"tf{i}", [P, 128]) for i in range(2)]
    dT = [sb(f"dT{i}", [P, 128]) for i in range(2)]
    A_b = [sb(f"A_b{i}", [P, 128], BF16) for i in range(2)]
    qs_b = [sb(f"qs_b{i}", [P, 128], BF16) for i in range(2)]
    Cs_b = [sb(f"Cs_b{i}", [P, 128], BF16) for i in range(2)]
    dtx_b = [sb(f"dtx_b{i}", [P, 64], BF16) for i in range(2)]
    dtxw_b = [sb(f"dtxw_b{i}", [P, 64], BF16) for i in range(2)]
    kw_b = [sb(f"kw_b{i}", [P, 96], BF16) for i in range(2)]
    wcol = [sb(f"wcol{i}", [P, 2]) for i in range(2)]
    hn = sb("hn", [P, 2, CH]); sqm = sb("sqm", [P, 2, CH], BF16); rd = sb("rd", [P, CH]); rs = sb("rs", [P, CH])
    yv = [sb(f"yv{i}", [P, CH]) for i in range(2)]
    mean = sb("mean", [P, TT]); m2 = sb("m2", [P, TT]); rstd = sb("rstd", [P, TT]); sqz = [None, None]; sqzb = [sb(f"sqzb{i}", [P, TT], BF16) for i in range(2)]
    ybuf = [sb(f"ybuf{i}", [P, TT]) for i in range(4)]
    lt = [sb(f"lt{i}", [P, TT]) for i in range(2)]

    pa = psum("pa", [P, 4, 256])
    ps_b = psum("ps_b", [P, 2, 256])
    ps_y = psum("ps_y", [P, 2, 256])
    ps_bc = psum("ps_bc", [P, 4, 128])
    ps_s = psum("ps_s", [P, 4, 128])
    ps_o = psum("ps_o", [P, 4, 128])
    ps_st = psum("ps_st", [P, 512])

    btok1 = sb("btok1", [P, 128], BF16); cbm1 = sb("cbm1", [P, 128])
    bp1 = sb("bp1", [P, 2, TT]); kk1 = sb("kk1", [P, 2, TT]); tq1x = [sb(f"tq1x{i}", [P, TT]) for i in range(2)]; tp1x = [sb(f"tp1x{i}", [P, TT]) for i in range(2)]
    S5L = [{"psb": ps_b, "k": "ps_b", "bp": bp, "kk": kk, "tq": (tq[0], tq[1]), "tp": (tp[0], tp[1])},
           {"psb": ps_st[:].rearrange("p (a j) -> p a j", a=2), "k": "S5b", "bp": bp1, "kk": kk1, "tq": (tq1x[0], tq1x[1]), "tp": (tp1x[0], tp1x[1])}]
    yvm = [sb(f"yvm{i}", [P, CH]) for i in range(2)]
    LN0 = {"par": 0, "k": "ps_", "bc": ps_bc[:, 0, :], "s": ps_s, "o": ps_o, "st": ps_st, "btok": btok, "cbm": cbm}
    LN1 = {"par": 1, "k": "L1_", "bc": pa[:, 0, 0:128], "s": pa[:, 2:4, :].rearrange("p a (b j) -> p (a b) j", j=128),
           "o": ps_b[:].rearrange("p a (b j) -> p (a b) j", j=128), "st": ps_y[:].rearrange("p a j -> p (a j)"), "btok": btok1, "cbm": cbm1}
    state = {"pa": 0, "wb": 0, "cb": 0, "yb": 0, "par": 0}
    al = {"pa0": "B_pa0", "pa1": "B_pa0", "pa2": "B_pa1", "pa3": "B_pa1", "ps_b0": "B_b", "ps_b1": "B_b",
          "ps_y0": "B_y", "ps_y1": "B_y", "ps_bc0": "B_bc", "ps_bc1": "B_bc",
          "ps_s0": "B_s", "ps_s1": "B_s", "ps_s1b": "B_s", "ps_s1c": "B_s", "ps_s2": "B_s", "ps_s3": "B_s",
          "ps_o0": "B_o", "ps_o1": "B_o", "ps_o2": "B_o", "ps_st": "B_st", "ps_stb0": "B_st", "ps_stb1": "B_st", "ps_stb2": "B_st",
          "ps_bc": "B_bc", "S5b0": "B_st", "S5b1": "B_st", "L1_bc": "B_pa0", "L1_s0": "B_pa1", "L1_s1c": "B_pa1", "L1_s2": "B_pa1", "L1_s3": "B_pa1",
          "L1_o0": "B_b", "L1_o1": "B_b", "L1_o2": "B_b", "L1_st": "B_y", "L1_stb0": "B_y", "L1_stb1": "B_y", "L1_stb2": "B_y"}
    S.ALIAS = al

    def next_pa():
        i = state["pa"]; state["pa"] = (i + 1) % 4
        return i

    def DVE(fn, r, w): return S.op("dve", fn, r, w)
    def ACT(fn, r, w): return S.op("act", fn, r, w)
    def PE(fn, r, w): return S.op("pe", fn, r, w)

    def mm(out, lhsT, rhs, start, stop, r, w):
        return PE(lambda e: e.matmul(out, lhsT=lhsT, rhs=rhs, start=start, stop=stop), r, w)

    def tt(out, a, b, op, r, w): return DVE(lambda e: e.tensor_tensor(out=out, in0=a, in1=b, op=op), r, w)
    def ts(out, a, s1, s2, op0, op1, r, w): return DVE(lambda e: e.tensor_scalar(out=out, in0=a, scalar1=s1, scalar2=s2, op0=op0, op1=op1), r, w)
    def ts1(out, a, s1, op0, r, w): return DVE(lambda e: e.tensor_single_scalar(out=out, in_=a, scalar=s1, op=op0), r, w)
    def stt(out, a, sc, b, op0, op1, r, w): return DVE(lambda e: e.scalar_tensor_tensor(out=out, in0=a, scalar=sc, in1=b, op0=op0, op1=op1), r, w)
    def cp(out, a, r, w): return DVE(lambda e: e.tensor_copy(out=out, in_=a), r, w)
    def act(out, a, func, r, w, bias=None, scale=None):
        kw = {}
        if bias is not None: kw["bias"] = bias
        if scale is not None: kw["scale"] = scale
        return ACT(lambda e: e.activation(out=out, in_=a, func=func, **kw), r, w)

    S.op("pool", lambda e: e.memset(ident_f[:], 1.0), [], ["ident_f"])
    S.op("pool", lambda e: e.affine_select(out=ident_f[:], in_=ident_f[:], pattern=[[-1, 128]], compare_op=ALU.is_equal, fill=0.0, base=0, channel_multiplier=1), ["ident_f"], ["ident_f"])
    S.op("pool", lambda e: e.memset(tri_f[:], 1.0), [], ["tri_f"])
    S.op("pool", lambda e: e.affine_select(out=tri_f[:], in_=tri_f[:], pattern=[[1, 128]], compare_op=ALU.is_ge, fill=0.0, base=0, channel_multiplier=-1), ["tri_f"], ["tri_f"])
    S.op("pool", lambda e: e.memset(ones_f[:], 1.0), [], ["ones_f"])
    S.op("pool", lambda e: e.memset(ones_b[:], 1.0), [], ["ones_b"])
    S.op("pool", lambda e: e.memset(v_b[:], 1.0), [], ["v_b"])
    cp(ident_b[:], ident_f[:], ["ident_f"], ["ident_b"])
    cp(tri_b[:], tri_f[:], ["tri_f"], ["tri_b"])
    S.dma("sp", lambda e: e.dma_start(out=jrow[:], in_=jrow_d), "jrow", [], ["jrow"])

    WKEYS = {}
    for l in range(L):
        keys = []
        WI = 16 * N_IN // 8
        for part in range(8):
            cs_ = slice(part * WI, (part + 1) * WI)
            k_ = f"winb{l}_{part}"
            S.dma("pool", lambda e, l=l, cs_=cs_: e.dma_start(out=winb_d[l][:, cs_], in_=win_d[l][:, cs_]), f"wcast{l}", [], [k_])
            keys.append(k_)
        WO = 16 * 3072 // 4
        for part in range(4):
            cs_ = slice(part * WO, (part + 1) * WO)
            k_ = f"woutb{l}_{part}"
            S.dma("pool", lambda e, l=l, cs_=cs_: e.dma_start(out=woutb_d[l][:, cs_], in_=wout_d[l][:, cs_]), f"wcast{l}", [], [k_])
            keys.append(k_)
        WKEYS[l] = keys

    def range_reduce_sin(out, ang, quarter, neg, keys_r, key_w, t0, t1, k0, k1):
        ts(t0, ang, 1.0 / TWO_PI, 0.25 * quarter, ALU.mult, ALU.add, keys_r, [k0])
        ts1(t0, t0, MAGIC, ALU.add, [k0], [k0])
        ts1(t0, t0, -MAGIC, ALU.add, [k0], [k0])
        stt(t1, t0, -CW1, ang, ALU.mult, ALU.add, keys_r + [k0], [k1])
        stt(t1, t0, -CW2, t1, ALU.mult, ALU.add, [k0, k1], [k1])
        lo = -math.pi - quarter * math.pi / 2
        ts(t1, t1, lo, lo + TWO_PI, ALU.max, ALU.min, [k1], [k1])
        if neg:
            act(out, t1, AF.Sin, [k1], [key_w], bias=-quarter * math.pi / 2, scale=-1.0)
        else:
            act(out, t1, AF.Sin, [k1], [key_w], bias=quarter * math.pi / 2, scale=1.0)

    def c(i):
        return s5c[:, i, :]

    def _phase(n):
        if phase <= n:
            raise _Stop()

    try:
        for li in range(L):
            xsrc = xT_d if li == 0 else xs_d[(li - 1) % 2]
            xdst = yT_d if li == L - 1 else xs_d[li % 2]
            xsrc_key = "xT" if li == 0 else f"xs{(li - 1) % 2}"
            xdst_key = "yT" if li == L - 1 else f"xs{li % 2}"

            if li > 0:
                for yb_ in range(4):
                    ent_ = S.dsem.get(f"out{yb_}")
                    if ent_:
                        S.wait_tok("pool", (id(ent_[0]), ent_[1], "dma"))
                        S.wait_tok("sp", (id(ent_[0]), ent_[1], "dma"))
            S.dma("sp", lambda e, li=li: e.dma_start(out=pp[:], in_=pp_d[li]), "pp", [], ["pp"])
            S.dma("pool", lambda e, li=li: e.dma_start(out=btb[:].rearrange("p a s c -> p (a s c)"), in_=bt_d[li]), "btb", [], ["btb"])
            S.dma("pool", lambda e, li=li: e.dma_start(out=wglub[:], in_=wglu_d[li].rearrange("(k p) c -> p k c", p=P)), "wglub", [], ["wglub"])

            def pk(name, i=0, n=1, rows=P):
                o = PK[name] + i
                return pp[0:rows, o:o + n]

            lre = pp[:, PK["lre"]:PK["lre"] + 16]; lim = pp[:, PK["lim"]:PK["lim"] + 16]; lst = pp[:, PK["lst"]:PK["lst"] + 16]
            K = "s5c"
            act(c(0), lst, AF.Exp, ["pp"], [K])
            tt(c(1), lre, c(0), ALU.mult, ["pp", K], [K])
            tt(c(2), lim, c(0), ALU.mult, ["pp", K], [K])
            act(c(3), c(1), AF.Exp, [K], [K])
            range_reduce_sin(c(4), c(2), 0, False, [K], K, c(20), c(21), K, K)
            range_reduce_sin(c(5), c(2), 1, False, [K], K, c(20), c(21), K, K)
            tt(c(6), c(3), c(5), ALU.mult, [K], [K])
            tt(c(7), c(3), c(4), ALU.mult, [K], [K])
            ts1(c(8), c(6), -1.0, ALU.add, [K], [K])
            tt(c(9), lre, lre, ALU.mult, ["pp"], [K])
            tt(c(10), lim, lim, ALU.mult, ["pp"], [K])
            tt(c(9), c(9), c(10), ALU.add, [K], [K])
            DVE(lambda e: e.reciprocal(out=c(9), in_=c(9)), [K], [K])
            tt(c(10), c(8), lre, ALU.mult, [K, "pp"], [K])
            tt(c(11), c(7), lim, ALU.mult, [K, "pp"], [K])
            tt(c(10), c(10), c(11), ALU.add, [K], [K])
            tt(c(12), c(10), c(9), ALU.mult, [K], [K])
            tt(c(10), c(7), lre, ALU.mult, [K, "pp"], [K])
            tt(c(11), c(8), lim, ALU.mult, [K, "pp"], [K])
            tt(c(10), c(10), c(11), ALU.subtract, [K], [K])
            tt(c(13), c(10), c(9), ALU.mult, [K], [K])
            ts1(c(14), c(12), -1.0, ALU.mult, [K], [K])
            ts1(c(15), c(13), -1.0, ALU.mult, [K], [K])
            for s in range(16):
                ts1(etmp[0][:, s, :], jrow[:], c(2)[:, s:s + 1], ALU.mult, ["jrow", K], ["xf"])
            E_re_fl = E_re[:].rearrange("p s j -> p (s j)"); E_im_fl = E_im[:].rearrange("p s j -> p (s j)")
            range_reduce_sin(E_re_fl, _xff[:, 0:2048], 1, False, ["xf"], "E_re", _xff[:, 2048:4096], _xsf[:, 0:2048], "xf", "x_f")
            range_reduce_sin(E_im_fl, _xff[:, 0:2048], 0, True, ["xf"], "E_im", _xff[:, 2048:4096], _xsf[:, 0:2048], "xf", "x_f")
            for s in range(16):
                b = s % 2
                S.dma("sp", lambda e, li=li, s=s, b=b: e.dma_start(out=ctst[b][:], in_=ct_d[li][:, :, s, :]), f"ctst{b}", [], [f"ctst{b}"])
                ts1(tq[0][:, 0:128], ctst[b][:, 0, :], c(12)[:, s:s + 1], ALU.mult, [f"ctst{b}", K], ["tq0"])
                stt(ckb[:, 0, s, :], ctst[b][:, 1, :], c(15)[:, s:s + 1], tq[0][:, 0:128], ALU.mult, ALU.add, [f"ctst{b}", K, "tq0"], ["ckb"])
                ts1(tq[1][:, 0:128], ctst[b][:, 0, :], c(15)[:, s:s + 1], ALU.mult, [f"ctst{b}", K], ["tq1"])
                stt(ckb[:, 1, s, :], ctst[b][:, 1, :], c(14)[:, s:s + 1], tq[1][:, 0:128], ALU.mult, ALU.add, [f"ctst{b}", K, "tq1"], ["ckb"])
            act(arow[:], pp[:, PK["al"]:PK["al"] + 12], AF.Exp, ["pp"], ["arow"])
            ts1(arow[:], arow[:], -1.0, ALU.mult, ["arow"], ["arow"])
            S.op("pool", lambda e: e.memset(hs[:], 0.0), [], [f"hs{i_}" for i_ in range(16)])
            S.op("pool", lambda e: e.memset(halo[:], 0.0), [], ["halo"])
            S.op("pool", lambda e: e.memset(cst_f[:], 0.0), [], ["cst_f"])
            S.op("pool", lambda e: e.memset(cst_b[:], 0.0), [], ["cst_b"])
            S.op("pool", lambda e: e.memset(nbc_b[:], 0.0), [], ["nbc_b"])
            S.op("pool", lambda e: e.memset(sst_f[:], 0.0), [], [f"sst_f{i_}" for i_ in range(12)])
            S.op("pool", lambda e: e.memset(sst_b[:], 0.0), [], [f"sst_b{i_}" for i_ in range(12)])

            _phase(0)
            win_v = winb_d[li]

            def load_chunk(name):
                c0, w = CH_OFF[name]
                b = state["wb"]; state["wb"] = 1 - b
                view = wfl[b][:, 0:16 * w].rearrange("p (k c) -> p k c", k=16)
                src = win_v[:, 16 * c0:16 * c0 + 16 * w]
                dst = wfl[b][:, 0:16 * w]
                S.dma("sp", lambda e, dst=dst, src=src: e.dma_start(out=dst, in_=src), f"wfl{b}", WKEYS[li], [f"wfl{b}"])
                return view, f"wfl{b}"

            pend = []

            def flush():
                for f_ in list(pend):
                    f_()
                pend.clear()

            def proj_fm(wv, wkey, c0, M):
                i = next_pa()
                for k in range(KT):
                    mm(pa[0:M, i, :], wv[:, k, c0:c0 + M], xb[:, k, :], k == 0, k == KT - 1, [wkey, "xb"], [f"pa{i}"])
                flush()
                return i

            def conv_tile(i, M, tile, out_ap, out_key):
                b = state["cb"]; state["cb"] = 1 - b
                ck = f"cbuf{b}"
                act(cbuf[b][0:M, 3:3 + TT], pa[0:M, i, :], AF.Copy, [f"pa{i}"], [ck])
                cp(cbuf[b][0:M, 0:3], halo[0:M, tile, :], ["halo"], [ck])
                cp(halo[0:M, tile, :], cbuf[b][0:M, TT:TT + 3], [ck], ["halo"])
                cw0 = PK["cw"] + tile * 4
                S.op("pool", lambda e, b=b, M=M, cw0=cw0: e.tensor_tensor(out=dg[b][0:M, :, 0:M], in0=ident_f[0:M, 0:M].unsqueeze(1).to_broadcast([M, 4, M]),
                                                                   in1=pp[0:M, cw0:cw0 + 4].unsqueeze(2).to_broadcast([M, 4, M]), op=ALU.mult),
                     ["ident_f", "pp"], [f"dg{b}"])
                def fin():
                    j = next_pa()
                    for k in range(4):
                        mm(pa[0:M, j, :], dg[b][0:M, k, 0:M], cbuf[b][0:M, k:k + TT], k == 0, k == 3, [f"dg{b}", ck], [f"pa{j}"])
                    act(out_ap, pa[0:M, j, :], AF.Silu, [f"pa{j}", "pp"], [out_key], bias=pp[0:M, PK["cb"] + tile:PK["cb"] + tile + 1])
                pend.append(fin)

            for ti in range(NT):
                t0 = ti * TT
                if ti == 0:
                    S.dma("pool", lambda e, t0=t0, xsrc=xsrc: e.dma_start(out=xb[:], in_=xsrc.rearrange("(k p) t -> p k t", p=P)[:, :, t0:t0 + TT]), "xb", [xsrc_key], ["xb"])
                S.dma("sp", lambda e, t0=t0, xsrc=xsrc: e.dma_start(out=xf[:], in_=xsrc.rearrange("(k p) t -> p k t", p=P)[:, :, t0:t0 + TT]), "xf", [xsrc_key], ["xf"])

                _phase(0.3)
                wv, wkey = load_chunk("gates")
                _phase(0.4)
                for cc in range(NCH):
                    for k in range(KT):
                        mm(ps_st[:, 0:20], xb[:, k, cc * CH:(cc + 1) * CH], wv[:, k, 0:20], k == 0, k == KT - 1, ["xb", wkey], ["ps_st"])
                    _phase(0.5 if cc == 0 else 0.96 + (0.5 - 0.5) * 0.06)
                    tt(gsb[:, cc, :], ps_st[:, 0:20], pp[:, PK["gb"]:PK["gb"] + 20], ALU.add, ["ps_st", "pp"], ["gsb"])
                    _phase(0.6 if cc == 0 else 0.96 + (0.6 - 0.5) * 0.06)
                    act(tmpg[:, cc, 0:4], gsb[:, cc, 4:8], AF.Exp, ["gsb"], ["tmpg"], scale=-1.0)
                    act(tmpg[:, cc, 4:16], gsb[:, cc, 8:20], AF.Exp, ["gsb"], ["tmpg"])
                    act(vals[:, cc, 0:4], tmpg[:, cc, 0:4], AF.Ln, ["tmpg"], ["vals"], bias=1.0)
                    act(dtv[:, cc, :], tmpg[:, cc, 4:16], AF.Ln, ["tmpg"], ["dtv"], bias=1.0)
                    ts1(vals[:, cc, 0:4], vals[:, cc, 0:4], -1.0, ALU.mult, ["vals"], ["vals"])
                    tt(vals[:, cc, 4:16], dtv[:, cc, :], arow[:], ALU.mult, ["dtv", "arow"], ["vals"])
                    _phase(0.7 if cc == 0 else 0.96 + (0.7 - 0.5) * 0.06)
                    cp(vh[:, cc, :], vals[:, cc, :], ["vals"], ["vh"])
                    tt(vtmp[:], vals[:, cc, :], vh[:, cc, :], ALU.subtract, ["vals", "vh"], ["vtmp"])
                    cp(vl[:, cc, :], vtmp[:], ["vtmp"], ["vl"])
                    mm(ps_s[:, 1, 32:48], tri_b[:], vh[:, cc, :], True, False, ["tri_b", "vh"], ["ps_s1b"])
                    mm(ps_s[:, 1, 32:48], tri_b[:], vl[:, cc, :], False, True, ["tri_b", "vl"], ["ps_s1b"])
                    _phase(0.8 if cc == 0 else 0.96 + (0.8 - 0.5) * 0.06)
                    cp(cum[:, cc, :], ps_s[:, 1, 32:48], ["ps_s1b"], ["cum"])
                    _phase(0.85 if cc == 0 else 0.96 + (0.85 - 0.5) * 0.06)
                    tt(bml[:, cc, :], gsb[:, cc, 0:4], cum[:, cc, 0:4], ALU.subtract, ["gsb", "cum"], ["bml"])
                    _phase(0.9 if cc == 0 else 0.96 + (0.9 - 0.5) * 0.06)
                    ts1(ncum[:, cc, :], cum[:, cc, :], -1.0, ALU.mult, ["cum"], ["ncum"])
                    _phase(0.95 if cc == 0 else 0.96 + (0.95 - 0.5) * 0.06)

                _phase(1)
                for kt in range(4):
                    wv, wkey = load_chunk(f"s5_{kt}")
                    iu = proj_fm(wv, wkey, 0, 128)
                    iz = proj_fm(wv, wkey, 128, 128)
                    act(u_f[:], pa[:, iu, :], AF.Copy, [f"pa{iu}"], ["u_f"])
                    cp(u_b[:], pa[:, iu, :], [f"pa{iu}"], ["u_b"])
                    act(zs5[:, kt, :], pa[:, iz, :], AF.Silu, [f"pa{iz}"], ["zs5"])
                    ycnt = [0]

                    def s5_tile(j, s):
                        psb = S5L[j]["psb"]; kb = S5L[j]["k"]; bp_ = S5L[j]["bp"]; kk_ = S5L[j]["kk"]; tqa, tqb = S5L[j]["tq"]; tpa, tpb = S5L[j]["tp"]
                        hbuf = hb[j]; hk = f"hb{j}"
                        L_ = f"s5l{j}_"
                        mm(psb[:, 0, :], btb[:, 0, s, :], u_b[:], True, True, ["btb", "u_b"], [kb + "0"])
                        mm(psb[:, 1, :], btb[:, 1, s, :], u_b[:], True, True, ["btb", "u_b"], [kb + "1"])
                        yield
                        Er = E_re[:, s:s + 1, :].to_broadcast([P, 2, 128]); Ei = E_im[:, s:s + 1, :].to_broadcast([P, 2, 128])
                        v2 = lambda ap: ap.rearrange("p (a j) -> p a j", a=2)
                        tt(v2(tqa[:]), v2(psb[:, 0, :]), Er, ALU.mult, [kb + "0", "E_re"], [L_ + "tqa"])
                        tt(v2(tqb[:]), v2(psb[:, 1, :]), Ei, ALU.mult, [kb + "1", "E_im"], [L_ + "tqb"])
                        tt(bp_[:, 0, :], tqa[:], tqb[:], ALU.subtract, [L_ + "tqa", L_ + "tqb"], [L_ + "bp0"])
                        tt(v2(tqa[:]), v2(psb[:, 1, :]), Er, ALU.mult, [kb + "1", "E_re"], [L_ + "tqa"])
                        tt(v2(tqb[:]), v2(psb[:, 0, :]), Ei, ALU.mult, [kb + "0", "E_im"], [L_ + "tqb"])
                        tt(bp_[:, 1, :], tqa[:], tqb[:], ALU.add, [L_ + "tqa", L_ + "tqb"], [L_ + "bp1"])
                        yield
                        for a in range(2):
                            sl_ = slice(a * 128, (a + 1) * 128)
                            rbc = c(3)[:, s:s + 1].to_broadcast([P, 128])
                            for ri in range(2):
                                DVE(lambda e, ri=ri, sl_=sl_, rbc=rbc, s=s, kk_=kk_, bp_=bp_: e.tensor_tensor_scan(out=kk_[:, ri, sl_], data0=rbc, data1=bp_[:, ri, sl_], initial=hs[:, ri, s:s + 1], op0=ALU.mult, op1=ALU.add),
                                    [K, L_ + f"bp{ri}", f"hs{s}"], [L_ + f"kk{ri}"])
                            last = a * 128 + 127
                            erl = E_re[:, s, 127:128]; eil = E_im[:, s, 127:128]
                            t0_ = tiny[:, 2 * j:2 * j + 1]; t1_ = tiny[:, 2 * j + 1:2 * j + 2]
                            tt(t0_, kk_[:, 1, last:last + 1], eil, ALU.mult, [L_ + "kk1", "E_im"], [L_ + "tiny0"])
                            tt(t1_, kk_[:, 0, last:last + 1], eil, ALU.mult, [L_ + "kk0", "E_im"], [L_ + "tiny1"])
                            stt(hs[:, 0, s:s + 1], kk_[:, 0, last:last + 1], erl, t0_, ALU.mult, ALU.add, [L_ + "kk0", "E_re", L_ + "tiny0"], [f"hs{s}"])
                            stt(hs[:, 1, s:s + 1], kk_[:, 1, last:last + 1], erl, t1_, ALU.mult, ALU.subtract, [L_ + "kk1", "E_re", L_ + "tiny1"], [f"hs{s}"])
                            yield

                        def PL(out, a_, b_, op, r, w):
                            return S.op("pool", lambda e: e.tensor_tensor(out=out, in0=a_, in1=b_, op=op), r, w)
                        PL(v2(tpa[:]), v2(kk_[:, 0, :]), Er, ALU.mult, [L_ + "kk0", "E_re"], [L_ + "tpa"])
                        PL(v2(tpb[:]), v2(kk_[:, 1, :]), Ei, ALU.mult, [L_ + "kk1", "E_im"], [L_ + "tpb"])
                        PL(hbuf[:, 0, :], tpa[:], tpb[:], ALU.add, [L_ + "tpa", L_ + "tpb"], [hk])
                        PL(v2(tpa[:]), v2(kk_[:, 1, :]), Er, ALU.mult, [L_ + "kk1", "E_re"], [L_ + "tpa"])
                        PL(v2(tpb[:]), v2(kk_[:, 0, :]), Ei, ALU.mult, [L_ + "kk0", "E_im"], [L_ + "tpb"])
                        PL(hbuf[:, 1, :], tpa[:], tpb[:], ALU.subtract, [L_ + "tpa", L_ + "tpb"], [hk])
                        yield
                        n_ = ycnt[0]; ycnt[0] += 2
                        mm(ps_y[:, 0, :], ckb[:, 0, s, :], hbuf[:, 0, :], n_ == 0, False, ["ckb", hk], ["ps_y0"])
                        mm(ps_y[:, 0, :], ckb[:, 1, s, :], hbuf[:, 1, :], False, n_ == 6, ["ckb", hk], ["ps_y0"])
                        yield

                    def _chain(*gs):
                        for g_ in gs:
                            yield from g_

                    def _run_lanes(gens):
                        gens = list(gens)
                        while gens:
                            for g_ in list(gens):
                                try:
                                    next(g_)
                                except StopIteration:
                                    gens.remove(g_)

                    _run_lanes([_chain(s5_tile(0, kt * 4 + 0), s5_tile(0, kt * 4 + 2)), _chain(s5_tile(1, kt * 4 + 1), s5_tile(1, kt * 4 + 3))])
                    stt(tq[2][:], u_f[:], pk("s5d", kt), ps_y[:, 0, :], ALU.mult, ALU.add, ["u_f", "pp", "ps_y0"], ["tq2"])
                    tt(tq[0][:], tq[2][:], tq[2][:], ALU.mult, ["tq2"], ["tq0"])
                    ts(tq[0][:], tq[0][:], 0.044715, 1.0, ALU.mult, ALU.add, ["tq0"], ["tq0"])
                    tt(tq[0][:], tq[0][:], tq[2][:], ALU.mult, ["tq0", "tq2"], ["tq0"])
                    act(tq[1][:], tq[0][:], AF.Sigmoid, ["tq0"], ["tq1"], scale=2.0 * math.sqrt(2.0 / math.pi))
                    tt(g_f[:, kt, :], tq[2][:], tq[1][:], ALU.mult, ["tq2", "tq1"], ["g_f"])
                    cp(g_b[:, kt, :], g_f[:, kt, :], ["g_f"], ["g_b"])
                for mt in range(4):
                    for k in range(4):
                        mm(ps_y[:, 1, :], wglub[:, k, mt * 128:(mt + 1) * 128], g_b[:, k, :], k == 0, k == 3, ["wglub", "g_b"], ["ps_y1"])
                    act(tq[1][:], ps_y[:, 1, :], AF.Sigmoid, ["ps_y1", "pp"], ["tq1"], bias=pk("bglu", mt))
                    tt(tq[0][:], g_f[:, mt, :], tq[1][:], ALU.mult, ["g_f", "tq1"], ["tq0"])
                    tt(mx_s5[:, mt, :], tq[0][:], zs5[:, mt, :], ALU.mult, ["tq0", "zs5"], ["mx_s5"])

                _phase(2)
                for h in range(4):
                    wv, wkey = load_chunk(f"mlA_{h}")
                    i = proj_fm(wv, wkey, 0, 96)
                    conv_tile(i, 96, h, q_b[0:96, h, :], "q_b")
                    i = proj_fm(wv, wkey, 96, 96)
                    conv_tile(i, 96, 4 + h, k_b[0:96, h, :], "k_b")
                    wvB, wkB = load_chunk(f"mlB_{h}")
                    wvC, wkC = load_chunk(f"mlC_{h}")
                    for e2 in range(2):
                        i = proj_fm(wvB, wkB, e2 * 96, 96)
                        act(osig[0:96, :], pa[0:96, i, :], AF.Sigmoid, [f"pa{i}"], ["osig"])
                        i = proj_fm(wvC, wkC, e2 * 96, 96)
                        act(tq[0][0:96, :], pa[0:96, i, :], AF.Silu, [f"pa{i}"], ["tq0"])
                        tt(oz[0:96, 2 * h + e2, :], tq[0][0:96, :], osig[0:96, :], ALU.mult, ["tq0", "osig"], ["oz"])
                    wv, wkey = load_chunk(f"mlV_{h}")
                    for cc in range(NCH):
                        i = next_pa()
                        for k in range(KT):
                            mm(pa[:, i, 0:192], xb[:, k, cc * CH:(cc + 1) * CH], wv[:, k, 0:192], k == 0, k == KT - 1, ["xb", wkey], [f"pa{i}"])
                        act(v_b[:, cc, h, 0:192], pa[:, i, 0:192], AF.Copy, [f"pa{i}"], ["v_b"])
                flush()
                _phase(3)
                for g in range(4):
                    wv, wkey = load_chunk(f"sX_{g}")
                    for r in range(3):
                        i = proj_fm(wv, wkey, r * 64, 64)
                        conv_tile(i, 64, 8 + g * 5 + r, x_f[0:64, 3 * g + r, :], "x_f")
                    wv, wkey = load_chunk(f"sBC_{g}")
                    i = proj_fm(wv, wkey, 0, 128)
                    conv_tile(i, 128, 8 + g * 5 + 3, B_b[:, g, :], "B_b")
                    i = proj_fm(wv, wkey, 128, 128)
                    conv_tile(i, 128, 8 + g * 5 + 4, C_b[:, g, :], "C_b")
                    wv, wkey = load_chunk(f"sZ_{g}")
                    for r in range(3):
                        i = proj_fm(wv, wkey, r * 64, 64)
                        act(zssd[0:64, 3 * g + r, :], pa[0:64, i, :], AF.Silu, [f"pa{i}"], ["zssd"])

                flush()
                if ti + 1 < NT:
                    S.dma("pool", lambda e, t1=t0 + TT, xsrc=xsrc: e.dma_start(out=xb[:], in_=xsrc.rearrange("(k p) t -> p k t", p=P)[:, :, t1:t1 + TT]), "xb", [xsrc_key], ["xb"])
                _phase(4)
                def get_bc(ln, cc, col):
                    par = ln["par"]
                    S.op("pool", lambda e, par=par, cc=cc, col=col: e.tensor_copy(out=lbh[par][:], in_=vh[:, cc, col:col + 1].to_broadcast([P, 128])), ["vh"], [f"lbh{par}"])
                    S.op("pool", lambda e, par=par, cc=cc, col=col: e.tensor_copy(out=lbl[par][:], in_=vl[:, cc, col:col + 1].to_broadcast([P, 128])), ["vl"], [f"lbl{par}"])
                    mm(ln["bc"], lbh[par][:], tri_b[:], True, False, [f"lbh{par}", "tri_b"], [ln["k"] + "bc"])
                    mm(ln["bc"], lbl[par][:], tri_b[:], False, True, [f"lbl{par}", "tri_b"], [ln["k"] + "bc"])

                def ml_head(ln, cc, h):
                    cs = slice(cc * CH, (cc + 1) * CH)
                    par = ln["par"]; kp = ln["k"]; bc = ln["bc"]; ps_s = ln["s"]; ps_o = ln["o"]; ps_st = ln["st"]
                    bck = kp + "bc"
                    get_bc(ln, cc, h)
                    mm(ps_s[:, 0, :], k_b[0:96, h, cs], q_b[0:96, h, cs], True, True, ["k_b", "q_b"], [kp + "s0"])
                    yield
                    act(ebc[par][:], bc, AF.Exp, [bck], [f"ebc{par}"])
                    ts(tf[par][:], bc, ncum[:, cc, h:h + 1], 0.0, ALU.add, ALU.min, [bck, "ncum"], [f"tf{par}"])
                    tt(wcol[par][:, 0:1], bc[:, 127:128], bml[:, cc, h:h + 1], ALU.add, [bck, "bml"], [f"wcol{par}"])
                    yield
                    act(dT[par][:], tf[par][:], AF.Exp, [f"tf{par}", "gsb"], [f"dT{par}"], bias=gsb[:, cc, h:h + 1])
                    act(wcol[par][:, 1:2], wcol[par][:, 0:1], AF.Exp, [f"wcol{par}"], [f"wcolb{par}"])
                    yield
                    tt(dT[par][:], dT[par][:], tri_f[:], ALU.mult, [f"dT{par}", "tri_f"], [f"dT{par}"])
                    stt(A_b[par][:], ps_s[:, 0, :], DQK ** -0.5, dT[par][:], ALU.mult, ALU.mult, [kp + "s0", f"dT{par}"], [f"A_b{par}"])
                    stt(qs_b[par][0:96, :], q_b[0:96, h, cs], DQK ** -0.5, ebc[par][0:96, :], ALU.mult, ALU.mult, ["q_b", f"ebc{par}"], [f"qs_b{par}"])
                    yield
                    for e2 in range(2):
                        mm(ps_o[0:96, e2, :], v_b[:, cc, h, e2 * 96:(e2 + 1) * 96], A_b[par][:], True, False, ["v_b", f"A_b{par}"], [kp + f"o{e2}"])
                        mm(ps_o[0:96, e2, :], cst_b[0:96, h, e2 * 96:(e2 + 1) * 96], qs_b[par][0:96, :], False, True, ["cst_b", f"qs_b{par}"], [kp + f"o{e2}"])
                    mm(ps_o[0:96, 2, :], ones_b[:, 0:96], A_b[par][:], True, False, ["ones_b", f"A_b{par}"], [kp + "o2"])
                    mm(ps_o[0:96, 2, :], nbc_b[0:96, h, :], qs_b[par][0:96, :], False, True, ["nbc_b", f"qs_b{par}"], [kp + "o2"])
                    mm(ps_s[:, 2, 0:96], k_b[0:96, h, cs], ident_b[0:96, 0:96], True, True, ["k_b", "ident_b"], [kp + "s2"])
                    yield
                    act(rd[0:96, :], ps_o[0:96, 2, :], AF.Abs, [kp + "o2"], ["rd"])
                    ts1(kw_b[par][:], ps_s[:, 2, 0:96], wcol[par][:, 1:2], ALU.mult, [kp + "s2", f"wcolb{par}"], [f"kw_b{par}"])
                    yield
                    mm(ps_st[0:96, 0:193], kw_b[par][:], v_b[:, cc, h, :], True, True, [f"kw_b{par}", "v_b"], [kp + "st"])
                    ts1(rd[0:96, :], rd[0:96, :], 1.0, ALU.max, ["rd"], ["rd"])
                    DVE(lambda e: e.reciprocal(out=rd[0:96, :], in_=rd[0:96, :]), ["rd"], ["rd"])
                    for e2 in range(2):
                        tt(hn[0:96, e2, :], ps_o[0:96, e2, :], rd[0:96, :], ALU.mult, [kp + f"o{e2}", "rd"], ["hn"])
                    tt(sqm[0:96, :, :], hn[0:96, :, :], hn[0:96, :, :], ALU.mult, ["hn"], ["sqm"])
                    yield
                    for e2 in range(2):
                        mm(ps_s[0:96, 3, :], ones_b[0:96, 0:96], sqm[0:96, e2, :], e2 == 0, e2 == 1, ["ones_b", "sqm"], [kp + "s3"])
                    stt(cst_f[0:96, h, :], cst_f[0:96, h, :], ebc[par][0:96, 127:128], ps_st[0:96, 0:193], ALU.mult, ALU.add, ["cst_f", f"ebc{par}", kp + "st"], ["cst_f"])
                    cp(cst_b[0:96, h, :], cst_f[0:96, h, 0:192], ["cst_f"], ["cst_b"])
                    cp(nbc_b[0:96, h, :], cst_f[0:96, h, 192:193].to_broadcast([96, 96]), ["cst_f"], ["nbc_b"])
                    yield
                    act(rs[0:96, :], ps_s[0:96, 3, :], AF.Sqrt, [kp + "s3"], ["rs"], bias=EPS, scale=1.0 / DV)
                    yield
                    DVE(lambda e: e.reciprocal(out=rs[0:96, :], in_=rs[0:96, :]), ["rs"], ["rs"])
                    for e2 in range(2):
                        stt(yvm[e2][0:96, :], hn[0:96, e2, :], pk("mlg", 2 * h + e2, 1, 96), rs[0:96, :], ALU.mult, ALU.mult, ["hn", "pp", "rs"], [f"yvm{e2}"])
                        tt(mx_ml[0:96, 2 * h + e2, cs], yvm[e2][0:96, :], oz[0:96, 2 * h + e2, cs], ALU.mult, [f"yvm{e2}", "oz"], ["mx_ml"])
                    yield

                def ssd_group(ln, cc, g):
                    cs = slice(cc * CH, (cc + 1) * CH)
                    par = ln["par"]; kp = ln["k"]; bc = ln["bc"]; ps_s = ln["s"]; ps_o = ln["o"]; ps_st = ln["st"]
                    btok_ = ln["btok"]; cbm_ = ln["cbm"]; bk = f"btok{par}"; ck = f"cbm{par}"
                    bck = kp + "bc"
                    mm(ps_s[:, 0, :], B_b[:, g, cs], C_b[:, g, cs], True, True, ["B_b", "C_b"], [kp + "s0"])
                    mm(ps_s[:, 2, :], B_b[:, g, cs], ident_b[:], True, True, ["B_b", "ident_b"], [kp + "s2"])
                    yield
                    tt(cbm_[:], ps_s[:, 0, :], tri_f[:], ALU.mult, [kp + "s0", "tri_f"], [ck])
                    act(btok_[:], ps_s[:, 2, :], AF.Copy, [kp + "s2"], [bk])
                    yield
                    for r in range(3):
                        hh = 3 * g + r
                        get_bc(ln, cc, 4 + hh)
                        cp(xh[par][0:64, :], x_f[0:64, hh, cs], ["x_f"], [f"xh{par}"])
                        tt(yv[par][0:64, :], x_f[0:64, hh, cs], xh[par][0:64, :], ALU.subtract, ["x_f", f"xh{par}"], [f"yv{par}"])
                        cp(xl[par][0:64, :], yv[par][0:64, :], [f"yv{par}"], [f"xl{par}"])
                        yield
                        mm(ps_s[:, 1, 64:128], xh[par][0:64, :], ident_b[0:64, 0:64], True, False, [f"xh{par}", "ident_b"], [kp + "s1c"])
                        mm(ps_s[:, 1, 64:128], xl[par][0:64, :], ident_b[0:64, 0:64], False, True, [f"xl{par}", "ident_b"], [kp + "s1c"])
                        act(ebc[par][:], bc, AF.Exp, [bck], [f"ebc{par}"])
                        ts(tf[par][:], bc, ncum[:, cc, 4 + hh:5 + hh], 0.0, ALU.add, ALU.min, [bck, "ncum"], [f"tf{par}"])
                        tt(wcol[par][:, 0:1], bc[:, 127:128], ncum[:, cc, 4 + hh:5 + hh], ALU.add, [bck, "ncum"], [f"wcol{par}"])
                        yield
                        act(dT[par][:], tf[par][:], AF.Exp, [f"tf{par}"], [f"dT{par}"])
                        act(wcol[par][:, 1:2], wcol[par][:, 0:1], AF.Exp, [f"wcol{par}"], [f"wcolb{par}"])
                        ts1(dtx_b[par][:], ps_s[:, 1, 64:128], dtv[:, cc, hh:hh + 1], ALU.mult, [kp + "s1c", "dtv"], [f"dtx_b{par}"])
                        tt(Cs_b[par][:], C_b[:, g, cs], ebc[par][:], ALU.mult, ["C_b", f"ebc{par}"], [f"Cs_b{par}"])
                        yield
                        tt(A_b[par][:], dT[par][:], cbm_[:], ALU.mult, [f"dT{par}", ck], [f"A_b{par}"])
                        ts1(dtxw_b[par][:], dtx_b[par][:], wcol[par][:, 1:2], ALU.mult, [f"dtx_b{par}", f"wcolb{par}"], [f"dtxw_b{par}"])
                        yield
                        mm(ps_o[0:64, r, :], dtx_b[par][:], A_b[par][:], True, False, [f"dtx_b{par}", f"A_b{par}"], [kp + f"o{r}"])
                        mm(ps_o[0:64, r, :], sst_b[:, hh, :], Cs_b[par][:], False, True, [f"sst_b{hh}", f"Cs_b{par}"], [kp + f"o{r}"])
                        mm(ps_st[:, 256 + r * 64:256 + (r + 1) * 64], btok_[:], dtxw_b[par][:], True, True, [bk, f"dtxw_b{par}"], [kp + f"stb{r}"])
                        yield
                        stt(yv[par][0:64, :], x_f[0:64, hh, cs], pk("sd", hh, 1, 64), ps_o[0:64, r, :], ALU.mult, ALU.add, ["x_f", "pp", kp + f"o{r}"], [f"yv{par}"])
                        tt(yz[0:64, hh, :], yv[par][0:64, :], zssd[0:64, hh, cs], ALU.mult, [f"yv{par}", "zssd"], [f"yz{hh}"])
                        stt(sst_f[:, hh, :], sst_f[:, hh, :], ebc[par][:, 127:128], ps_st[:, 256 + r * 64:256 + (r + 1) * 64], ALU.mult, ALU.add, [f"sst_f{hh}", f"ebc{par}", kp + f"stb{r}"], [f"sst_f{hh}"])
                        cp(sst_b[:, hh, :], sst_f[:, hh, :], [f"sst_f{hh}"], [f"sst_b{hh}"])
                        yield

                def run_lanes(gens):
                    gens = list(gens)
                    while gens:
                        for g_ in list(gens):
                            try:
                                next(g_)
                            except StopIteration:
                                gens.remove(g_)

                def chain(*gs):
                    for g_ in gs:
                        yield from g_

                for cc in range(NCH):
                    cs = slice(cc * CH, (cc + 1) * CH)
                    run_lanes([chain(*[ml_head(LN0, cc, h) for h in range(4)], ssd_group(LN0, cc, 3)),
                               chain(*[ssd_group(LN1, cc, g) for g in range(3)])])
                    act(sqb[0:64, :, :], yz[0:64, :, :], AF.Square, [f"yz{i_}" for i_ in range(12)], ["sqb"])
                    for hh in range(12):
                        mm(ps_s[0:64, 3, :], ones_b[0:64, 0:64], sqb[0:64, hh, :], hh == 0, hh == 11, ["ones_b", "sqb"], ["ps_s3"])
                    act(rs[0:64, :], ps_s[0:64, 3, :], AF.Sqrt, ["ps_s3"], ["rs"], bias=EPS, scale=1.0 / 768.0)
                    DVE(lambda e: e.reciprocal(out=rs[0:64, :], in_=rs[0:64, :]), ["rs"], ["rs"])
                    for hh in range(12):
                        stt(mx_ssd[0:64, hh, cs], yz[0:64, hh, :], pk("sg", hh, 1, 64), rs[0:64, :], ALU.mult, ALU.mult, [f"yz{hh}", "pp", "rs"], ["mx_ssd"])

                _phase(5)
                if debug and li == 0:
                    S.dma("pool", lambda e, t0=t0: e.dma_start(out=dbg_d[0:4, :, t0:t0 + TT].rearrange("k p t -> p k t"), in_=mx_s5[:]), "dbg0", ["mx_s5"], [])
                    S.dma("pool", lambda e, t0=t0: e.dma_start(out=dbg_d[4:12, 0:96, t0:t0 + TT].rearrange("k p t -> p k t"), in_=mx_ml[0:96]), "dbg1", ["mx_ml"], [])
                    S.dma("pool", lambda e, t0=t0: e.dma_start(out=dbg_d[12:24, 0:64, t0:t0 + TT].rearrange("k p t -> p k t"), in_=mx_ssd[0:64]), "dbg2", ["mx_ssd"], [])
                for m in range(KT):
                    b = state["wb"]; state["wb"] = 1 - b
                    wv = wfl[b][:, 0:24 * 128].rearrange("p (k c) -> p k c", k=24)
                    wk = f"wfl{b}"
                    mc = slice(m * 128, (m + 1) * 128)
                    S.dma("sp", lambda e, b=b, m=m, li=li: e.dma_start(out=wfl[b][:, 0:3072], in_=woutb_d[li][:, m * 3072:(m + 1) * 3072]), wk, WKEYS[li], [wk])
                    i = next_pa()
                    n = 0
                    for k in range(4):
                        mm(pa[:, i, :], wv[:, k, :], mx_s5[:, k, :], n == 0, False, [wk, "mx_s5"], [f"pa{i}"]); n += 1
                    for k in range(8):
                        mm(pa[:, i, :], wv[0:96, 4 + k, :], mx_ml[0:96, k, :], False, False, [wk, "mx_ml"], [f"pa{i}"]); n += 1
                    for k in range(12):
                        mm(pa[:, i, :], wv[0:64, 12 + k, :], mx_ssd[0:64, k, :], False, k == 11, [wk, "mx_ssd"], [f"pa{i}"]); n += 1
                    stt(xf[:, m, :], xf[:, m, :], ALPHA, pa[:, i, :], ALU.mult, ALU.add, ["xf", f"pa{i}"], ["xf"])
                    sq = sqz[m % 2]
                    act(sq[:, 0:TT // 2].bitcast(BF16) if False else zb[m % 2][:], xf[:, m, :], AF.Copy, ["xf"], [f"zb{m % 2}"])
                    act(sqzb[m % 2][:], xf[:, m, :], AF.Square, ["xf"], [f"sqzb{m % 2}"])
                    mm(ps_b[:, 0, :], ones_b[:], zb[m % 2][:], m == 0, m == KT - 1, ["ones_b", f"zb{m % 2}"], ["ps_b0"])
                    mm(ps_y[:, 0, :], ones_b[:], sqzb[m % 2][:], m == 0, m == KT - 1, ["ones_b", f"sqzb{m % 2}"], ["ps_y0"])
                act(mean[:], ps_b[:, 0, :], AF.Copy, ["ps_b0"], ["mean"], scale=1.0 / D)
                tt(m2[:], mean[:], mean[:], ALU.mult, ["mean"], ["m2"])
                stt(m2[:], ps_y[:, 0, :], 1.0 / D, m2[:], ALU.mult, ALU.subtract, ["ps_y0", "m2"], ["m2"])
                act(rstd[:], m2[:], AF.Sqrt, ["m2"], ["rstd"], bias=EPS, scale=1.0)
                DVE(lambda e: e.reciprocal(out=rstd[:], in_=rstd[:]), ["rstd"], ["rstd"])
                out_toks = []
                for m in range(KT):
                    l_ = lt[m % 2]; lk = f"lt{m % 2}"
                    yb = state["yb"]; state["yb"] = (yb + 1) % 4
                    tt(l_[:], xf[:, m, :], mean[:], ALU.subtract, ["xf", "mean"], [lk])
                    tt(l_[:], l_[:], rstd[:], ALU.mult, [lk, "rstd"], [lk])
                    act(ybuf[yb][:], l_[:], AF.Identity, [lk, "pp"], [f"ybuf{yb}"], bias=pk("lnb", m), scale=pk("lng", m))
                    out_toks.append(S.dma("sp", lambda e, yb=yb, m=m, t0=t0, xdst=xdst: e.dma_start(out=xdst[m * 128:(m + 1) * 128, t0:t0 + TT], in_=ybuf[yb][:]), f"out{yb}", [f"ybuf{yb}"], [xdst_key]))
                state["out_toks"] = out_toks
    except _Stop:
        pass
    for t in state.get("out_toks", []):
        S.wait_tok("sp", t)
    for dk in ("dbg0", "dbg1", "dbg2"):
        ent = S.dsem.get(dk)
        if ent:
            S.wait_tok("pool", (id(ent[0]), ent[1], "dma"))
    for yb in range(4):
        ent = S.dsem.get(f"out{yb}")
        if ent:
            S.wait_tok("sp", (id(ent[0]), ent[1], "dma"))
    S.replay()
    return S


def _pack_layer_params(inp, l):
    f = np.float32
    pp = np.zeros((P, NPK), f)
    pp[:, PK["lng"]:PK["lng"] + 16] = inp["ln_g"][l].reshape(16, 128).T
    pp[:, PK["lnb"]:PK["lnb"] + 16] = inp["ln_b"][l].reshape(16, 128).T
    lre = inp["s5_lambda_re"][l].reshape(16, 2, 64).reshape(16, 128).T
    lim = inp["s5_lambda_im"][l].reshape(16, 2, 64).reshape(16, 128).T
    lst = np.repeat(inp["s5_log_step"][l].reshape(16, 2, 1), 64, axis=2).reshape(16, 128).T
    pp[:, PK["lre"]:PK["lre"] + 16] = lre
    pp[:, PK["lim"]:PK["lim"] + 16] = lim
    pp[:, PK["lst"]:PK["lst"] + 16] = lst
    pp[:, PK["s5d"]:PK["s5d"] + 4] = inp["s5_d"][l].reshape(4, 128).T
    pp[:, PK["bglu"]:PK["bglu"] + 4] = inp["s5_b_glu"][l].reshape(4, 128).T
    mcw = inp["ml_conv_w"][l]; mcb = inp["ml_conv_b"][l]
    scw = inp["ssd_conv_w"][l]; scb = inp["ssd_conv_b"][l]
    tiles = []
    for h in range(4):
        tiles.append((mcw[:, h * 96:(h + 1) * 96], mcb[h * 96:(h + 1) * 96]))
    for h in range(4):
        tiles.append((mcw[:, 384 + h * 96:384 + (h + 1) * 96], mcb[384 + h * 96:384 + (h + 1) * 96]))
    for g in range(4):
        for r in range(3):
            o = (3 * g + r) * 64
            tiles.append((scw[:, o:o + 64], scb[o:o + 64]))
        o = 768 + g * 128
        tiles.append((scw[:, o:o + 128], scb[o:o + 128]))
        o = 1280 + g * 128
        tiles.append((scw[:, o:o + 128], scb[o:o + 128]))
    for t, (w, b) in enumerate(tiles):
        m = w.shape[1]
        pp[0:m, PK["cw"] + 4 * t:PK["cw"] + 4 * t + 4] = w.T
        pp[0:m, PK["cb"] + t] = b
    pp[0:96, PK["mlg"]:PK["mlg"] + 8] = inp["ml_norm_g"][l].reshape(8, 96).T
    pp[0:64, PK["sd"]:PK["sd"] + 12] = np.repeat(inp["ssd_d"][l][None, :], 64, axis=0)
    pp[0:64, PK["sg"]:PK["sg"] + 12] = inp["ssd_norm_g"][l].reshape(12, 64).T
    gb = np.concatenate([inp["ml_i_bias"][l], inp["ml_f_bias"][l], inp["ssd_dt_bias"][l]])
    pp[:, PK["gb"]:PK["gb"] + 20] = np.repeat(gb[None, :], P, axis=0)
    pp[:, PK["al"]:PK["al"] + 12] = np.repeat(inp["ssd_a_log"][l][None, :], P, axis=0)
    bt = np.zeros((P, 2, 16, 128), f)
    ct = np.zeros((P, 2, 16, 128), f)
    for ri, (bsrc, csrc) in enumerate([(inp["s5_b_re"][l], inp["s5_c_re"][l]), (inp["s5_b_im"][l], inp["s5_c_im"][l])]):
        for s in range(16):
            for gg in range(2):
                g = 2 * s + gg
                r0 = (g % 8) * 16
                bt[r0:r0 + 16, ri, s, gg * 64:(gg + 1) * 64] = bsrc[g].T
                ct[gg * 64:(gg + 1) * 64, ri, s, r0:r0 + 16] = csrc[g].T
    return pp, bt.reshape(P, -1), ct


def _prep(inputs):
    inp = {k: np.asarray(v) for k, v in inputs.items()}
    L = inp["w_in"].shape[0]
    win = np.empty((L, P, 16 * N_IN), np.float32)
    wout = np.zeros((L, P, 16 * 3072), np.float32)
    for l in range(L):
        wl = inp["w_in"][l]
        for name, cols in CHUNKS:
            c0, w = CH_OFF[name]
            blk = wl[:, cols].reshape(16, P, w).transpose(1, 0, 2).reshape(P, 16 * w)
            win[l, :, 16 * c0:16 * c0 + 16 * w] = blk
        wo = inp["w_out"][l]
        dst = wout[l].reshape(P, 16, 24, 128)
        src = wo.reshape(D, 16, 128)
        dst[:, :, 0:4, :] = src[0:512].reshape(4, 128, 16, 128).transpose(1, 2, 0, 3)
        dst[0:96, :, 4:12, :] = src[512:1280].reshape(8, 96, 16, 128).transpose(1, 2, 0, 3)
        dst[0:64, :, 12:24, :] = src[1280:2048].reshape(12, 64, 16, 128).transpose(1, 2, 0, 3)
    pps, bts, cts = [], [], []
    for l in range(L):
        a, b, c_ = _pack_layer_params(inp, l)
        pps.append(a); bts.append(b); cts.append(c_)
    common = {
        "win": win,
        "wout": wout,
        "pp": np.stack(pps), "bt": np.stack(bts), "ct": np.stack(cts),
        "wglu": np.ascontiguousarray(inp["s5_w_glu"]),
        "jrow": np.repeat(np.arange(1, 129, dtype=np.float32)[None, :], P, axis=0),
    }
    return inp, common


_CACHE = {}


def _get_prog(L, T):
    key = (L, T)
    if key not in _CACHE:
        nc = bass.Bass("TRN2", target_bir_lowering=False)
        es = ExitStack()
        build(nc, es, L, T)
        _CACHE[key] = (nc, es)
    return _CACHE[key][0]


FUSED = True


def kernel(**inputs):
    inp, common = _prep(inputs)
    x = inp["x"]
    B, T, _ = x.shape
    L = inp["w_in"].shape[0]
    n_cores = 8
    xT = [np.ascontiguousarray(x[b].T) for b in range(B)]
    if FUSED:
        nc = _get_prog(L, T)
        in_maps = [dict(common, xT=xT[c % B]) for c in range(n_cores)]
        res = run_bass_kernel_spmd(nc, in_maps, core_ids=list(range(n_cores)))
        outs = [res.results[b]["yT"] for b in range(B)]
    else:
        nc = _get_prog(1, T)
        cur = xT
        for l in range(L):
            cl = {k: (v[l:l + 1] if k in ("win", "wout", "pp", "bt", "ct", "wglu") else v) for k, v in common.items()}
            in_maps = [dict(cl, xT=cur[c % B]) for c in range(n_cores)]
            res = run_bass_kernel_spmd(nc, in_maps, core_ids=list(range(n_cores)))
            cur = [res.results[b]["yT"] for b in range(B)]
        outs = cur
    return np.stack([o.T for o in outs]).astype(np.float32)
```
